# Optimizing a Trainium2 kernel written in Bass

```python
import math
import jax
import jax.numpy as jnp
from jax import lax
import numpy as np

D_MODEL = 2048
BATCH = 4
SEQ = 2048
DEPTH = 4
DEC_BATCH = 32
DEC_SEQ = 8
PAST_LEN = 16384
PAGE_SIZE = 128

N_MIXERS = 4
N_SWA_LAYERS = (DEPTH + 3) // 4
N_SSD_LAYERS = (DEPTH + 2) // 4
N_DIL_LAYERS = (DEPTH + 1) // 4
N_LRU_LAYERS = DEPTH // 4

RMS_EPS = 1e-6
ROPE_THETA = 10000.0
ATTN_BLOCK = 128
NEG_INF = -1e30
D_FF = 4 * D_MODEL

SWA_WINDOW = 128
SWA_HEAD_DIM = 64
SWA_Q_HEADS = 32
SWA_KV_HEADS = 4
SWA_GRP = SWA_Q_HEADS // SWA_KV_HEADS

SSM_D_INNER = 2 * D_MODEL
SSM_HEAD_DIM = 64
SSM_HEADS = SSM_D_INNER // SSM_HEAD_DIM
SSM_GROUPS = 8
SSM_HPG = SSM_HEADS // SSM_GROUPS
SSM_D_STATE = 128
SSM_CONV = 4
SSM_CHUNK = 128
SSM_CONV_DIM = SSM_D_INNER + 2 * SSM_GROUPS * SSM_D_STATE

DIL_PATTERN = ((128, 1), (512, 4), (2048, 16))
DIL_KEYS = ('dil_kv_w128', 'dil_kv_w512', 'dil_kv_w2048')
N_DIL = 3
DIL_HEAD_DIM = 128
DIL_Q_HEADS = 16
DIL_KV_HEADS = 4
DIL_GRP = DIL_Q_HEADS // DIL_KV_HEADS

LRU_WIDTH = D_MODEL
LRU_BLOCKS = 8
LRU_BLOCK_DIM = LRU_WIDTH // LRU_BLOCKS
LRU_CONV = 4
LRU_C = 8.0

STATE_KEYS = ('swa_kv', 'ssd_conv', 'ssd', 'dil_kv_w128', 'dil_kv_w512', 'dil_kv_w2048', 'lru_conv', 'lru')

kernel_name = 'hybrid_swa_ssd_dilated_rglru_adaln_step'


def rmsnorm(x, g):
    xf = x.astype(jnp.float32)
    y = xf * lax.rsqrt(jnp.mean(xf * xf, axis=-1, keepdims=True) + RMS_EPS)
    return (y * g.astype(jnp.float32)).astype(x.dtype)


def rope(x, pos):
    half = x.shape[-1] // 2
    inv = ROPE_THETA ** (-jnp.arange(half, dtype=jnp.float32) / half)
    ang = pos.astype(jnp.float32)[:, None] * inv[None, :]
    cos = jnp.cos(ang)[None, :, None, :]
    sin = jnp.sin(ang)[None, :, None, :]
    xf = x.astype(jnp.float32)
    x1, x2 = xf[..., :half], xf[..., half:]
    return jnp.concatenate([x1 * cos - x2 * sin, x2 * cos + x1 * sin], axis=-1).astype(x.dtype)


def causal_conv(x, buf, w, b):
    k = w.shape[0]
    l = x.shape[1]
    xp = jnp.concatenate([buf.astype(x.dtype), x], axis=1)
    y = b
    for i in range(k):
        y = y + xp[:, i:i + l] * w[i]
    return y, xp[:, l:]


def banded_attn(q, k, v, k_pre, v_pre, pre_valid, window, sinks=None):
    n, l, kvh, grp, hd = q.shape
    bq = min(ATTN_BLOCK, l)
    nb = -(-l // bq)
    pad = nb * bq - l
    q = jnp.pad(q, ((0, 0), (0, pad), (0, 0), (0, 0), (0, 0)))
    kf = jnp.concatenate([k_pre.astype(k.dtype), jnp.pad(k, ((0, 0), (0, pad), (0, 0), (0, 0)))], axis=1)
    vf = jnp.concatenate([v_pre.astype(v.dtype), jnp.pad(v, ((0, 0), (0, pad), (0, 0), (0, 0)))], axis=1)
    valid = jnp.concatenate([pre_valid, jnp.ones((n, nb * bq), bool)], axis=1)
    idx = jnp.arange(nb)[:, None] * bq + jnp.arange(bq + window)[None, :]
    kb = kf[:, idx]
    vb = vf[:, idx]
    vmask = valid[:, idx]
    qb = q.reshape(n, nb, bq, kvh, grp, hd)
    s = jnp.einsum('nbqkgd,nbskd->nbkgqs', qb, kb, preferred_element_type=jnp.float32) * (hd ** -0.5)
    rel = jnp.arange(bq)[:, None] + window - jnp.arange(bq + window)[None, :]
    band = (rel >= 0) & (rel <= window)
    mask = band[None, None, None, None] & vmask[:, :, None, None, None, :]
    s = jnp.where(mask, s, NEG_INF)
    m = jnp.max(s, axis=-1)
    if sinks is not None:
        sk = sinks.astype(jnp.float32).reshape(kvh, grp)[None, None, :, :, None]
        m = jnp.maximum(m, sk)
    p = jnp.exp(s - m[..., None])
    denom = jnp.sum(p, axis=-1)
    if sinks is not None:
        denom = denom + jnp.exp(sk - m)
    o = jnp.einsum('nbkgqs,nbskd->nbqkgd', p, vb.astype(jnp.float32))
    o = o / jnp.moveaxis(denom, -1, 2)[..., None]
    lse = jnp.moveaxis(m + jnp.log(denom), -1, 2)
    o = o.reshape(n, nb * bq, kvh, grp, hd)[:, :l].astype(q.dtype)
    lse = lse.reshape(n, nb * bq, kvh, grp)[:, :l]
    return o, lse


def dilated_attn(q, k, v, k_pre, v_pre, pre_valid, window, dilation, sinks=None):
    n, l = q.shape[:2]
    d = dilation
    lp = -(-l // d) * d

    def padl(a):
        return jnp.pad(a, [(0, 0), (0, lp - l)] + [(0, 0)] * (a.ndim - 2))

    def split(a, length):
        a = a.reshape((n, length // d, d) + a.shape[2:])
        a = jnp.moveaxis(a, 2, 1)
        return a.reshape((n * d, length // d) + a.shape[3:])

    def merge(a):
        a = a.reshape((n, d, lp // d) + a.shape[2:])
        a = jnp.moveaxis(a, 1, 2)
        return a.reshape((n, lp) + a.shape[3:])[:, :l]

    o, lse = banded_attn(split(padl(q), lp), split(padl(k), lp), split(padl(v), lp),
                         split(k_pre, window), split(v_pre, window), split(pre_valid, window),
                         window // d, sinks)
    return merge(o), merge(lse)


def window_prefix(cache_kv, n, rows, kvh, hd, dtype):
    if cache_kv is None:
        z = jnp.zeros((n, rows, kvh, hd), dtype)
        return z, z, jnp.zeros((n, rows), bool)
    have = cache_kv.shape[1]
    kv = jnp.pad(cache_kv, ((0, 0), (rows - have, 0), (0, 0), (0, 0), (0, 0)))
    valid = jnp.broadcast_to(jnp.arange(rows) >= rows - have, (n, rows))
    return kv[:, :, 0], kv[:, :, 1], valid


def swa_mixer(h, pos, cache_kv, prm, j):
    n, l, _ = h.shape
    nq = SWA_Q_HEADS * SWA_HEAD_DIM
    nk = SWA_KV_HEADS * SWA_HEAD_DIM
    q, k, v = jnp.split(h @ prm['swa_w_qkv'][j], [nq, nq + nk], axis=-1)
    q = rope(q.reshape(n, l, SWA_Q_HEADS, SWA_HEAD_DIM), pos)
    k = rope(k.reshape(n, l, SWA_KV_HEADS, SWA_HEAD_DIM), pos)
    v = v.reshape(n, l, SWA_KV_HEADS, SWA_HEAD_DIM)
    k_pre, v_pre, valid = window_prefix(cache_kv, n, SWA_WINDOW, SWA_KV_HEADS, SWA_HEAD_DIM, h.dtype)
    o, _ = dilated_attn(q.reshape(n, l, SWA_KV_HEADS, SWA_GRP, SWA_HEAD_DIM), k, v, k_pre, v_pre, valid,
                        SWA_WINDOW, 1, prm['swa_sinks'][j])
    y = o.reshape(n, l, nq) @ prm['swa_w_o'][j]
    return y, jnp.stack([k, v], axis=2)


def ssd_scan(x, dt, a, b, c, h0):
    n, l, g, hpg, p = x.shape
    q = SSM_CHUNK if l % SSM_CHUNK == 0 else l
    nc = l // q
    xdt = (x * dt[..., None]).reshape(n, nc, q, g, hpg, p)
    b = b.reshape(n, nc, q, g, -1)
    c = c.reshape(n, nc, q, g, -1)
    cs = jnp.cumsum(jnp.moveaxis((dt * a).reshape(n, nc, q, g, hpg), 2, -1), axis=-1)
    causal = jnp.tril(jnp.ones((q, q), bool))
    seg = cs[..., :, None] - cs[..., None, :]
    decay = jnp.where(causal, jnp.exp(jnp.where(causal, seg, 0.0)), 0.0)
    cb = jnp.einsum('nclgs,ncmgs->ncglm', c, b)
    y_diag = jnp.einsum('ncglm,ncghlm,ncmghp->nclghp', cb, decay, xdt)
    to_end = jnp.exp(cs[..., -1:] - cs)
    states = jnp.einsum('ncmgs,ncghm,ncmghp->ncghps', b, to_end, xdt)
    chunk_decay = jnp.exp(cs[..., -1])

    def step(hc, inp):
        st, dc = inp
        return hc * dc[..., None, None] + st, hc

    h_last, h_in = lax.scan(step, h0, (jnp.moveaxis(states, 1, 0), jnp.moveaxis(chunk_decay, 1, 0)))
    h_in = jnp.moveaxis(h_in, 0, 1)
    y_off = jnp.einsum('nclgs,ncghps,ncghl->nclghp', c, h_in, jnp.exp(cs))
    return (y_diag + y_off).reshape(n, l, g, hpg, p), h_last


def ssd_mixer(h, conv_buf, ssm_state, prm, j):
    n, l, _ = h.shape
    z, xbc, dt = jnp.split(h @ prm['ssd_w_in'][j], [SSM_D_INNER, SSM_D_INNER + SSM_CONV_DIM], axis=-1)
    if conv_buf is None:
        conv_buf = jnp.zeros((n, SSM_CONV - 1, SSM_CONV_DIM), h.dtype)
    xbc, new_buf = causal_conv(xbc, conv_buf, prm['ssd_conv_w'][j], prm['ssd_conv_b'][j])
    xbc = jax.nn.silu(xbc)
    xs, bm, cm = jnp.split(xbc, [SSM_D_INNER, SSM_D_INNER + SSM_GROUPS * SSM_D_STATE], axis=-1)
    f32 = jnp.float32
    xs = xs.reshape(n, l, SSM_GROUPS, SSM_HPG, SSM_HEAD_DIM).astype(f32)
    bm = bm.reshape(n, l, SSM_GROUPS, SSM_D_STATE).astype(f32)
    cm = cm.reshape(n, l, SSM_GROUPS, SSM_D_STATE).astype(f32)
    dt = jax.nn.softplus(dt.astype(f32) + prm['ssd_dt_bias'][j].astype(f32)).reshape(n, l, SSM_GROUPS, SSM_HPG)
    a = -jnp.exp(prm['ssd_a_log'][j].astype(f32)).reshape(SSM_GROUPS, SSM_HPG)
    if ssm_state is None:
        h0 = jnp.zeros((n, SSM_GROUPS, SSM_HPG, SSM_HEAD_DIM, SSM_D_STATE), f32)
    else:
        h0 = ssm_state.astype(f32).reshape(n, SSM_GROUPS, SSM_HPG, SSM_HEAD_DIM, SSM_D_STATE)
    y, h_last = ssd_scan(xs, dt, a, bm, cm, h0)
    y = y + xs * prm['ssd_d'][j].astype(f32).reshape(SSM_GROUPS, SSM_HPG, 1)
    y = y.reshape(n, l, SSM_D_INNER) * jax.nn.silu(z.astype(f32))
    yg = y.reshape(n, l, SSM_GROUPS, SSM_D_INNER // SSM_GROUPS)
    yg = yg * lax.rsqrt(jnp.mean(yg * yg, axis=-1, keepdims=True) + RMS_EPS)
    y = (yg.reshape(n, l, SSM_D_INNER) * prm['ssd_norm'][j].astype(f32)).astype(h.dtype)
    new_state = h_last.reshape(n, SSM_HEADS, SSM_HEAD_DIM, SSM_D_STATE).astype(h.dtype)
    return y @ prm['ssd_w_out'][j], new_buf, new_state


def dil_mixer(h, pos, caches, prm, j):
    n, l, _ = h.shape
    nq = N_DIL * DIL_Q_HEADS * DIL_HEAD_DIM
    nk = N_DIL * DIL_KV_HEADS * DIL_HEAD_DIM
    q, k, v = jnp.split(h @ prm['dil_w_qkv'][j], [nq, nq + nk], axis=-1)
    q = rope(q.reshape(n, l, N_DIL * DIL_Q_HEADS, DIL_HEAD_DIM), pos)
    q = q.reshape(n, l, N_DIL, DIL_KV_HEADS, DIL_GRP, DIL_HEAD_DIM)
    k = rope(k.reshape(n, l, N_DIL * DIL_KV_HEADS, DIL_HEAD_DIM), pos).reshape(n, l, N_DIL, DIL_KV_HEADS, DIL_HEAD_DIM)
    v = v.reshape(n, l, N_DIL, DIL_KV_HEADS, DIL_HEAD_DIM)
    outs, lses, new_kv = [], [], []
    for g, (w, d) in enumerate(DIL_PATTERN):
        k_pre, v_pre, valid = window_prefix(None if caches is None else caches[g], n, w, DIL_KV_HEADS,
                                            DIL_HEAD_DIM, h.dtype)
        o, lse = dilated_attn(q[:, :, g], k[:, :, g], v[:, :, g], k_pre, v_pre, valid, w, d)
        outs.append(o.astype(jnp.float32))
        lses.append(lse)
        new_kv.append(jnp.stack([k[:, :, g], v[:, :, g]], axis=2))
    wts = jax.nn.softmax(jnp.stack(lses), axis=0)
    o = jnp.einsum('rnlkg,rnlkgd->nlkgd', wts, jnp.stack(outs))
    y = o.astype(h.dtype).reshape(n, l, DIL_Q_HEADS * DIL_HEAD_DIM) @ prm['dil_w_o'][j]
    return y, new_kv


def lru_mixer(h, conv_buf, h_state, prm, j):
    n, l, _ = h.shape
    f32 = jnp.float32
    gate, xb = jnp.split(h @ prm['lru_w_in'][j] + prm['lru_b_in'][j], [LRU_WIDTH], axis=-1)
    gate = jax.nn.gelu(gate)
    if conv_buf is None:
        conv_buf = jnp.zeros((n, LRU_CONV - 1, LRU_WIDTH), h.dtype)
    xb, new_buf = causal_conv(xb, conv_buf, prm['lru_conv_w'][j], prm['lru_conv_b'][j])
    xblk = xb.reshape(n, l, LRU_BLOCKS, LRU_BLOCK_DIM)
    r = jax.nn.sigmoid(jnp.einsum('nlbi,bij->nlbj', xblk, prm['lru_w_r'][j]).reshape(n, l, LRU_WIDTH)
                       + prm['lru_b_r'][j]).astype(f32)
    i = jax.nn.sigmoid(jnp.einsum('nlbi,bij->nlbj', xblk, prm['lru_w_i'][j]).reshape(n, l, LRU_WIDTH)
                       + prm['lru_b_i'][j]).astype(f32)
    log_a = -LRU_C * r * jax.nn.softplus(-prm['lru_lam'][j].astype(f32))
    a = jnp.exp(log_a)
    u = jnp.sqrt(-jnp.expm1(2.0 * log_a)) * (i * xb.astype(f32))
    h0 = jnp.zeros((n, LRU_WIDTH), f32) if h_state is None else h_state.astype(f32)
    u = u.at[:, 0].add(a[:, 0] * h0)

    def combine(left, right):
        a1, b1 = left
        a2, b2 = right
        return a1 * a2, a2 * b1 + b2

    _, hs = lax.associative_scan(combine, (a, u), axis=1)
    y = (hs.astype(h.dtype) * gate) @ prm['lru_w_out'][j]
    return y, new_buf, hs[:, -1].astype(h.dtype)


def trunk(x, c, pos, cache, prm):
    prompt = cache is None
    n, l, _ = x.shape
    new = {name: [] for name in STATE_KEYS}
    cond = jax.nn.silu(c)
    for layer in range(DEPTH):
        kind, j = layer % N_MIXERS, layer // N_MIXERS
        mod = cond @ prm['w_ada'][layer] + prm['b_ada'][layer]
        sh_m, sc_m, gt_m, sh_f, sc_f, gt_f = jnp.split(mod[:, None, :], 6, axis=-1)
        h = rmsnorm(x, prm['g_mix'][layer]) * (1.0 + sc_m) + sh_m
        if kind == 0:
            y, kv = swa_mixer(h, pos, None if prompt else cache['swa_kv'][j], prm, j)
            new['swa_kv'].append(kv[:, l - min(SWA_WINDOW, l):] if prompt else kv)
        elif kind == 1:
            y, cb, st = ssd_mixer(h, None if prompt else cache['ssd_conv'][j],
                                  None if prompt else cache['ssd'][j], prm, j)
            new['ssd_conv'].append(cb)
            new['ssd'].append(st)
        elif kind == 2:
            y, kvs = dil_mixer(h, pos, None if prompt else [cache[nm][j] for nm in DIL_KEYS], prm, j)
            for (w, _), nm, kv in zip(DIL_PATTERN, DIL_KEYS, kvs):
                new[nm].append(kv[:, l - min(w, l):] if prompt else kv)
        else:
            y, cb, st = lru_mixer(h, None if prompt else cache['lru_conv'][j],
                                  None if prompt else cache['lru'][j], prm, j)
            new['lru_conv'].append(cb)
            new['lru'].append(st)
        x = x + gt_m * y
        h = rmsnorm(x, prm['g_ffn'][layer]) * (1.0 + sc_f) + sh_f
        x = x + gt_f * (jnp.square(jax.nn.relu(h @ prm['w_ff1'][layer])) @ prm['w_ff2'][layer])
    y = rmsnorm(x, prm['g_final'])
    return y, {name: jnp.stack(v) for name, v in new.items()}


def setup_inputs(seed: int = 0) -> dict:
    key = jax.random.key(seed)
    ks = jax.random.split(key, 64)
    f32 = jnp.float32
    D = D_MODEL

    def nrm(i, shape, scale=1.0):
        return jax.random.normal(ks[i], shape, f32) * scale

    def unif(i, shape, lo, hi):
        return jax.random.uniform(ks[i], shape, f32, lo, hi)

    swa_rows = min(SWA_WINDOW, PAST_LEN)
    dil_rows = [min(w, PAST_LEN) for w, _ in DIL_PATTERN]
    dt0 = jnp.exp(unif(30, (N_SSD_LAYERS, SSM_HEADS), math.log(1e-3), math.log(1e-1)))
    a_pow = unif(40, (N_LRU_LAYERS, LRU_WIDTH), 0.9, 0.999)
    s = a_pow ** (1.0 / LRU_C)
    nq_dil = N_DIL * DIL_Q_HEADS * DIL_HEAD_DIM
    nkv_dil = 2 * N_DIL * DIL_KV_HEADS * DIL_HEAD_DIM
    return {
        'x_prompt': nrm(0, (BATCH, SEQ, D)),
        'x_sample': nrm(1, (DEC_BATCH, DEC_SEQ, D)),
        'cache_swa_kv': nrm(2, (N_SWA_LAYERS, DEC_BATCH, swa_rows, 2, SWA_KV_HEADS, SWA_HEAD_DIM)),
        'state_ssd_conv': nrm(3, (N_SSD_LAYERS, DEC_BATCH, SSM_CONV - 1, SSM_CONV_DIM)),
        'state_ssd': nrm(4, (N_SSD_LAYERS, DEC_BATCH, SSM_HEADS, SSM_HEAD_DIM, SSM_D_STATE), 0.1),
        'cache_dil_kv_w128': nrm(5, (N_DIL_LAYERS, DEC_BATCH, dil_rows[0], 2, DIL_KV_HEADS, DIL_HEAD_DIM)),
        'cache_dil_kv_w512': nrm(6, (N_DIL_LAYERS, DEC_BATCH, dil_rows[1], 2, DIL_KV_HEADS, DIL_HEAD_DIM)),
        'cache_dil_kv_w2048': nrm(7, (N_DIL_LAYERS, DEC_BATCH, dil_rows[2], 2, DIL_KV_HEADS, DIL_HEAD_DIM)),
        'state_lru_conv': nrm(8, (N_LRU_LAYERS, DEC_BATCH, LRU_CONV - 1, LRU_WIDTH)),
        'state_lru': nrm(9, (N_LRU_LAYERS, DEC_BATCH, LRU_WIDTH), 0.5),
        'c_prompt': nrm(10, (BATCH, D)),
        'c_sample': nrm(11, (DEC_BATCH, D)),
        'w_ada': nrm(12, (DEPTH, D, 6 * D), 0.5 * D ** -0.5),
        'b_ada': nrm(13, (DEPTH, 6 * D), 0.02),
        'g_mix': 1.0 + nrm(14, (DEPTH, D), 0.05),
        'g_ffn': 1.0 + nrm(15, (DEPTH, D), 0.05),
        'w_ff1': nrm(16, (DEPTH, D, D_FF), D ** -0.5),
        'w_ff2': nrm(17, (DEPTH, D_FF, D), D_FF ** -0.5),
        'g_final': 1.0 + nrm(18, (D,), 0.05),
        'swa_w_qkv': nrm(19, (N_SWA_LAYERS, D, (SWA_Q_HEADS + 2 * SWA_KV_HEADS) * SWA_HEAD_DIM), D ** -0.5),
        'swa_sinks': nrm(20, (N_SWA_LAYERS, SWA_Q_HEADS), 0.5),
        'swa_w_o': nrm(21, (N_SWA_LAYERS, SWA_Q_HEADS * SWA_HEAD_DIM, D), (SWA_Q_HEADS * SWA_HEAD_DIM) ** -0.5),
        'ssd_w_in': nrm(22, (N_SSD_LAYERS, D, SSM_D_INNER + SSM_CONV_DIM + SSM_HEADS), D ** -0.5),
        'ssd_conv_w': nrm(23, (N_SSD_LAYERS, SSM_CONV, SSM_CONV_DIM), SSM_CONV ** -0.5),
        'ssd_conv_b': nrm(24, (N_SSD_LAYERS, SSM_CONV_DIM), 0.02),
        'ssd_dt_bias': dt0 + jnp.log(-jnp.expm1(-dt0)),
        'ssd_a_log': jnp.log(unif(25, (N_SSD_LAYERS, SSM_HEADS), 1.0, 16.0)),
        'ssd_d': 1.0 + nrm(26, (N_SSD_LAYERS, SSM_HEADS), 0.1),
        'ssd_norm': 1.0 + nrm(27, (N_SSD_LAYERS, SSM_D_INNER), 0.05),
        'ssd_w_out': nrm(28, (N_SSD_LAYERS, SSM_D_INNER, D), SSM_D_INNER ** -0.5),
        'dil_w_qkv': nrm(29, (N_DIL_LAYERS, D, nq_dil + nkv_dil), D ** -0.5),
        'dil_w_o': nrm(31, (N_DIL_LAYERS, DIL_Q_HEADS * DIL_HEAD_DIM, D), (DIL_Q_HEADS * DIL_HEAD_DIM) ** -0.5),
        'lru_w_in': nrm(32, (N_LRU_LAYERS, D, 2 * LRU_WIDTH), D ** -0.5),
        'lru_b_in': nrm(33, (N_LRU_LAYERS, 2 * LRU_WIDTH), 0.02),
        'lru_conv_w': nrm(34, (N_LRU_LAYERS, LRU_CONV, LRU_WIDTH), LRU_CONV ** -0.5),
        'lru_conv_b': nrm(35, (N_LRU_LAYERS, LRU_WIDTH), 0.02),
        'lru_w_r': nrm(36, (N_LRU_LAYERS, LRU_BLOCKS, LRU_BLOCK_DIM, LRU_BLOCK_DIM), LRU_BLOCK_DIM ** -0.5),
        'lru_b_r': nrm(37, (N_LRU_LAYERS, LRU_WIDTH), 0.02),
        'lru_w_i': nrm(38, (N_LRU_LAYERS, LRU_BLOCKS, LRU_BLOCK_DIM, LRU_BLOCK_DIM), LRU_BLOCK_DIM ** -0.5),
        'lru_b_i': nrm(39, (N_LRU_LAYERS, LRU_WIDTH), 0.02),
        'lru_lam': jnp.log(s) - jnp.log1p(-s),
        'lru_w_out': nrm(41, (N_LRU_LAYERS, LRU_WIDTH, D), LRU_WIDTH ** -0.5),
    }


def reference(x_prompt, x_sample, cache_swa_kv, state_ssd_conv, state_ssd, cache_dil_kv_w128, cache_dil_kv_w512,
              cache_dil_kv_w2048, state_lru_conv, state_lru, c_prompt, c_sample, w_ada, b_ada, g_mix, g_ffn,
              w_ff1, w_ff2, g_final, swa_w_qkv, swa_sinks, swa_w_o, ssd_w_in, ssd_conv_w, ssd_conv_b,
              ssd_dt_bias, ssd_a_log, ssd_d, ssd_norm, ssd_w_out, dil_w_qkv, dil_w_o, lru_w_in, lru_b_in,
              lru_conv_w, lru_conv_b, lru_w_r, lru_b_r, lru_w_i, lru_b_i, lru_lam, lru_w_out):
    prm = dict(w_ada=w_ada, b_ada=b_ada, g_mix=g_mix, g_ffn=g_ffn, w_ff1=w_ff1, w_ff2=w_ff2, g_final=g_final,
               swa_w_qkv=swa_w_qkv, swa_sinks=swa_sinks, swa_w_o=swa_w_o,
               ssd_w_in=ssd_w_in, ssd_conv_w=ssd_conv_w, ssd_conv_b=ssd_conv_b, ssd_dt_bias=ssd_dt_bias,
               ssd_a_log=ssd_a_log, ssd_d=ssd_d, ssd_norm=ssd_norm, ssd_w_out=ssd_w_out,
               dil_w_qkv=dil_w_qkv, dil_w_o=dil_w_o,
               lru_w_in=lru_w_in, lru_b_in=lru_b_in, lru_conv_w=lru_conv_w, lru_conv_b=lru_conv_b,
               lru_w_r=lru_w_r, lru_b_r=lru_b_r, lru_w_i=lru_w_i, lru_b_i=lru_b_i, lru_lam=lru_lam,
               lru_w_out=lru_w_out)
    cache = dict(swa_kv=cache_swa_kv, ssd_conv=state_ssd_conv, ssd=state_ssd, dil_kv_w128=cache_dil_kv_w128,
                 dil_kv_w512=cache_dil_kv_w512, dil_kv_w2048=cache_dil_kv_w2048, lru_conv=state_lru_conv,
                 lru=state_lru)
    pos_p = jnp.arange(x_prompt.shape[1], dtype=jnp.int32)
    pos_s = PAST_LEN + jnp.arange(x_sample.shape[1], dtype=jnp.int32)
    y_prompt, sp = trunk(x_prompt, c_prompt, pos_p, None, prm)
    y_sample, ss = trunk(x_sample, c_sample, pos_s, cache, prm)
    return (y_prompt, y_sample,
            sp['swa_kv'], ss['swa_kv'],
            sp['ssd_conv'], ss['ssd_conv'],
            sp['ssd'], ss['ssd'],
            sp['dil_kv_w128'], ss['dil_kv_w128'],
            sp['dil_kv_w512'], ss['dil_kv_w512'],
            sp['dil_kv_w2048'], ss['dil_kv_w2048'],
            sp['lru_conv'], ss['lru_conv'],
            sp['lru'], ss['lru'])
```

```python
import contextlib
import math
import numpy as np
import concourse.bass as bass
import concourse.mybir as mybir
from concourse.bass_utils import run_bass_kernel_spmd

F32 = mybir.dt.float32
BF16 = mybir.dt.bfloat16
AF = mybir.ActivationFunctionType
ALU = mybir.AluOpType
AX = mybir.AxisListType

NCORES = 8
D = 2048
KD = D // 128
DFF = 8192
DEPTH = 4
LP = 2048
NS = 4
LS = 8
TS = NS * LS
T = LP + TS
NCOND = 1 + NS
PAST = 16384
EPS = 1e-6
TILES = [(0, 512), (512, 512), (1024, 512), (1536, 544)]
TT = 544
WSLAB = 8192
NEG = -1e30
MASKV = -30000.0
NA = 49 * 1024

DEBUG = {}


class Trk:
    __slots__ = ("w", "r")

    def __init__(self):
        self.w = None
        self.r = {}


class Sched:
    def __init__(self, nc, es, n_dma_sems=56):
        self.nc = nc
        self.es = es
        self.sems = []
        self.engs = {}
        for name in ("pe", "act", "dve", "pool", "sp"):
            sem = es.enter_context(nc.semaphore("s_" + name))
            self.sems.append(sem)
            self.engs[name] = dict(semi=len(self.sems) - 1, cnt=0, prog=[], seen={})
        self.dslots = []
        for i in range(n_dma_sems):
            sem = es.enter_context(nc.semaphore("d_%d" % i))
            self.sems.append(sem)
            self.dslots.append([len(self.sems) - 1, 0])
        self.drr = 0
        self.same_engine_sync = True

    def _waits(self, ename, raw, war):
        E = self.engs[ename]
        need = {}
        for (semi, val) in raw:
            if semi == E["semi"] and (ename == "pe" or not self.same_engine_sync):
                continue
            if need.get(semi, 0) < val:
                need[semi] = val
        for (semi, val) in war:
            if semi == E["semi"]:
                continue
            if need.get(semi, 0) < val:
                need[semi] = val
        out = []
        for semi, val in need.items():
            if E["seen"].get(semi, 0) >= val:
                continue
            E["seen"][semi] = val
            out.append((semi, val))
        return out

    def op(self, ename, fn, reads=(), writes=()):
        E = self.engs[ename]
        raw = [t.w for t in reads if t.w] + [t.w for t in writes if t.w]
        war = [ev for t in writes for ev in t.r.values()]
        waits = self._waits(ename, raw, war)
        E["cnt"] += 1
        ev = (E["semi"], E["cnt"])
        for t in reads:
            t.r[ename] = ev
        for t in writes:
            t.w = ev
            t.r = {}
        E["prog"].append((waits, fn, True))

    def dma(self, qname, out, in_, reads=(), writes=(), **kw):
        Q = self.engs[qname]
        slot = self.dslots[self.drr]
        self.drr = (self.drr + 1) % len(self.dslots)
        raw = [t.w for t in reads if t.w] + [t.w for t in writes if t.w]
        if slot[1] > 0:
            raw.append((slot[0], 16 * slot[1]))
        war = [ev for t in writes for ev in t.r.values()]
        waits = self._waits(qname, raw, war)
        slot[1] += 1
        ev = (slot[0], 16 * slot[1])
        for t in reads:
            t.r[("d", slot[0])] = ev
        for t in writes:
            t.w = ev
            t.r = {}
        sem = self.sems[slot[0]]
        Q["prog"].append((waits, lambda e: e.dma_start(out=out, in_=in_, **kw).then_inc(sem, 16), False))

    def barrier(self):
        for ename, E in self.engs.items():
            raw = [(F["semi"], F["cnt"]) for fn, F in self.engs.items() if fn != ename and F["cnt"] > 0]
            raw += [(s[0], 16 * s[1]) for s in self.dslots if s[1] > 0]
            waits = self._waits(ename, [], raw)
            if waits:
                E["prog"].append((waits, None, False))

    def emit(self):
        sems = self.sems

        def mk(ename):
            E = self.engs[ename]

            def body(e):
                for waits, fn, inc in E["prog"]:
                    for (semi, val) in waits:
                        e.wait_ge(sems[semi], val)
                    if fn is not None:
                        ins = fn(e)
                        if inc:
                            ins.then_inc(sems[E["semi"]], 1)
            return body

        with self.nc.Block() as block:
            block.tensor(mk("pe"))
            block.scalar(mk("act"))
            block.vector(mk("dve"))
            block.gpsimd(mk("pool"))
            block.sync(mk("sp"))


class Tile:
    def __init__(self, ap, nsub=0, trk=None, sub=None):
        self.ap = ap
        self.k = trk if trk is not None else Trk()
        self.sub = sub if sub is not None else [Trk() for _ in range(nsub)]

    def __getitem__(self, key):
        return self.ap[key]

    @property
    def all(self):
        return [self.k] + self.sub


class Ctx:
    pass


def splits(n, m=512):
    out = []
    c = 0
    while c < n:
        out.append((c, min(m, n - c)))
        c += m
    return out


def col_segments(t0, nt):
    segs = []
    t1 = t0 + nt
    if t0 < LP:
        segs.append((0, min(t1, LP) - t0, "p", 0, 1))
    if t1 > LP:
        s0 = max(t0, LP)
        assert (s0 - LP) % LS == 0 and (t1 - LP) % LS == 0
        segs.append((s0 - t0, t1 - s0, "s", 1 + (s0 - LP) // LS, (t1 - s0) // LS))
    return segs


def tile_w(W, ncols):
    K, N = W.shape
    assert K % 128 == 0 and N % ncols == 0
    return np.ascontiguousarray(W.reshape(K // 128, 128, N // ncols, ncols).transpose(2, 1, 0, 3))


def build(dbg=None):
    dbg = dbg or {}
    nc = bass.Bass("TRN2", target_bir_lowering=False)
    es = contextlib.ExitStack()
    C = Ctx()
    C.nc = nc
    C.dbg = dbg
    S = Sched(nc, es)
    C.S = S

    def din(name, shape, dt=F32):
        return nc.dram_tensor(name, list(shape), dt, kind="ExternalInput").ap()

    def dout(name, shape, dt=F32):
        return nc.dram_tensor(name, list(shape), dt, kind="ExternalOutput").ap()

    def dscr(name, shape, dt=F32):
        if dbg.get("dump_" + name):
            return nc.dram_tensor(name, list(shape), dt, kind="ExternalOutput").ap()
        return nc.dram_tensor(name, list(shape), dt, kind="Internal").ap()

    C.din, C.dout, C.dscr = din, dout, dscr

    I = Ctx()
    C.I = I
    I.xin = din("xin", [T, D])
    I.cond = din("cond", [NCOND, D])
    I.ident = din("ident", [128, 128])
    I.vecs = din("vecs", [NVEC, 128])
    I.w_ada = din("w_ada", [DEPTH, 24, 128, KD * 512])
    I.w_ff1 = din("w_ff1", [DEPTH, 16, 128, KD * 512])
    I.w_ff2 = din("w_ff2", [DEPTH, 16, 128, 64 * 128])
    I.lru_conv_in = din("lru_conv_in", [NS * 3, D])
    I.lru_h0 = din("lru_h0", [NS, D])
    I.lru_win = din("lru_win", [8, 128, KD * 512])
    I.lru_wr = din("lru_wr", [128, 4096])
    I.lru_wi = din("lru_wi", [128, 4096])
    I.lru_wout = din("lru_wout", [4, 128, KD * 512])
    I.rope64 = din("rope64", [2, 128, T])
    I.rope128 = din("rope128", [2, 128, T])
    I.perm64 = din("perm64", [128, 128])
    I.perm128 = din("perm128", [128, 128])
    I.pmask = din("pmask", [128, 256])
    I.smask = din("smask", [3, 32, 2176])
    I.swa_sinks = din("swa_sinks", [1, 32])
    I.swa_sinkS = din("swa_sinkS", [32, 8])
    I.swa_cache = din("swa_cache", [NS, 128, 2, 4, 64])
    I.swa_wq = din("swa_wq", [4, 128, KD * 512])
    I.swa_wk = din("swa_wk", [2, 128, KD * 512])
    I.swa_wv = din("swa_wv", [1, 128, KD * 256])
    I.swa_wqp = din("swa_wqp", [4, 128, KD * 512])
    I.swa_wkp = din("swa_wkp", [2, 128, KD * 512])
    I.dil_wqp = din("dil_wqp", [12, 128, KD * 512])
    I.dil_wkp = din("dil_wkp", [3, 128, KD * 512])
    I.swa_wo = din("swa_wo", [4, 128, KD * 512])
    I.dil_c0 = din("dil_c0", [NS, 128, 2, 4, 128])
    I.dil_c1 = din("dil_c1", [NS, 512, 2, 4, 128])
    I.dil_c2 = din("dil_c2", [NS, 2048, 2, 4, 128])
    I.dil_wq = din("dil_wq", [12, 128, KD * 512])
    I.dil_wk = din("dil_wk", [3, 128, KD * 512])
    I.dil_wv = din("dil_wv", [3, 128, KD * 512])
    I.dil_wo = din("dil_wo", [4, 128, KD * 512])
    I.ssd_conv_in = din("ssd_conv_in", [NS * 3, 6144])
    I.ssd_h0 = din("ssd_h0", [NS, 32, 128, 128])
    I.ssd_wz = din("ssd_wz", [8, 128, KD * 512])
    I.ssd_wx = din("ssd_wx", [12, 128, KD * 512])
    I.ssd_wdt = din("ssd_wdt", [1, 128, KD * 128])
    I.ssd_wout = din("ssd_wout", [8, 128, 32 * 256])
    O = Ctx()
    C.O = O
    O.ssd_conv_p = dout("ssd_conv_p", [3, 6144])
    O.ssd_conv_s = dout("ssd_conv_s", [NS * 3, 6144])
    O.ssd_p = dout("ssd_p", [32, 128, 128])
    O.ssd_s = dout("ssd_s", [NS, 32, 128, 128])
    C.zT = dscr("zT", [32, 128, T], BF16)
    C.xsT = dscr("xsT", [32, 128, T], F32)
    C.bcT = dscr("bcT", [16, 128, T], BF16)
    C.ygT = dscr("ygT", [32, 128, T], F32)
    O.swa_kv_p = dout("swa_kv_p", [128, 512])
    O.swa_kv_s = dout("swa_kv_s", [TS, 512])
    for g, w in enumerate((128, 512, 2048)):
        setattr(O, "dil_kv%d_p" % g, dout("dil_kv%d_p" % g, [w, 1024]))
        setattr(O, "dil_kv%d_s" % g, dout("dil_kv%d_s" % g, [TS, 1024]))
    C.Oh = dscr("Oh", [3, T, D])
    C.lseh = dscr("lseh", [3, T, 32])
    O.y = dout("y", [T, D])
    O.lru_conv_p = dout("lru_conv_p", [3, D])
    O.lru_conv_s = dout("lru_conv_s", [NS * 3, D])
    O.lru_p = dout("lru_p", [1, D])
    O.lru_s = dout("lru_s", [NS, D])
    C.oT = dscr("oT", [32, 128, T], BF16)
    C.gT = dscr("gT", [KD, 128, T], BF16)
    C.xres = dscr("xres", [KD, 128, T])

    arena = es.enter_context(nc.sbuf_tensor("arena", [128, NA], F32))
    C.arena = arena
    consts = es.enter_context(nc.sbuf_tensor("consts", [128, 128 + 64 + 128 + NVEC], F32))
    C.ident = Tile(consts[:, 0:128])
    C.identb = Tile(consts[:, 128:192].bitcast(BF16))
    C.onesb = Tile(consts[:, 192:256].bitcast(BF16))
    C.onesf = Tile(consts[:, 256:320])
    C.vec = Tile(consts[:, 320:320 + NVEC])
    psall = es.enter_context(nc.psum_tensor("psall", [128, 4096], F32))
    C.psall = psall
    C.banks = [Tile(psall[:, i * 512:(i + 1) * 512]) for i in range(8)]
    C.brr = 0

    prologue(C)
    for l in range(DEPTH):
        layer(C, l)
    final(C)
    S.barrier()
    S.emit()
    es.close()
    return nc


def bank(C):
    b = C.banks[C.brr]
    C.brr = (C.brr + 1) % 8
    return b


def mbank(C, nb):
    if C.brr + nb > 8:
        C.brr = 0
    i0 = C.brr
    C.brr = (C.brr + nb) % 8
    t = Tile(C.psall[:, i0 * 512:(i0 + nb) * 512], trk=C.banks[i0].k, sub=[C.banks[i].k for i in range(i0 + 1, i0 + nb)])
    t.banks = [C.banks[i] for i in range(i0, i0 + nb)]
    return t


class Arena:
    def __init__(self, C):
        self.C = C
        self.off = 0

    def f32(self, n, shape=None, nsub=0):
        ap = self.C.arena[:, self.off:self.off + n]
        self.off += n
        assert self.off <= getattr(self.C, "ptop", NA), "arena overflow %d" % self.off
        if shape:
            ap = ap.rearrange(shape[0], **shape[1])
        return Tile(ap, nsub)

    def bf16(self, n, shape=None, nsub=0):
        n32 = (n + 1) // 2
        ap = self.C.arena[:, self.off:self.off + n32].bitcast(BF16)
        self.off += n32
        assert self.off <= getattr(self.C, "ptop", NA), "arena overflow %d" % self.off
        if shape:
            ap = ap.rearrange(shape[0], **shape[1])
        return Tile(ap, nsub)


V_BADA = 0
V_GMIX = V_BADA + DEPTH * 96
V_GFFN = V_GMIX + DEPTH * 16
V_GFIN = V_GFFN + DEPTH * 16
V_LBIN = V_GFIN + 16
V_LCW = V_LBIN + 32
V_LCB = V_LCW + 64
V_LBR = V_LCB + 16
V_LBI = V_LBR + 16
V_LLAM = V_LBI + 16
V_SCW = V_LLAM + 16
V_SCB = V_SCW + 192
V_SNORM = V_SCB + 48
V_SD = V_SNORM + 32
V_SDTB = V_SD + 32
V_SALOG = V_SDTB + 1
NVEC = V_SALOG + 1


def prologue(C):
    S, I = C.S, C.I
    A = Arena(C)
    S.dma("sp", C.ident.ap, I.ident, writes=[C.ident.k])
    S.op("dve", lambda e: e.tensor_copy(C.identb.ap, C.ident.ap), reads=[C.ident.k], writes=[C.identb.k])
    S.op("dve", lambda e: e.memset(C.onesb.ap, 1.0), writes=[C.onesb.k])
    S.op("dve", lambda e: e.memset(C.onesf.ap, 1.0), writes=[C.onesf.k])
    stg = A.f32(128, nsub=0)
    r0 = 0
    while r0 < NVEC:
        n = min(128, NVEC - r0)
        S.dma("sp", stg.ap[0:n, :], I.vecs[r0:r0 + n, :], writes=[stg.k])
        b = bank(C)
        S.op("pe", lambda e, n=n, b=b: e.transpose(b.ap[:, 0:n], stg.ap[0:n, :], C.ident.ap[0:n, 0:n]),
             reads=[stg.k, C.ident.k], writes=[b.k])
        S.op("dve", lambda e, n=n, b=b, r0=r0: e.tensor_copy(C.vec.ap[:, r0:r0 + n], b.ap[:, 0:n]),
             reads=[b.k], writes=[C.vec.k])
        r0 += n
    cst = A.f32(D)
    S.dma("sp", cst.ap[0:NCOND, :], I.cond, writes=[cst.k])
    S.op("act", lambda e: e.activation(out=cst.ap[0:NCOND, :], in_=cst.ap[0:NCOND, :], func=AF.Silu),
         reads=[cst.k], writes=[cst.k])
    scT = es_persist(C, "scT", KD * NCOND, BF16)
    C.scT = scT
    b = bank(C)
    for k in range(KD):
        S.op("pe", lambda e, k=k: e.transpose(b.ap[:, k * NCOND:(k + 1) * NCOND], cst.ap[0:NCOND, k * 128:(k + 1) * 128],
                                              C.ident.ap[0:NCOND, 0:NCOND]),
             reads=[cst.k, C.ident.k], writes=[b.k])
    S.op("dve", lambda e: e.tensor_copy(scT.ap, b.ap[:, 0:KD * NCOND]), reads=[b.k], writes=[scT.k])
    xin_t = [A.f32(D) for _ in range(2)]
    xo_t = [A.f32(KD * 128, ("p (k t) -> p k t", dict(k=KD))) for _ in range(2)]
    nchunk = (T + 127) // 128
    for c in range(nchunk):
        t0 = c * 128
        n = min(128, T - t0)
        xi = xin_t[c % 2]
        xo = xo_t[c % 2]
        S.dma("sp", xi.ap[0:n, :], I.xin[t0:t0 + n, :], writes=[xi.k])
        for g in range(4):
            b = bank(C)
            for j in range(4):
                k = g * 4 + j
                S.op("pe", lambda e, k=k, j=j, b=b, n=n, xi=xi: e.transpose(
                    b.ap[:, j * 128:j * 128 + n], xi.ap[0:n, k * 128:(k + 1) * 128], C.ident.ap[0:n, 0:n]),
                    reads=[xi.k, C.ident.k], writes=[b.k])
            eng = "act" if g % 2 == 0 else "dve"
            if eng == "act":
                S.op("act", lambda e, g=g, b=b, n=n, xo=xo: e.copy(
                    xo.ap[:, g * 4:(g + 1) * 4, 0:n], b.ap.rearrange("p (j t) -> p j t", j=4)[:, :, 0:n]),
                    reads=[b.k], writes=[xo.k])
            else:
                S.op("dve", lambda e, g=g, b=b, n=n, xo=xo: e.tensor_copy(
                    xo.ap[:, g * 4:(g + 1) * 4, 0:n], b.ap.rearrange("p (j t) -> p j t", j=4)[:, :, 0:n]),
                    reads=[b.k], writes=[xo.k])
        S.dma("sp", C.xres[:, :, t0:t0 + n].rearrange("k p t -> p k t"), xo.ap[:, :, 0:n], reads=[xo.k])
    S.barrier()


def es_persist(C, name, n, dt):
    if not hasattr(C, "ptop"):
        C.ptop = NA
    n32 = n if dt == F32 else (n + 1) // 2
    C.ptop -= n32
    ap = C.arena[:, C.ptop:C.ptop + n32]
    if dt == BF16:
        ap = ap.bitcast(BF16)
    return Tile(ap)


def wslab_load(C, wb, src, n):
    S = C.S
    assert n % 2048 == 0 or n <= 2048
    if n > 2048:
        S.dma("pool", wb.ap[:, 0:n].rearrange("p (a b) -> p a b", b=2048),
              src.rearrange("p (a b) -> p a b", b=2048), writes=[wb.k])
    else:
        S.dma("pool", wb.ap[:, 0:n], src, writes=[wb.k])


def ada_phase(C, l, A):
    S, I = C.S, C.I
    wbs = C.wbs
    modT = C.modT
    b = bank(C)
    for s in range(24):
        wb = wbs[C.wrr % len(wbs)]
        C.wrr += 1
        wslab_load(C, wb, I.w_ada[l, s], KD * 512)
        wv = wb.ap[:, 0:KD * 512].rearrange("p (k c) -> p k c", k=KD)
        for j in range(4):
            n = s * 4 + j
            for k in range(KD):
                S.op("pe", lambda e, n=n, j=j, k=k, wv=wv: e.matmul(
                    b.ap[:, n * NCOND:(n + 1) * NCOND], wv[:, k, j * 128:(j + 1) * 128], C.scT.ap[:, k * NCOND:(k + 1) * NCOND],
                    start=(k == 0), stop=(k == KD - 1)),
                    reads=[wb.k, C.scT.k], writes=[b.k])
    bv = C.vec.ap[:, V_BADA + l * 96:V_BADA + (l + 1) * 96]
    S.op("dve", lambda e: e.tensor_tensor(
        modT.ap, b.ap[:, 0:96 * NCOND].rearrange("p (n c) -> p n c", c=NCOND),
        bv.unsqueeze(2).broadcast_to([128, 96, NCOND]), ALU.add),
        reads=[b.k, C.vec.k], writes=[modT.k])
    for (dst, sc0, g0) in ((C.Amix, 16, V_GMIX + l * 16), (C.Affn, 64, V_GFFN + l * 16)):
        gv = C.vec.ap[:, g0:g0 + 16]
        S.op("dve", lambda e, dst=dst, sc0=sc0, gv=gv: e.scalar_tensor_tensor(
            out=dst.ap, in0=modT.ap[:, sc0:sc0 + 16, :], scalar=1.0,
            in1=gv.unsqueeze(2).broadcast_to([128, 16, NCOND]), op0=ALU.add, op1=ALU.mult),
            reads=[modT.k, C.vec.k], writes=[dst.k])


def mod_cols(ap3, seg, K=KD):
    c0, n, kind, ci, nseq = seg
    if kind == "p":
        return ap3[:, :, 0:1].broadcast_to([128, K, n])
    return ap3[:, :, ci:ci + nseq].unsqueeze(3).broadcast_to([128, K, nseq, LS])


def seg_view(ap3, seg):
    c0, n, kind, ci, nseq = seg
    v = ap3[:, :, c0:c0 + n]
    if kind == "s":
        v = v.rearrange("p k (s t) -> p k s t", t=LS)
    return v


def norm_mod(C, xT, hT, sq, rstd, tmp, nt, t0, Amod, shift0):
    S = C.S
    S.op("act", lambda e: e.activation(out=sq.ap[:, :, 0:nt], in_=xT.ap[:, :, 0:nt], func=AF.Square),
         reads=xT.all, writes=[sq.k])
    for (c0, cn) in splits(nt):
        b = bank(C)
        for k in range(KD):
            S.op("pe", lambda e, k=k, b=b, c0=c0, cn=cn: e.matmul(
                b.ap[:, 0:cn], C.onesb.ap, sq.ap[:, k, c0:c0 + cn], start=(k == 0), stop=(k == KD - 1)),
                reads=[sq.k, C.onesb.k], writes=[b.k])
        S.op("act", lambda e, b=b, c0=c0, cn=cn: e.activation(
            out=rstd.ap[:, c0:c0 + cn], in_=b.ap[:, 0:cn], func=AF.Sqrt, bias=C.epsb.ap, scale=1.0 / D),
            reads=[b.k, C.epsb.k], writes=[rstd.k])
    S.op("dve", lambda e: e.reciprocal(rstd.ap[:, 0:nt], rstd.ap[:, 0:nt]), reads=[rstd.k], writes=[rstd.k])
    Bmod = C.modT.ap[:, shift0:shift0 + 16, :]
    HK = KD // 2
    for kh in range(2):
        ks = slice(kh * HK, (kh + 1) * HK)
        for seg in col_segments(t0, nt):
            c0, n, kind, ci, nseq = seg
            if kind == "p":
                rv = rstd.ap[:, c0:c0 + n].unsqueeze(1).broadcast_to([128, HK, n])
            else:
                rv = rstd.ap[:, c0:c0 + n].rearrange("p (s t) -> p s t", t=LS).unsqueeze(1).broadcast_to([128, HK, nseq, LS])
            tv = seg_view(tmp.ap, seg)
            S.op("dve", lambda e, seg=seg, rv=rv, ks=ks, tv=tv: e.tensor_tensor(tv, seg_view(xT.ap[:, ks], seg), rv, ALU.mult),
                 reads=xT.all + [rstd.k], writes=[tmp.k])
            S.op("dve", lambda e, seg=seg, ks=ks, tv=tv: e.tensor_tensor(tv, tv, mod_cols(Amod.ap[:, ks], seg, HK), ALU.mult),
                 reads=[tmp.k, Amod.k], writes=[tmp.k])
            S.op("dve", lambda e, seg=seg, ks=ks, tv=tv: e.tensor_tensor(seg_view(hT.ap[:, ks], seg), tv, mod_cols(Bmod[:, ks], seg, HK), ALU.add),
                 reads=[tmp.k, C.modT.k], writes=[hT.k])


def resid_update(C, xT, k, bk, c0, cn, t0, gate0):
    S = C.S
    for seg in col_segments(t0 + c0, cn):
        s0, n, kind, ci, nseq = seg
        lo = c0 + s0
        if kind == "p":
            S.op("dve", lambda e, lo=lo, n=n, s0=s0: e.scalar_tensor_tensor(
                out=xT.ap[:, k, lo:lo + n], in0=bk.ap[:, s0:s0 + n], scalar=C.modT.ap[:, gate0 + k, 0:1],
                in1=xT.ap[:, k, lo:lo + n], op0=ALU.mult, op1=ALU.add),
                reads=[bk.k, C.modT.k, xT.sub[k]], writes=[xT.sub[k]])
        else:
            for q in range(nseq):
                S.op("dve", lambda e, lo=lo, s0=s0, q=q, ci=ci: e.scalar_tensor_tensor(
                    out=xT.ap[:, k, lo + q * LS:lo + (q + 1) * LS], in0=bk.ap[:, s0 + q * LS:s0 + (q + 1) * LS],
                    scalar=C.modT.ap[:, gate0 + k, ci + q:ci + q + 1],
                    in1=xT.ap[:, k, lo + q * LS:lo + (q + 1) * LS], op0=ALU.mult, op1=ALU.add),
                    reads=[bk.k, C.modT.k, xT.sub[k]], writes=[xT.sub[k]])


def ffn_phase(C, l, A, mixer_out=None):
    S, I = C.S, C.I
    xT = A.f32(KD * TT, ("p (k t) -> p k t", dict(k=KD)), nsub=KD)
    sq = A.bf16(KD * TT, ("p (k t) -> p k t", dict(k=KD)))
    hT = A.bf16(KD * TT, ("p (k t) -> p k t", dict(k=KD)))
    tmp = A.f32((KD // 2) * TT, ("p (k t) -> p k t", dict(k=KD // 2)))
    uT = A.bf16(64 * TT, ("p (k t) -> p k t", dict(k=64)), nsub=64)
    rstd = A.f32(TT)
    rl = [A.f32(512) for _ in range(2)]
    wbs = C.wbs
    for (t0, nt) in TILES:
        sp = splits(nt)
        S.dma("sp", xT.ap[:, :, 0:nt], C.xres[:, :, t0:t0 + nt].rearrange("k p t -> p k t"), writes=xT.all)
        if mixer_out is not None:
            mixer_out(C, l, xT, uT, t0, nt)
        norm_mod(C, xT, hT, sq, rstd, tmp, nt, t0, C.Affn, 48)
        for s in range(16):
            wb = wbs[C.wrr % len(wbs)]
            C.wrr += 1
            wslab_load(C, wb, I.w_ff1[l, s], KD * 512)
            wv = wb.ap[:, 0:KD * 512].rearrange("p (k c) -> p k c", k=KD)
            for j in range(4):
                n = s * 4 + j
                for (c0, cn) in sp:
                    b = bank(C)
                    for k in range(KD):
                        S.op("pe", lambda e, b=b, k=k, j=j, c0=c0, cn=cn, wv=wv: e.matmul(
                            b.ap[:, 0:cn], wv[:, k, j * 128:(j + 1) * 128], hT.ap[:, k, c0:c0 + cn],
                            start=(k == 0), stop=(k == KD - 1)),
                            reads=[wb.k, hT.k], writes=[b.k])
                    r = rl[C.rrr % 2]
                    C.rrr += 1
                    S.op("act", lambda e, b=b, r=r, cn=cn: e.activation(out=r.ap[:, 0:cn], in_=b.ap[:, 0:cn], func=AF.Relu),
                         reads=[b.k], writes=[r.k])
                    S.op("dve", lambda e, r=r, n=n, c0=c0, cn=cn: e.tensor_tensor(
                        uT.ap[:, n, c0:c0 + cn], r.ap[:, 0:cn], r.ap[:, 0:cn], ALU.mult),
                        reads=[r.k], writes=[uT.sub[n]])
        for s in range(16):
            wb = wbs[C.wrr % len(wbs)]
            C.wrr += 1
            wslab_load(C, wb, I.w_ff2[l, s], 64 * 128)
            wv = wb.ap[:, 0:64 * 128].rearrange("p (k c) -> p k c", k=64)
            for (c0, cn) in sp:
                b = bank(C)
                for n in range(64):
                    S.op("pe", lambda e, b=b, n=n, c0=c0, cn=cn, wv=wv: e.matmul(
                        b.ap[:, 0:cn], wv[:, n, :], uT.ap[:, n, c0:c0 + cn], start=(n == 0), stop=(n == 63)),
                        reads=[wb.k, uT.sub[n]], writes=[b.k])
                resid_update(C, xT, s, b, c0, cn, t0, 80)
        S.dma("sp", C.xres[:, :, t0:t0 + nt].rearrange("k p t -> p k t"), xT.ap[:, :, 0:nt], reads=xT.all)


TSPL = splits(T)


def hT_all_phase(C, l, A):
    S = C.S
    hTa = A.bf16(KD * T, ("p (k t) -> p k t", dict(k=KD)))
    mark = A.off
    xT = A.f32(KD * TT, ("p (k t) -> p k t", dict(k=KD)))
    sq = A.bf16(KD * TT, ("p (k t) -> p k t", dict(k=KD)))
    tmp = A.f32((KD // 2) * TT, ("p (k t) -> p k t", dict(k=KD // 2)))
    rstd = A.f32(TT)
    for (t0, nt) in TILES:
        S.dma("sp", xT.ap[:, :, 0:nt], C.xres[:, :, t0:t0 + nt].rearrange("k p t -> p k t"), writes=xT.all)
        hv = Tile(hTa.ap[:, :, t0:t0 + nt], trk=hTa.k)
        norm_mod(C, xT, hv, sq, rstd, tmp, nt, t0, C.Amix, 0)
    S.barrier()
    A.off = mark
    return hTa


def proj_fm(C, hTa, wsrc, nslab, ncols, epi, K=KD, after_chunk=None):
    S = C.S
    npc = ncols // 128
    for s in range(nslab):
        wb = C.wbs[C.wrr % len(C.wbs)]
        C.wrr += 1
        wslab_load(C, wb, wsrc[s], K * ncols)
        wv = wb.ap[:, 0:K * ncols].rearrange("p (k c) -> p k c", k=K)
        for j in range(npc):
            n = s * npc + j
            for (c0, cn) in TSPL:
                b = bank(C)
                for k in range(K):
                    S.op("pe", lambda e, b=b, k=k, j=j, c0=c0, cn=cn, wv=wv: e.matmul(
                        b.ap[:, 0:cn], wv[:, k, j * 128:(j + 1) * 128], hTa.ap[:, k, c0:c0 + cn],
                        start=(k == 0), stop=(k == K - 1)),
                        reads=[wb.k, hTa.k], writes=[b.k])
                epi(n, c0, cn, b)
            if after_chunk is not None:
                after_chunk(n)


def proj_fm2(C, hTa, wsrcA, wsrcB, nslab, ncols, epi2, K=KD):
    S = C.S
    npc = ncols // 128
    for s in range(nslab):
        wA, wB = C.wbs[0], C.wbs[1]
        wslab_load(C, wA, wsrcA[s], K * ncols)
        wslab_load(C, wB, wsrcB[s], K * ncols)
        wvs = [w.ap[:, 0:K * ncols].rearrange("p (k c) -> p k c", k=K) for w in (wA, wB)]
        for j in range(npc):
            n = s * npc + j
            for (c0, cn) in TSPL:
                bs = []
                for wi, (wb, wv) in enumerate(zip((wA, wB), wvs)):
                    b = bank(C)
                    for k in range(K):
                        S.op("pe", lambda e, b=b, k=k, j=j, c0=c0, cn=cn, wv=wv: e.matmul(
                            b.ap[:, 0:cn], wv[:, k, j * 128:(j + 1) * 128], hTa.ap[:, k, c0:c0 + cn],
                            start=(k == 0), stop=(k == K - 1)),
                            reads=[wb.k, hTa.k], writes=[b.k])
                    bs.append(b)
                epi2(n, c0, cn, bs[0], bs[1])


def make_mixer_out(Kc, wsrc_fn):
    def f(C, l, xT, uT, t0, nt):
        S = C.S
        oTt = Tile(uT.ap.rearrange("p k t -> p (k t)")[:, 0:Kc * TT].rearrange("p (k t) -> p k t", k=Kc), trk=uT.k, sub=uT.sub)
        S.dma("sp", oTt.ap[:, :, 0:nt], C.oT[0:Kc, :, t0:t0 + nt].rearrange("k p t -> p k t"), writes=oTt.all)
        ncols = WSLAB // Kc
        npc = ncols // 128
        wsrc = wsrc_fn(C)
        for s in range(D // ncols):
            wb = C.wbs[C.wrr % len(C.wbs)]
            C.wrr += 1
            wslab_load(C, wb, wsrc[s], Kc * ncols)
            wv = wb.ap[:, 0:Kc * ncols].rearrange("p (k c) -> p k c", k=Kc)
            for j in range(npc):
                dch = s * npc + j
                for (c0, cn) in splits(nt):
                    b = bank(C)
                    for k in range(Kc):
                        S.op("pe", lambda e, b=b, k=k, j=j, c0=c0, cn=cn, wv=wv: e.matmul(
                            b.ap[:, 0:cn], wv[:, k, j * 128:(j + 1) * 128], oTt.ap[:, k, c0:c0 + cn],
                            start=(k == 0), stop=(k == Kc - 1)),
                            reads=[wb.k] + oTt.all, writes=[b.k])
                    resid_update(C, xT, dch, b, c0, cn, t0, 32)
    return f


def fm_to_tm(C, src_ap, nk, ncol, stage, dsts):
    S = C.S
    for g0 in range(0, nk, 4):
        b = bank(C)
        ng = min(4, nk - g0)
        for j in range(ng):
            k = g0 + j
            S.op("pe", lambda e, k=k, j=j, b=b: e.transpose(b.ap[0:ncol, j * 128:(j + 1) * 128], src_ap[:, k, :], C.ident.ap),
                 reads=stage.src_trk + [C.ident.k], writes=[b.k])
        S.op("dve", lambda e, g0=g0, ng=ng, b=b: e.tensor_copy(stage.ap[0:ncol, g0 * 128:(g0 + ng) * 128], b.ap[0:ncol, 0:ng * 128]),
             reads=[b.k], writes=[stage.k])
    for (dap, r0, nr) in dsts:
        S.dma("sp", dap, stage.ap[r0:r0 + nr, 0:nk * 128], reads=[stage.k])


def lru_phase(C, l, A):
    S, I, O = C.S, C.I, C.O
    hTa = hT_all_phase(C, l, A)
    V = C.vec.ap
    one = C.onesf.ap[:, 0:1]
    pst = A.f32(D)
    S.dma("sp", pst.ap[0:12, :], I.lru_conv_in, writes=[pst.k])
    S.dma("sp", pst.ap[12:16, :], I.lru_h0, writes=[pst.k])
    preT = A.f32(KD * 16, ("p (k r) -> p k r", dict(k=KD)))
    b = bank(C)
    for k in range(KD):
        S.op("pe", lambda e, k=k: e.transpose(b.ap[:, k * 16:(k + 1) * 16], pst.ap[0:16, k * 128:(k + 1) * 128], C.ident.ap[0:16, 0:16]),
             reads=[pst.k, C.ident.k], writes=[b.k])
    S.op("dve", lambda e: e.tensor_copy(preT.ap, b.ap[:, 0:KD * 16].rearrange("p (k r) -> p k r", k=KD)), reads=[b.k], writes=[preT.k])
    cf = A.f32(KD)
    S.op("act", lambda e: e.activation(out=cf.ap, in_=V[:, V_LLAM:V_LLAM + KD], func=AF.Exp, scale=-1.0), reads=[C.vec.k], writes=[cf.k])
    S.op("act", lambda e: e.activation(out=cf.ap, in_=cf.ap, func=AF.Ln, bias=one, scale=1.0), reads=[cf.k, C.onesf.k], writes=[cf.k])
    S.op("dve", lambda e: e.tensor_scalar(cf.ap, cf.ap, -8.0, None, ALU.mult), reads=[cf.k], writes=[cf.k])
    wr = A.bf16(4096, ("p (b i j) -> p b i j", dict(b=8, i=2)))
    wi = A.bf16(4096, ("p (b i j) -> p b i j", dict(b=8, i=2)))
    S.dma("pool", wr.ap.rearrange("p b i j -> p (b i) j"), I.lru_wr.rearrange("p (a j) -> p a j", j=256), writes=[wr.k])
    S.dma("pool", wi.ap.rearrange("p b i j -> p (b i) j"), I.lru_wi.rearrange("p (a j) -> p a j", j=256), writes=[wi.k])
    mark_g = A.off
    gx = A.f32(512)
    g2 = A.f32(512)
    gout = A.bf16(T)
    XW = 3 + LP + NS * (3 + LS)
    def epi_gate(n, c0, cn, bk):
        S.op("act", lambda e: e.activation(out=gx.ap[:, 0:cn], in_=bk.ap[:, 0:cn], func=AF.Identity,
                                           bias=V[:, V_LBIN + n:V_LBIN + n + 1], scale=1.0),
             reads=[bk.k, C.vec.k], writes=[gx.k])
        S.op("dve", lambda e: e.tensor_tensor(g2.ap[:, 0:cn], gx.ap[:, 0:cn], gx.ap[:, 0:cn], ALU.mult), reads=[gx.k], writes=[g2.k])
        S.op("dve", lambda e: e.tensor_scalar(g2.ap[:, 0:cn], g2.ap[:, 0:cn], 0.044715, 1.0, ALU.mult, ALU.add), reads=[g2.k], writes=[g2.k])
        S.op("dve", lambda e: e.tensor_tensor(g2.ap[:, 0:cn], g2.ap[:, 0:cn], gx.ap[:, 0:cn], ALU.mult), reads=[g2.k, gx.k], writes=[g2.k])
        S.op("act", lambda e: e.activation(out=g2.ap[:, 0:cn], in_=g2.ap[:, 0:cn], func=AF.Sigmoid, scale=1.5957691216057308),
             reads=[g2.k], writes=[g2.k])
        S.op("dve", lambda e: e.tensor_tensor(gout.ap[:, c0:c0 + cn], g2.ap[:, 0:cn], gx.ap[:, 0:cn], ALU.mult),
             reads=[g2.k, gx.k], writes=[gout.k])

    def after_gate(n):
        S.dma("sp", C.gT[n], gout.ap, reads=[gout.k])

    proj_fm(C, hTa, I.lru_win[0:4], 4, 512, epi_gate, after_chunk=after_gate)

    S.barrier()
    A.off = mark_g
    xpre = A.f32(2 * XW, ("p (c t) -> p c t", dict(c=2)))
    xc = A.f32(2 * T, ("p (c t) -> p c t", dict(c=2)))
    xcb = A.bf16(2 * T, ("p (c t) -> p c t", dict(c=2)))
    rr = A.f32(T)
    ii = A.f32(T)
    gt = A.bf16(2 * T, ("p (c t) -> p c t", dict(c=2)))
    stT = A.f32(KD * 20, ("p (k r) -> p k r", dict(k=KD)))
    S.op("dve", lambda e: e.memset(xpre.ap[:, :, 0:3], 0.0), writes=[xpre.k])

    def samp(ap2, w):
        return ap2.rearrange("p (s t) -> p s t", t=w)

    def epi_xb(n, c0, cn, bk):
        kk = n - 16
        c = kk % 2
        bias = V[:, V_LBIN + n:V_LBIN + n + 1]
        if c0 < LP:
            S.op("act", lambda e: e.activation(out=xpre.ap[:, c, 3 + c0:3 + c0 + cn], in_=bk.ap[:, 0:cn], func=AF.Identity, bias=bias, scale=1.0),
                 reads=[bk.k, C.vec.k], writes=[xpre.k])
        else:
            S.op("act", lambda e: e.activation(out=samp(xpre.ap[:, c, 3 + LP:XW], 3 + LS)[:, :, 3:3 + LS], in_=samp(bk.ap[:, 0:TS], LS),
                                               func=AF.Identity, bias=bias, scale=1.0),
                 reads=[bk.k, C.vec.k], writes=[xpre.k])

    def after_xb(n):
        kk = n - 16
        if kk % 2 == 0:
            return
        blk = kk // 2
        S.dma("sp", gt.ap, C.gT[2 * blk:2 * blk + 2].rearrange("c p t -> p c t"), writes=[gt.k])
        for c in range(2):
            k = 2 * blk + c
            xs_pre = samp(xpre.ap[:, c, 3 + LP:XW], 3 + LS)
            S.op("dve", lambda e, c=c, k=k, xs_pre=xs_pre: e.tensor_copy(xs_pre[:, :, 0:3], preT.ap[:, k, 0:12].rearrange("p (s r) -> p s r", r=3)),
                 reads=[preT.k], writes=[xpre.k])
            w = [V[:, V_LCW + i * KD + k:V_LCW + i * KD + k + 1] for i in range(4)]
            cb = V[:, V_LCB + k:V_LCB + k + 1]
            for (dst, src) in ((xc.ap[:, c, 0:LP], lambda i, c=c: xpre.ap[:, c, i:i + LP]),
                               (samp(xc.ap[:, c, LP:T], LS), lambda i, xs_pre=xs_pre: xs_pre[:, :, i:i + LS])):
                S.op("dve", lambda e, dst=dst, src=src, w=w, cb=cb: e.tensor_scalar(dst, src(0), w[0], cb, ALU.mult, ALU.add),
                     reads=[xpre.k, C.vec.k], writes=[xc.k])
                for i in range(1, 4):
                    S.op("dve", lambda e, dst=dst, src=src, w=w, i=i: e.scalar_tensor_tensor(out=dst, in0=src(i), scalar=w[i], in1=dst, op0=ALU.mult, op1=ALU.add),
                         reads=[xpre.k, C.vec.k, xc.k], writes=[xc.k])
            S.op("act", lambda e, c=c, k=k: e.copy(stT.ap[:, k, 0:3], xpre.ap[:, c, LP:LP + 3]), reads=[xpre.k], writes=[stT.k])
            S.op("act", lambda e, c=c, k=k, xs_pre=xs_pre: e.copy(stT.ap[:, k, 3:15].rearrange("p (s r) -> p s r", r=3), xs_pre[:, :, LS:LS + 3]),
                 reads=[xpre.k], writes=[stT.k])
        S.op("act", lambda e: e.copy(xcb.ap, xc.ap), reads=[xc.k], writes=[xcb.k])
        for c in range(2):
            k = 2 * blk + c
            for (wt, dstt, bvec) in ((wr, rr, V_LBR), (wi, ii, V_LBI)):
                for (c0, cn) in TSPL:
                    bk = bank(C)
                    for ic in range(2):
                        S.op("pe", lambda e, bk=bk, ic=ic, c=c, c0=c0, cn=cn, wt=wt: e.matmul(
                            bk.ap[:, 0:cn], wt.ap[:, blk, ic, c * 128:(c + 1) * 128], xcb.ap[:, ic, c0:c0 + cn], start=(ic == 0), stop=(ic == 1)),
                            reads=[wt.k, xcb.k], writes=[bk.k])
                    S.op("act", lambda e, bk=bk, c0=c0, cn=cn, dstt=dstt, bvec=bvec, k=k: e.activation(
                        out=dstt.ap[:, c0:c0 + cn], in_=bk.ap[:, 0:cn], func=AF.Sigmoid, bias=V[:, bvec + k:bvec + k + 1], scale=1.0),
                        reads=[bk.k, C.vec.k], writes=[dstt.k])
            S.op("act", lambda e, k=k: e.activation(out=rr.ap, in_=rr.ap, func=AF.Exp, scale=cf.ap[:, k:k + 1]),
                 reads=[rr.k, cf.k], writes=[rr.k])
            S.op("dve", lambda e, c=c: e.tensor_tensor(ii.ap, ii.ap, xc.ap[:, c, :], ALU.mult), reads=[ii.k, xc.k], writes=[ii.k])
            S.op("dve", lambda e, c=c: e.tensor_tensor(xc.ap[:, c, :], rr.ap, rr.ap, ALU.mult), reads=[rr.k, xc.k], writes=[xc.k])
            S.op("act", lambda e, c=c: e.activation(out=xc.ap[:, c, :], in_=xc.ap[:, c, :], func=AF.Sqrt, bias=one, scale=-1.0),
                 reads=[xc.k, C.onesf.k], writes=[xc.k])
            S.op("dve", lambda e, c=c: e.tensor_tensor(ii.ap, ii.ap, xc.ap[:, c, :], ALU.mult), reads=[ii.k, xc.k], writes=[ii.k])
            S.op("dve", lambda e, c=c: e.tensor_tensor_scan(xc.ap[:, c, 0:LP], rr.ap[:, 0:LP], ii.ap[:, 0:LP], 0.0, ALU.mult, ALU.add),
                 reads=[rr.k, ii.k, xc.k], writes=[xc.k])
            for q in range(NS):
                cs = slice(LP + q * LS, LP + (q + 1) * LS)
                S.op("dve", lambda e, c=c, cs=cs, q=q, k=k: e.tensor_tensor_scan(xc.ap[:, c, cs], rr.ap[:, cs], ii.ap[:, cs],
                                                                              preT.ap[:, k, 12 + q:13 + q], ALU.mult, ALU.add),
                     reads=[rr.k, ii.k, xc.k, preT.k], writes=[xc.k])
            S.op("act", lambda e, c=c, k=k: e.copy(stT.ap[:, k, 15:16], xc.ap[:, c, LP - 1:LP]), reads=[xc.k], writes=[stT.k])
            S.op("act", lambda e, c=c, k=k: e.copy(stT.ap[:, k, 16:20], samp(xc.ap[:, c, LP:T], LS)[:, :, LS - 1]), reads=[xc.k], writes=[stT.k])
            S.op("dve", lambda e, c=c: e.tensor_tensor(gt.ap[:, c, :], xc.ap[:, c, :], gt.ap[:, c, :], ALU.mult), reads=[xc.k, gt.k], writes=[gt.k])
        S.dma("sp", C.oT[2 * blk:2 * blk + 2].rearrange("c p t -> p c t"), gt.ap, reads=[gt.k])

    proj_fm(C, hTa, I.lru_win[4:8], 4, 512, lambda n, c0, cn, bk: epi_xb(n + 16, c0, cn, bk), after_chunk=lambda n: after_xb(n + 16))
    stage = pst
    stage.src_trk = [stT.k]
    fm_to_tm(C, stT.ap, KD, 20, stage, [(O.lru_conv_p, 0, 3), (O.lru_conv_s, 3, 12), (O.lru_p, 15, 1), (O.lru_s, 16, 4)])
    S.barrier()


def attn_unit(C, B_, R, nB, hd, q_aps, q_trk, sgroups, pvblocks, mask_ap, mask_trk, scale, sink_ap, out_cb):
    S = C.S
    NK = max(off + max(n, mn) for (off, n, _, _, mn) in sgroups)
    NKP = 256 if nB > 1 else ((NK + 511) // 512) * 512
    nb = (nB * NKP + 511) // 512
    sc = mbank(C, nb)
    scv = sc.ap[0:R, 0:nB * NKP].rearrange("p (b n) -> p b n", b=nB)
    for b in range(nB):
        for (off, n, kf, ktrk, mn) in sgroups:
            bt = sc.banks[(b * NKP + off) // 512].k
            S.op("pe", lambda e, b=b, off=off, n=n, kf=kf: e.matmul(scv[:, b, off:off + n], q_aps[b], kf(b), start=True, stop=False),
                 reads=q_trk + ktrk, writes=[bt])
            S.op("pe", lambda e, b=b, off=off, mn=mn: e.matmul(scv[:, b, off:off + mn], C.identb.ap[:, 0:R], mask_ap[:, off:off + mn], start=False, stop=True),
                 reads=[C.identb.k] + mask_trk, writes=[bt])
    mx, negm, rs, es, lse = B_.mx, B_.negm, B_.rs, B_.es, B_.lse
    S.op("dve", lambda e: e.tensor_reduce(mx.ap[0:R, 0:nB], scv[:, :, 0:NK], AX.X, ALU.max), reads=sc.all, writes=[mx.k])
    S.op("dve", lambda e: e.tensor_scalar(mx.ap[0:R, 0:nB], mx.ap[0:R, 0:nB], scale, None, ALU.mult), reads=[mx.k], writes=[mx.k])
    if sink_ap is not None:
        S.op("dve", lambda e: e.tensor_tensor(mx.ap[0:R, 0:nB], mx.ap[0:R, 0:nB], sink_ap, ALU.max), reads=[mx.k, B_.sink_trk], writes=[mx.k])
    S.op("dve", lambda e: e.tensor_scalar(negm.ap[0:R, 0:nB], mx.ap[0:R, 0:nB], -1.0, None, ALU.mult), reads=[mx.k], writes=[negm.k])
    S.op("dve", lambda e: e.memset(rs.ap[0:R, 0:nB], 0.0), writes=[rs.k])
    pb = B_.pb
    pbv = pb.ap[0:R, 0:nB * NK].rearrange("p (b n) -> p b n", b=nB)
    for b in range(nB):
        S.op("act", lambda e, b=b: e.activation(out=pbv[:, b, :], in_=scv[:, b, 0:NK], func=AF.Exp, bias=negm.ap[0:R, b:b + 1], scale=scale,
                                                accum_out=rs.ap[0:R, b:b + 1]),
             reads=sc.all + [negm.k, rs.k], writes=[pb.k, rs.k])
    if sink_ap is not None:
        S.op("dve", lambda e: e.tensor_tensor(es.ap[0:R, 0:nB], sink_ap, mx.ap[0:R, 0:nB], ALU.subtract), reads=[mx.k, B_.sink_trk], writes=[es.k])
        S.op("act", lambda e: e.activation(out=es.ap[0:R, 0:nB], in_=es.ap[0:R, 0:nB], func=AF.Exp), reads=[es.k], writes=[es.k])
        S.op("dve", lambda e: e.tensor_tensor(rs.ap[0:R, 0:nB], rs.ap[0:R, 0:nB], es.ap[0:R, 0:nB], ALU.add), reads=[rs.k, es.k], writes=[rs.k])
    S.op("act", lambda e: e.activation(out=lse.ap[0:R, 0:nB], in_=rs.ap[0:R, 0:nB], func=AF.Ln), reads=[rs.k], writes=[lse.k])
    S.op("dve", lambda e: e.tensor_tensor(lse.ap[0:R, 0:nB], lse.ap[0:R, 0:nB], mx.ap[0:R, 0:nB], ALU.add), reads=[lse.k, mx.k], writes=[lse.k])
    S.op("dve", lambda e: e.reciprocal(rs.ap[0:R, 0:nB], rs.ap[0:R, 0:nB]), reads=[rs.k], writes=[rs.k])
    S.op("dve", lambda e: e.tensor_tensor(pbv, pbv, rs.ap[0:R, 0:nB].unsqueeze(2).broadcast_to([R, nB, NK]), ALU.mult), reads=[pb.k, rs.k], writes=[pb.k])
    pT = B_.pT
    per_bank = 1024 // R
    nidx = nB * len(pvblocks)
    idx = 0
    tb = None
    filled = []
    for b in range(nB):
        for (off, n, vf, vtrk) in pvblocks:
            if idx % per_bank == 0:
                tb = bank(C)
                filled.append((tb, idx))
            j = idx % per_bank
            tbv = tb.ap.bitcast(BF16)
            S.op("pe", lambda e, b=b, off=off, n=n, j=j, tbv=tbv: e.transpose(tbv[0:n, j * R:(j + 1) * R], pbv[:, b, off:off + n], C.identb.ap[0:R, 0:R]),
                 reads=[pb.k, C.identb.k], writes=[tb.k])
            idx += 1
    for ci, (tb, i0) in enumerate(filled):
        cnt = min(per_bank, nidx - i0)
        tbv = tb.ap.bitcast(BF16)
        if ci % 2 == 0:
            S.op("act", lambda e, tbv=tbv, i0=i0, cnt=cnt: e.copy(pT.ap[:, i0 * R:(i0 + cnt) * R], tbv[:, 0:cnt * R]), reads=[tb.k], writes=[pT.k])
        else:
            S.op("dve", lambda e, tbv=tbv, i0=i0, cnt=cnt: e.tensor_copy(pT.ap[:, i0 * R:(i0 + cnt) * R], tbv[:, 0:cnt * R]), reads=[tb.k], writes=[pT.k])
    ob = bank(C)
    idx = 0
    for b in range(nB):
        for bi, (off, n, vf, vtrk) in enumerate(pvblocks):
            S.op("pe", lambda e, b=b, n=n, vf=vf, idx=idx, bi=bi: e.matmul(ob.ap[0:R, b * hd:(b + 1) * hd], pT.ap[0:n, idx * R:(idx + 1) * R], vf(b),
                                                                         start=(bi == 0), stop=(bi == len(pvblocks) - 1)),
                 reads=[pT.k] + vtrk, writes=[ob.k])
            idx += 1
    Osb = B_.Osb
    S.op("act", lambda e: e.copy(Osb.ap[0:R, 0:nB * hd], ob.ap[0:R, 0:nB * hd]), reads=[ob.k], writes=[Osb.k])
    out_cb(Osb, lse)


def attn_phase(C, l, A, cfg):
    S, I, O = C.S, C.I, C.O
    hd, G, HQ = cfg["hd"], cfg["G"], cfg["HQ"]
    NKV = 4
    HPC = 128 // hd
    QC = HQ // HPC
    scale = hd ** -0.5
    hTa = hT_all_phase(C, l, A)
    if C.dbg.get("rope_f32", cfg["name"] == "swa"):
        cosT = A.f32(T)
        sinT = A.f32(T)
        S.dma("sp", cosT.ap, cfg["rope"][0], writes=[cosT.k])
        S.dma("sp", sinT.ap, cfg["rope"][1], writes=[sinT.k])
    else:
        cosT = A.bf16(T)
        sinT = A.bf16(T)
        for (c0, cn) in splits(T, 1040):
            S.dma("pool", cosT.ap[:, c0:c0 + cn], cfg["rope"][0][:, c0:c0 + cn], writes=[cosT.k])
            S.dma("pool", sinT.ap[:, c0:c0 + cn], cfg["rope"][1][:, c0:c0 + cn], writes=[sinT.k])
    perm = A.bf16(128)
    S.dma("pool", perm.ap, cfg["perm"], writes=[perm.k])
    pmask = A.bf16(256)
    S.dma("pool", pmask.ap, I.pmask, writes=[pmask.k])
    smask = A.bf16(max(cfg["wins"]) + 128)
    S.op("dve", lambda e: e.memset(smask.ap, 0.0), writes=[smask.k])
    QT = A.bf16(QC * T, ("p (k t) -> p k t", dict(k=QC)))
    NKC = NKV * HPC
    KT = A.bf16(NKC * T, ("p (k t) -> p k t", dict(k=NKC)))
    VW = NKV * hd
    Vg = A.bf16(16 * VW, ("p (u c) -> p u c", dict(u=16)))
    Vs = A.bf16(NS * VW, ("p (s c) -> p s c", dict(s=NS)))
    S.op("dve", lambda e: e.memset(Vs.ap, 0.0), writes=[Vs.k])
    qb = A.bf16(512)
    t1 = A.f32(512)
    t2 = A.f32(512)
    B_ = Ctx()
    B_.pb = A.bf16(max(2048, max(cfg["wins"]) + 128))
    B_.pT = A.bf16(2048)
    B_.Osb = A.f32(512)
    B_.mx, B_.negm, B_.rs, B_.es, B_.lse = [A.f32(8) for _ in range(5)]
    wmax = max(cfg["wins"])
    kcb = A.bf16(HPC * wmax, ("p (a b) -> p a b", dict(b=128)))
    S.op("dve", lambda e: e.memset(kcb.ap, 0.0), writes=[kcb.k])
    Vc = A.bf16((wmax // 128) * hd, ("p (a b) -> p a b", dict(b=hd)))
    KcT = A.bf16(HPC * wmax)
    qs = A.bf16(32)
    sinkp = A.f32(32)
    sinks = A.f32(8)
    B_.sink_trk = sinkp.k
    if cfg["name"] == "swa":
        srow = A.f32(32)
        orow = A.f32(128)
        S.op("dve", lambda e: e.memset(orow.ap[0:1, :], 1.0), writes=[orow.k])
        S.dma("sp", srow.ap[0:1, :], I.swa_sinks, writes=[srow.k])
        sb_ = bank(C)
        S.op("pe", lambda e: e.matmul(sb_.ap[:, 0:32], orow.ap[0:1, :], srow.ap[0:1, :], start=True, stop=True), reads=[orow.k, srow.k], writes=[sb_.k])
        S.op("dve", lambda e: e.tensor_copy(sinkp.ap, sb_.ap[:, 0:32]), reads=[sb_.k], writes=[sinkp.k])
        S.dma("sp", sinks.ap[0:32, :], I.swa_sinkS, writes=[sinkp.k])
    stop = C.dbg.get("attn_stop", 99)
    if stop <= 1:
        S.barrier()
        return

    def rope_epi(dst, dst_trk, kout):
        def epi(n, c0, cn, bk, b2):
            S.op("act", lambda e: e.activation(out=t1.ap[:, 0:cn], in_=bk.ap[:, 0:cn], func=AF.Identity, scale=1.0), reads=[bk.k], writes=[t1.k])
            S.op("act", lambda e: e.activation(out=t2.ap[:, 0:cn], in_=b2.ap[:, 0:cn], func=AF.Identity, scale=1.0), reads=[b2.k], writes=[t2.k])
            S.op("dve", lambda e: e.tensor_tensor(t1.ap[:, 0:cn], t1.ap[:, 0:cn], cosT.ap[:, c0:c0 + cn], ALU.mult), reads=[t1.k, cosT.k], writes=[t1.k])
            S.op("dve", lambda e: e.tensor_tensor(t2.ap[:, 0:cn], t2.ap[:, 0:cn], sinT.ap[:, c0:c0 + cn], ALU.mult), reads=[t2.k, sinT.k], writes=[t2.k])
            if kout is None:
                S.op("dve", lambda e: e.tensor_tensor(dst[:, n, c0:c0 + cn], t1.ap[:, 0:cn], t2.ap[:, 0:cn], ALU.add), reads=[t1.k, t2.k], writes=[dst_trk])
            else:
                S.op("dve", lambda e: e.tensor_tensor(t1.ap[:, 0:cn], t1.ap[:, 0:cn], t2.ap[:, 0:cn], ALU.add), reads=[t1.k, t2.k], writes=[t1.k])
                S.op("dve", lambda e: e.tensor_copy(dst[:, n, c0:c0 + cn], t1.ap[:, 0:cn]), reads=[t1.k], writes=[dst_trk])
                kout(n, c0, cn)
        return epi

    Ost = A.f32(512)
    for g in range(G):
        d, w = cfg["dils"][g], cfg["wins"][g]
        nun = 16
        kvp, kvs = cfg["kv_out"][g]
        if cfg["name"] == "dil" or g == 0:
            for (c0, cn) in splits(w + 128, 1088):
                S.dma("pool", smask.ap[0:32, c0:c0 + cn], I.smask[g, :, c0:c0 + cn], writes=[smask.k])

        def kout(n, c0, cn, g=g, d=d, w=w, kvp=kvp, kvs=kvs):
            if n % HPC != 0 or C.dbg.get("no_kout"):
                return
            kv_i = n // HPC
            for (o, nn) in splits(cn, 128):
                tok = c0 + o
                if tok < LP and tok < LP - w:
                    continue
                tb = bank(C)
                S.op("pe", lambda e, o=o, nn=nn, tb=tb: e.transpose(tb.ap[0:nn, 0:128], t1.ap[:, o:o + nn], C.ident.ap),
                     reads=[t1.k, C.ident.k], writes=[tb.k])
                S.op("act", lambda e, nn=nn, tb=tb: e.copy(Ost.ap[0:nn, 0:hd], tb.ap[0:nn, 0:hd]), reads=[tb.k], writes=[Ost.k])
                if tok < LP:
                    S.dma("sp", kvp[tok - (LP - w):tok - (LP - w) + nn, kv_i * hd:(kv_i + 1) * hd], Ost.ap[0:nn, 0:hd], reads=[Ost.k])
                else:
                    S.dma("sp", kvs[:, kv_i * hd:(kv_i + 1) * hd], Ost.ap[0:nn, 0:hd], reads=[Ost.k])

        nks = NKC // 4
        proj_fm2(C, hTa, cfg["wk"][g * nks:(g + 1) * nks], cfg["wkp"][g * nks:(g + 1) * nks], nks, 512, rope_epi(KT.ap, KT.k, kout))
        if stop <= 2:
            S.barrier()
            return
        wb = C.wbs[C.wrr % len(C.wbs)]
        C.wrr += 1
        wslab_load(C, wb, cfg["wv"][g], KD * VW)
        wv = wb.ap[:, 0:KD * VW].rearrange("p (k c) -> p k c", k=KD)
        for u in range(nun):
            r, blk = u % d, u // d
            tk0 = r + d * 128 * blk
            tb = bank(C)
            for k in range(KD):
                lhs_ = hTa.ap[:, k, tk0:tk0 + d * 127 + 1:d]
                rhs_ = wv[:, k, :]
                S.op("pe", lambda e, k=k, tb=tb, lhs_=lhs_, rhs_=rhs_: e.matmul(tb.ap[:, 0:VW], lhs_, rhs_, start=(k == 0), stop=(k == KD - 1)),
                     reads=[hTa.k, wb.k], writes=[tb.k])
            S.op("act", lambda e, u=u, tb=tb: e.copy(Vg.ap[:, u, :], tb.ap[:, 0:VW]), reads=[tb.k], writes=[Vg.k])
            if d * 128 * blk >= LP - w:
                S.op("act", lambda e, tb=tb: e.copy(Ost.ap[:, 0:VW], tb.ap[:, 0:VW]), reads=[tb.k], writes=[Ost.k])
                r0 = tk0 - (LP - w)
                S.dma("sp", kvp[r0:r0 + d * 127 + 1:d, VW:2 * VW], Ost.ap[:, 0:VW], reads=[Ost.k])
        for q in range(NS):
            tb = bank(C)
            for k in range(KD):
                rhs_ = wv[:, k, :]
                S.op("pe", lambda e, k=k, tb=tb, q=q, rhs_=rhs_: e.matmul(tb.ap[0:LS, 0:VW], hTa.ap[:, k, LP + q * LS:LP + (q + 1) * LS], rhs_, start=(k == 0), stop=(k == KD - 1)),
                     reads=[hTa.k, wb.k], writes=[tb.k])
            S.op("act", lambda e, q=q, tb=tb: e.copy(Vs.ap[0:LS, q, :], tb.ap[0:LS, 0:VW]), reads=[tb.k], writes=[Vs.k])
            S.op("act", lambda e, tb=tb: e.copy(Ost.ap[0:LS, 0:VW], tb.ap[0:LS, 0:VW]), reads=[tb.k], writes=[Ost.k])
            S.dma("sp", kvs[q * LS:(q + 1) * LS, VW:2 * VW], Ost.ap[0:LS, 0:VW], reads=[Ost.k])
        if stop <= 3:
            S.barrier()
            return
        for kvh in range(NKV):
            proj_fm2(C, hTa, cfg["wq"][g * NKV + kvh:g * NKV + kvh + 1], cfg["wqp"][g * NKV + kvh:g * NKV + kvh + 1], 1, 512, rope_epi(QT.ap, QT.k, None))
            if stop <= 4:
                continue
            for u in range(nun if not C.dbg.get("skip_punits") else 0):
                r, blk = u % d, u // d
                tq0 = r + d * 128 * blk
                qsl = slice(tq0, tq0 + d * 127 + 1, d)
                if blk == 0:
                    ksl, nk, moff = qsl, 128, 128
                else:
                    tk0 = tq0 - d * 128
                    ksl, nk, moff = slice(tk0, tk0 + d * 255 + 1, d), 256, 0
                q_aps, kfs = [], None
                for b in range(HQ):
                    q_aps.append(QT.ap[:, b // HPC, qsl])
                kf = lambda b, ksl=ksl, kvh=kvh: KT.ap[:, kvh * HPC + (b % HPC), ksl]
                sg = [(0, nk, kf, [KT.k], nk)]
                pv = []
                if blk > 0:
                    pv.append((0, 128, lambda b, u=u, d=d, kvh=kvh: Vg.ap[:, u - d, kvh * hd:(kvh + 1) * hd], [Vg.k]))
                pv.append((nk - 128, 128, lambda b, u=u, kvh=kvh: Vg.ap[:, u, kvh * hd:(kvh + 1) * hd], [Vg.k]))
                sink_ap = sinkp.ap[:, kvh * HQ:(kvh + 1) * HQ] if cfg["name"] == "swa" else None

                def ocb(Osb, lse, g=g, kvh=kvh, tq0=tq0, d=d):
                    S.dma("sp", C.Oh[g, tq0:tq0 + d * 127 + 1:d, kvh * HQ * hd:(kvh + 1) * HQ * hd], Osb.ap[:, 0:HQ * hd], reads=[Osb.k])
                    if G > 1:
                        S.dma("sp", C.lseh[g, tq0:tq0 + d * 127 + 1:d, kvh * HQ:(kvh + 1) * HQ], lse.ap[:, 0:HQ], reads=[lse.k])
                attn_unit(C, B_, 128, HQ, hd, q_aps, [QT.k], sg, pv, pmask.ap[:, moff:moff + nk],
                          [pmask.k], scale, sink_ap, ocb)
            for q in range(NS if not C.dbg.get("skip_sunits") else 0):
                nblk = w // 128
                csrc = cfg["cache"][g]
                for par in range(HPC):
                    S.dma("pool", kcb.ap[:, par * nblk:(par + 1) * nblk, par * hd:(par + 1) * hd],
                          csrc[q, :, 0, kvh, :].rearrange("(a p) c -> p a c", p=128), writes=[kcb.k])
                S.dma("pool", Vc.ap[:, 0:nblk, :], csrc[q, :, 1, kvh, :].rearrange("(a p) c -> p a c", p=128), writes=[Vc.k])
                KcTv = KcT.ap[:, 0:HPC * w].rearrange("p (r c) -> p r c", r=HPC)
                for par in range(HPC):
                    for a0 in range(0, nblk, 8):
                        na = min(8, nblk - a0)
                        tb = bank(C)
                        tbv = tb.ap.bitcast(BF16)
                        for a in range(na):
                            S.op("pe", lambda e, a=a, a0=a0, tbv=tbv, par=par, nblk=nblk: e.transpose(tbv[:, a * 128:(a + 1) * 128], kcb.ap[:, par * nblk + a0 + a, :], C.identb.ap) if True else None,
                                 reads=[kcb.k, C.identb.k], writes=[tb.k])
                        S.op("act", lambda e, a0=a0, na=na, tbv=tbv, par=par, KcTv=KcTv: e.copy(KcTv[:, par, a0 * 128:(a0 + na) * 128], tbv[:, 0:na * 128]), reads=[tb.k], writes=[KcT.k])
                for par in range(HPC):
                    tsl = slice(LP + q * LS, LP + (q + 1) * LS)
                    S.op("dve", lambda e, tsl=tsl: e.tensor_copy(qs.ap[:, 0:QC * LS].rearrange("p (c t) -> p c t", t=LS), QT.ap[:, :, tsl]),
                         reads=[QT.k], writes=[qs.k])
                    q_aps = [qs.ap[:, 0:QC * LS]]
                    sg = []
                    for (o, nn) in splits(w, 512):
                        sg.append((o, nn, lambda b, o=o, nn=nn, par=par, KcTv=KcTv: KcTv[:, par, o:o + nn], [KcT.k], nn))
                    sg.append((w, LS, lambda b, par=par, tsl=tsl, kvh=kvh: KT.ap[:, kvh * HPC + par, tsl], [KT.k], 128))
                    pv = [(a * 128, 128, lambda b, a=a: Vc.ap[:, a, :], [Vc.k]) for a in range(nblk)]
                    pv.append((w, 128, lambda b, q=q, kvh=kvh: Vs.ap[:, q, kvh * hd:(kvh + 1) * hd], [Vs.k]))
                    sink_ap = sinks.ap[0:32, kvh * HPC + par:kvh * HPC + par + 1] if cfg["name"] == "swa" else None

                    def ocb(Osb, lse, g=g, kvh=kvh, q=q, par=par):
                        for hh in range(QC):
                            head = kvh * HQ + hh * HPC + par
                            S.dma("sp", C.Oh[g, LP + q * LS:LP + (q + 1) * LS, head * hd:(head + 1) * hd], Osb.ap[hh * LS:(hh + 1) * LS, 0:hd], reads=[Osb.k])
                            if G > 1:
                                S.dma("sp", C.lseh[g, LP + q * LS:LP + (q + 1) * LS, head:head + 1], lse.ap[hh * LS:(hh + 1) * LS, 0:1], reads=[lse.k], allow_slow_non_contiguous=True)
                    attn_unit(C, B_, QC * LS, 1, hd, q_aps, [qs.k], sg, pv, smask.ap[:, 0:w + 128], [smask.k], scale, sink_ap, ocb)
    S.barrier()
    if stop <= 6:
        return
    A.off = 0
    NH = NKV * HQ
    Og = [A.f32(D) for _ in range(G)]
    acc = A.f32(D)
    lt = A.f32(G * 16, ("p (g h) -> p g h", dict(g=G)))
    mxl = A.f32(16)
    sml = A.f32(16)
    obf = A.bf16(D)
    oTs = A.bf16(KD * 128, ("p (k t) -> p k t", dict(k=KD)))
    for (t0, n) in splits(T, 128):
        for g in range(G):
            S.dma("sp", Og[g].ap[0:n, :], C.Oh[g, t0:t0 + n, :], writes=[Og[g].k])
        if G > 1:
            S.dma("sp", lt.ap[0:n], C.lseh[0:G, t0:t0 + n, 0:16].rearrange("g t h -> t g h"), writes=[lt.k])
            S.op("dve", lambda e, n=n: e.tensor_tensor(mxl.ap[0:n], lt.ap[0:n, 0, :], lt.ap[0:n, 1, :], ALU.max), reads=[lt.k], writes=[mxl.k])
            S.op("dve", lambda e, n=n: e.tensor_tensor(mxl.ap[0:n], mxl.ap[0:n], lt.ap[0:n, 2, :], ALU.max), reads=[lt.k, mxl.k], writes=[mxl.k])
            S.op("dve", lambda e, n=n: e.tensor_tensor(lt.ap[0:n], lt.ap[0:n], mxl.ap[0:n].unsqueeze(1).broadcast_to([n, G, 16]), ALU.subtract), reads=[lt.k, mxl.k], writes=[lt.k])
            S.op("act", lambda e, n=n: e.activation(out=lt.ap[0:n], in_=lt.ap[0:n], func=AF.Exp), reads=[lt.k], writes=[lt.k])
            S.op("dve", lambda e, n=n: e.tensor_tensor(sml.ap[0:n], lt.ap[0:n, 0, :], lt.ap[0:n, 1, :], ALU.add), reads=[lt.k], writes=[sml.k])
            S.op("dve", lambda e, n=n: e.tensor_tensor(sml.ap[0:n], sml.ap[0:n], lt.ap[0:n, 2, :], ALU.add), reads=[lt.k, sml.k], writes=[sml.k])
            S.op("dve", lambda e, n=n: e.reciprocal(sml.ap[0:n], sml.ap[0:n]), reads=[sml.k], writes=[sml.k])
            S.op("dve", lambda e, n=n: e.tensor_tensor(lt.ap[0:n], lt.ap[0:n], sml.ap[0:n].unsqueeze(1).broadcast_to([n, G, 16]), ALU.mult), reads=[lt.k, sml.k], writes=[lt.k])
            for g in range(G):
                ov = Og[g].ap[0:n, :].rearrange("p (h d) -> p h d", h=16)
                wv_ = lt.ap[0:n, g, :].unsqueeze(2).broadcast_to([n, 16, 128])
                S.op("dve", lambda e, ov=ov, wv_=wv_: e.tensor_tensor(ov, ov, wv_, ALU.mult), reads=[Og[g].k, lt.k], writes=[Og[g].k])
            S.op("dve", lambda e, n=n: e.tensor_tensor(acc.ap[0:n], Og[0].ap[0:n], Og[1].ap[0:n], ALU.add), reads=[Og[0].k, Og[1].k], writes=[acc.k])
            S.op("dve", lambda e, n=n: e.tensor_tensor(obf.ap[0:n], acc.ap[0:n], Og[2].ap[0:n], ALU.add), reads=[acc.k, Og[2].k], writes=[obf.k])
        else:
            S.op("act", lambda e, n=n: e.copy(obf.ap[0:n], Og[0].ap[0:n]), reads=[Og[0].k], writes=[obf.k])
        for h2 in range(2):
            tb = bank(C)
            tbv = tb.ap.bitcast(BF16)
            for j in range(8):
                k = h2 * 8 + j
                S.op("pe", lambda e, k=k, j=j, n=n, tbv=tbv: e.transpose(tbv[:, j * 128:j * 128 + n], obf.ap[0:n, k * 128:(k + 1) * 128], C.identb.ap[0:n, 0:n]),
                     reads=[obf.k, C.identb.k], writes=[tb.k])
            src = tbv.rearrange("p (j t) -> p j t", j=8)[:, :, 0:n]
            dst = oTs.ap[:, h2 * 8:(h2 + 1) * 8, 0:n]
            if h2 == 0:
                S.op("act", lambda e, src=src, dst=dst: e.copy(dst, src), reads=[tb.k], writes=[oTs.k])
            else:
                S.op("dve", lambda e, src=src, dst=dst: e.tensor_copy(dst, src), reads=[tb.k], writes=[oTs.k])
        S.dma("sp", C.oT[0:KD, :, t0:t0 + n].rearrange("k p t -> p k t"), oTs.ap[:, :, 0:n], reads=[oTs.k])
    S.barrier()


def ssd_phase(C, l, A):
    S, I, O = C.S, C.I, C.O
    V = C.vec.ap
    one = C.onesf.ap[:, 0:1]
    XW = 3 + LP + NS * (3 + LS)

    def samp(ap2, w):
        return ap2.rearrange("p (s t) -> p s t", t=w)

    dtT = A.f32(T)
    aT = A.f32(T)
    stT = A.f32(48 * 15, ("p (k r) -> p k r", dict(k=48)))
    preT = A.f32(48 * 12, ("p (k r) -> p k r", dict(k=48)))
    Ah = A.f32(1)
    keep = A.off
    hTa = hT_all_phase(C, l, A)
    pst = A.f32(D)
    for pc in range(3):
        S.dma("sp", pst.ap[0:12, :], I.ssd_conv_in[:, pc * D:(pc + 1) * D], writes=[pst.k])
        b = bank(C)
        for k in range(KD):
            S.op("pe", lambda e, k=k, b=b: e.transpose(b.ap[:, k * 12:(k + 1) * 12], pst.ap[0:12, k * 128:(k + 1) * 128], C.ident.ap[0:12, 0:12]),
                 reads=[pst.k, C.ident.k], writes=[b.k])
        S.op("dve", lambda e, pc=pc, b=b: e.tensor_copy(preT.ap[:, pc * KD:(pc + 1) * KD, :], b.ap[:, 0:KD * 12].rearrange("p (k r) -> p k r", k=KD)),
             reads=[b.k], writes=[preT.k])
    tx = A.f32(512)
    ty = A.f32(512)

    def epi_dt(n, c0, cn, bk):
        S.op("act", lambda e: e.activation(out=tx.ap[:, 0:cn], in_=bk.ap[:, 0:cn], func=AF.Identity, bias=V[:, V_SDTB:V_SDTB + 1], scale=1.0),
             reads=[bk.k, C.vec.k], writes=[tx.k])
        S.op("act", lambda e: e.activation(out=ty.ap[:, 0:cn], in_=tx.ap[:, 0:cn], func=AF.Abs), reads=[tx.k], writes=[ty.k])
        S.op("act", lambda e: e.activation(out=ty.ap[:, 0:cn], in_=ty.ap[:, 0:cn], func=AF.Exp, scale=-1.0), reads=[ty.k], writes=[ty.k])
        S.op("act", lambda e: e.activation(out=ty.ap[:, 0:cn], in_=ty.ap[:, 0:cn], func=AF.Ln, bias=one, scale=1.0), reads=[ty.k, C.onesf.k], writes=[ty.k])
        S.op("dve", lambda e: e.tensor_scalar(tx.ap[:, 0:cn], tx.ap[:, 0:cn], 0.0, None, ALU.max), reads=[tx.k], writes=[tx.k])
        S.op("dve", lambda e: e.tensor_tensor(dtT.ap[:, c0:c0 + cn], tx.ap[:, 0:cn], ty.ap[:, 0:cn], ALU.add), reads=[tx.k, ty.k], writes=[dtT.k])

    proj_fm(C, hTa, I.ssd_wdt, 1, 128, epi_dt)
    S.op("act", lambda e: e.activation(out=Ah.ap, in_=V[:, V_SALOG:V_SALOG + 1], func=AF.Exp), reads=[C.vec.k], writes=[Ah.k])
    S.op("dve", lambda e: e.tensor_scalar(Ah.ap, Ah.ap, -1.0, None, ALU.mult), reads=[Ah.k], writes=[Ah.k])
    S.op("act", lambda e: e.activation(out=aT.ap, in_=dtT.ap, func=AF.Exp, scale=Ah.ap), reads=[dtT.k, Ah.k], writes=[aT.k])
    zo = A.bf16(T)

    def epi_z(n, c0, cn, bk):
        S.op("act", lambda e: e.activation(out=zo.ap[:, c0:c0 + cn], in_=bk.ap[:, 0:cn], func=AF.Silu), reads=[bk.k], writes=[zo.k])

    proj_fm(C, hTa, I.ssd_wz, 8, 512, epi_z, after_chunk=lambda n: S.dma("sp", C.zT[n], zo.ap, reads=[zo.k]))
    xpre = A.f32(XW)
    xc = A.f32(T)
    xcb = A.bf16(T)
    S.op("dve", lambda e: e.memset(xpre.ap[:, 0:3], 0.0), writes=[xpre.k])

    def epi_x(n, c0, cn, bk):
        if c0 < LP:
            S.op("act", lambda e: e.copy(xpre.ap[:, 3 + c0:3 + c0 + cn], bk.ap[:, 0:cn]), reads=[bk.k], writes=[xpre.k])
        else:
            S.op("act", lambda e: e.copy(samp(xpre.ap[:, 3 + LP:XW], 3 + LS)[:, :, 3:3 + LS], samp(bk.ap[:, 0:TS], LS)), reads=[bk.k], writes=[xpre.k])

    def after_x(n):
        xs_pre = samp(xpre.ap[:, 3 + LP:XW], 3 + LS)
        S.op("dve", lambda e: e.tensor_copy(xs_pre[:, :, 0:3], preT.ap[:, n, :].rearrange("p (s r) -> p s r", r=3)), reads=[preT.k], writes=[xpre.k])
        w = [V[:, V_SCW + i * 48 + n:V_SCW + i * 48 + n + 1] for i in range(4)]
        cb = V[:, V_SCB + n:V_SCB + n + 1]
        for (dst, src) in ((xc.ap[:, 0:LP], lambda i: xpre.ap[:, i:i + LP]), (samp(xc.ap[:, LP:T], LS), lambda i: xs_pre[:, :, i:i + LS])):
            S.op("dve", lambda e, dst=dst, src=src: e.tensor_scalar(dst, src(0), w[0], cb, ALU.mult, ALU.add), reads=[xpre.k, C.vec.k], writes=[xc.k])
            for i in range(1, 4):
                S.op("dve", lambda e, dst=dst, src=src, i=i: e.scalar_tensor_tensor(out=dst, in0=src(i), scalar=w[i], in1=dst, op0=ALU.mult, op1=ALU.add),
                     reads=[xpre.k, C.vec.k, xc.k], writes=[xc.k])
        S.op("act", lambda e: e.copy(stT.ap[:, n, 0:3], xpre.ap[:, LP:LP + 3]), reads=[xpre.k], writes=[stT.k])
        S.op("act", lambda e: e.copy(stT.ap[:, n, 3:15].rearrange("p (s r) -> p s r", r=3), xs_pre[:, :, LS:LS + 3]), reads=[xpre.k], writes=[stT.k])
        if n < 32:
            S.op("act", lambda e: e.activation(out=xc.ap, in_=xc.ap, func=AF.Silu), reads=[xc.k], writes=[xc.k])
            S.dma("sp", C.xsT[n], xc.ap, reads=[xc.k])
        else:
            S.op("act", lambda e: e.activation(out=xcb.ap, in_=xc.ap, func=AF.Silu), reads=[xc.k], writes=[xcb.k])
            S.dma("sp", C.bcT[n - 32], xcb.ap, reads=[xcb.k])

    proj_fm(C, hTa, I.ssd_wx, 12, 512, epi_x, after_chunk=after_x)
    stage = pst
    for pc in range(3):
        stage.src_trk = [stT.k]
        fm_to_tm(C, stT.ap[:, pc * KD:(pc + 1) * KD, :], KD, 15, stage,
                 [(O.ssd_conv_p[:, pc * D:(pc + 1) * D], 0, 3), (O.ssd_conv_s[:, pc * D:(pc + 1) * D], 3, 12)])
    S.barrier()
    A.off = keep
    NCP = 2
    xs = A.f32(NCP * T, ("p (c t) -> p c t", dict(c=NCP)))
    abc = A.f32(NCP * T, ("p (c t) -> p c t", dict(c=NCP)))
    yacc = A.f32(NCP * T, ("p (c t) -> p c t", dict(c=NCP)))
    H0 = A.f32(NCP * NS * 128, ("p (c q s) -> p c q s", dict(c=NCP, q=NS)))
    stF = A.f32(NCP * NCOND * 128, ("p (c q s) -> p c q s", dict(c=NCP, q=NCOND)))
    BTg = A.bf16(T)
    CTg = A.bf16(T)
    Bsb = A.f32(T)
    Csb = A.f32(T)
    d1s = [A.f32(T) for _ in range(2)]
    Hss = [A.f32(T) for _ in range(2)]
    tms = [A.f32(T) for _ in range(2)]
    selt = A.f32(128)
    selb = A.bf16(128)
    zt = A.bf16(NCP * T, ("p (c t) -> p c t", dict(c=NCP)))
    rot = 0
    for ps_ in range(32 // NCP):
        g = (ps_ * NCP) // 4
        c_lo = ps_ * NCP
        S.dma("sp", xs.ap, C.xsT[c_lo:c_lo + NCP].rearrange("c p t -> p c t"), writes=[xs.k])
        for c in range(NCP):
            S.dma("sp", H0.ap[:, c], I.ssd_h0[:, c_lo + c].rearrange("q p s -> p q s"), writes=[H0.k])
        S.dma("sp", BTg.ap, C.bcT[g], writes=[BTg.k])
        S.dma("sp", CTg.ap, C.bcT[8 + g], writes=[CTg.k])
        for c in range(NCP):
            cg = c_lo + c
            for hl in range(2):
                S.op("dve", lambda e, hl=hl, cg=cg: e.tensor_copy(selt.ap[:, hl * 64:(hl + 1) * 64], C.ident.ap[:, 2 * cg + hl:2 * cg + hl + 1].broadcast_to([128, 64])),
                     reads=[C.ident.k], writes=[selt.k])
            for (c0, cn) in TSPL:
                b1 = bank(C)
                S.op("pe", lambda e, b1=b1, c0=c0, cn=cn: e.matmul(b1.ap[:, 0:cn], selt.ap, aT.ap[:, c0:c0 + cn], start=True, stop=True),
                     reads=[selt.k, aT.k], writes=[b1.k])
                S.op("act", lambda e, b1=b1, c=c, c0=c0, cn=cn: e.copy(abc.ap[:, c, c0:c0 + cn], b1.ap[:, 0:cn]), reads=[b1.k], writes=[abc.k])
                b2 = bank(C)
                S.op("pe", lambda e, b2=b2, c0=c0, cn=cn: e.matmul(b2.ap[:, 0:cn], selt.ap, dtT.ap[:, c0:c0 + cn], start=True, stop=True),
                     reads=[selt.k, dtT.k], writes=[b2.k])
                S.op("dve", lambda e, c=c, cg=cg, c0=c0, cn=cn: e.tensor_scalar(yacc.ap[:, c, c0:c0 + cn], xs.ap[:, c, c0:c0 + cn], V[:, V_SD + cg:V_SD + cg + 1], None, ALU.mult),
                     reads=[xs.k, C.vec.k], writes=[yacc.k])
                S.op("dve", lambda e, b2=b2, c=c, c0=c0, cn=cn: e.tensor_tensor(xs.ap[:, c, c0:c0 + cn], xs.ap[:, c, c0:c0 + cn], b2.ap[:, 0:cn], ALU.mult),
                     reads=[xs.k, b2.k, yacc.k], writes=[xs.k])
        for s_ in range(128):
            S.op("dve", lambda e, s_=s_: e.tensor_copy(selb.ap, C.identb.ap[:, s_:s_ + 1].broadcast_to([128, 128])), reads=[C.identb.k], writes=[selb.k])
            for (src, dst) in ((BTg, Bsb), (CTg, Csb)):
                for (c0, cn) in TSPL:
                    b1 = bank(C)
                    S.op("pe", lambda e, b1=b1, c0=c0, cn=cn, src=src: e.matmul(b1.ap[:, 0:cn], selb.ap, src.ap[:, c0:c0 + cn], start=True, stop=True),
                         reads=[selb.k, src.k], writes=[b1.k])
                    S.op("act", lambda e, b1=b1, c0=c0, cn=cn, dst=dst: e.copy(dst.ap[:, c0:c0 + cn], b1.ap[:, 0:cn]), reads=[b1.k], writes=[dst.k])
            for c in range(NCP):
                d1, Hs, tm = d1s[rot % 2], Hss[rot % 2], tms[rot % 2]
                rot += 1
                S.op("pool", lambda e, c=c, d1=d1: e.tensor_tensor(d1.ap, xs.ap[:, c, :], Bsb.ap, ALU.mult), reads=[xs.k, Bsb.k], writes=[d1.k])
                S.op("dve", lambda e, c=c, d1=d1, Hs=Hs: e.tensor_tensor_scan(Hs.ap[:, 0:LP], abc.ap[:, c, 0:LP], d1.ap[:, 0:LP], 0.0, ALU.mult, ALU.add),
                     reads=[abc.k, d1.k], writes=[Hs.k])
                for q in range(NS):
                    cs = slice(LP + q * LS, LP + (q + 1) * LS)
                    S.op("dve", lambda e, c=c, cs=cs, q=q, s_=s_, d1=d1, Hs=Hs: e.tensor_tensor_scan(Hs.ap[:, cs], abc.ap[:, c, cs], d1.ap[:, cs], H0.ap[:, c, q, s_:s_ + 1], ALU.mult, ALU.add),
                         reads=[abc.k, d1.k, H0.k], writes=[Hs.k])
                S.op("act", lambda e, c=c, s_=s_, Hs=Hs: e.copy(stF.ap[:, c, 0, s_:s_ + 1], Hs.ap[:, LP - 1:LP]), reads=[Hs.k], writes=[stF.k])
                S.op("act", lambda e, c=c, s_=s_, Hs=Hs: e.copy(stF.ap[:, c, 1:NCOND, s_], samp(Hs.ap[:, LP:T], LS)[:, :, LS - 1]), reads=[Hs.k], writes=[stF.k])
                S.op("pool", lambda e, Hs=Hs, tm=tm: e.tensor_tensor(tm.ap, Hs.ap, Csb.ap, ALU.mult), reads=[Hs.k, Csb.k], writes=[tm.k])
                S.op("dve", lambda e, c=c, tm=tm: e.tensor_tensor(yacc.ap[:, c, :], yacc.ap[:, c, :], tm.ap, ALU.add), reads=[yacc.k, tm.k], writes=[yacc.k])
        S.dma("sp", zt.ap, C.zT[c_lo:c_lo + NCP].rearrange("c p t -> p c t"), writes=[zt.k])
        S.op("dve", lambda e: e.tensor_tensor(yacc.ap, yacc.ap, zt.ap, ALU.mult), reads=[yacc.k, zt.k], writes=[yacc.k])
        S.dma("sp", C.ygT[c_lo:c_lo + NCP].rearrange("c p t -> p c t"), yacc.ap, reads=[yacc.k])
        for c in range(NCP):
            S.dma("sp", O.ssd_p[c_lo + c], stF.ap[:, c, 0, :], reads=[stF.k])
            S.dma("sp", O.ssd_s[:, c_lo + c].rearrange("q p s -> p q s"), stF.ap[:, c, 1:NCOND, :], reads=[stF.k])
    S.barrier()
    A.off = keep
    yg = A.f32(4 * T, ("p (c t) -> p c t", dict(c=4)))
    sq = A.bf16(4 * T, ("p (c t) -> p c t", dict(c=4)))
    rstd = A.f32(T)
    yo = A.bf16(4 * T, ("p (c t) -> p c t", dict(c=4)))
    for g in range(8):
        S.dma("sp", yg.ap, C.ygT[4 * g:4 * g + 4].rearrange("c p t -> p c t"), writes=[yg.k])
        S.op("act", lambda e: e.activation(out=sq.ap, in_=yg.ap, func=AF.Square), reads=[yg.k], writes=[sq.k])
        for (c0, cn) in TSPL:
            b = bank(C)
            for c in range(4):
                S.op("pe", lambda e, b=b, c=c, c0=c0, cn=cn: e.matmul(b.ap[:, 0:cn], C.onesb.ap, sq.ap[:, c, c0:c0 + cn], start=(c == 0), stop=(c == 3)),
                     reads=[sq.k, C.onesb.k], writes=[b.k])
            S.op("act", lambda e, b=b, c0=c0, cn=cn: e.activation(out=rstd.ap[:, c0:c0 + cn], in_=b.ap[:, 0:cn], func=AF.Sqrt, bias=C.epsb.ap, scale=1.0 / 512),
                 reads=[b.k, C.epsb.k], writes=[rstd.k])
        S.op("dve", lambda e: e.reciprocal(rstd.ap, rstd.ap), reads=[rstd.k], writes=[rstd.k])
        for c in range(4):
            S.op("dve", lambda e, c=c, g=g: e.scalar_tensor_tensor(out=yo.ap[:, c, :], in0=yg.ap[:, c, :], scalar=V[:, V_SNORM + 4 * g + c:V_SNORM + 4 * g + c + 1],
                                                                 in1=rstd.ap, op0=ALU.mult, op1=ALU.mult),
                 reads=[yg.k, rstd.k, C.vec.k], writes=[yo.k])
        S.dma("sp", C.oT[4 * g:4 * g + 4].rearrange("c p t -> p c t"), yo.ap, reads=[yo.k])
    S.barrier()


MIXERS = {}


def layer(C, l):
    S = C.S
    A = Arena(C)
    if l == 0:
        C.modT = es_persist(C, "modT", 96 * NCOND, F32)
        C.modT.ap = C.modT.ap.rearrange("p (n c) -> p n c", c=NCOND)
        C.Amix = es_persist(C, "Amix", 16 * NCOND, F32)
        C.Amix.ap = C.Amix.ap.rearrange("p (n c) -> p n c", c=NCOND)
        C.Affn = es_persist(C, "Affn", 16 * NCOND, F32)
        C.Affn.ap = C.Affn.ap.rearrange("p (n c) -> p n c", c=NCOND)
        C.epsb = es_persist(C, "epsb", 1, F32)
        S.op("dve", lambda e: e.memset(C.epsb.ap, EPS), writes=[C.epsb.k])
        C.wbs = [es_persist(C, "wb%d" % i, WSLAB, BF16) for i in range(2)]
        C.wrr = 0
        C.rrr = 0
    ada_phase(C, l, A)
    S.barrier()
    kind = l % 4
    mo = None
    en = C.dbg.get("mixers", (0, 1, 2, 3))
    I, O = C.I, C.O
    if kind == 3 and 3 in en:
        lru_phase(C, l, A)
        mo = make_mixer_out(16, lambda C: C.I.lru_wout)
    if kind == 1 and 1 in en:
        ssd_phase(C, l, A)
        mo = make_mixer_out(32, lambda C: C.I.ssd_wout)
    if kind == 0 and 0 in en:
        cfg = dict(name="swa", hd=64, G=1, HQ=8, dils=[1], wins=[128], rope=I.rope64, perm=I.perm64,
                   wq=I.swa_wq, wk=I.swa_wk, wv=I.swa_wv, wqp=I.swa_wqp, wkp=I.swa_wkp, cache=[I.swa_cache], kv_out=[(O.swa_kv_p, O.swa_kv_s)])
        attn_phase(C, l, A, cfg)
        mo = make_mixer_out(16, lambda C: C.I.swa_wo)
    if kind == 2 and 2 in en:
        cfg = dict(name="dil", hd=128, G=3, HQ=4, dils=[1, 4, 16], wins=[128, 512, 2048], rope=I.rope128, perm=I.perm128,
                   wq=I.dil_wq, wk=I.dil_wk, wv=I.dil_wv, wqp=I.dil_wqp, wkp=I.dil_wkp, cache=[I.dil_c0, I.dil_c1, I.dil_c2],
                   kv_out=[(O.dil_kv0_p, O.dil_kv0_s), (O.dil_kv1_p, O.dil_kv1_s), (O.dil_kv2_p, O.dil_kv2_s)])
        attn_phase(C, l, A, cfg)
        mo = make_mixer_out(16, lambda C: C.I.dil_wo)
    A.off = 0
    ffn_phase(C, l, A, mo)
    S.barrier()


def final(C):
    S = C.S
    A = Arena(C)
    NT = 512
    xT = A.f32(KD * NT, ("p (k t) -> p k t", dict(k=KD)))
    sq = A.bf16(KD * NT, ("p (k t) -> p k t", dict(k=KD)))
    rstd = A.f32(NT)
    yo = [A.f32(D) for _ in range(2)]
    gv = C.vec.ap[:, V_GFIN:V_GFIN + 16]
    cnt = 0
    for (t0, nt) in splits(T, NT):
        S.dma("sp", xT.ap[:, :, 0:nt], C.xres[:, :, t0:t0 + nt].rearrange("k p t -> p k t"), writes=[xT.k])
        S.op("act", lambda e, nt=nt: e.activation(out=sq.ap[:, :, 0:nt], in_=xT.ap[:, :, 0:nt], func=AF.Square),
             reads=[xT.k], writes=[sq.k])
        b = bank(C)
        for k in range(KD):
            S.op("pe", lambda e, k=k, b=b, nt=nt: e.matmul(b.ap[:, 0:nt], C.onesb.ap, sq.ap[:, k, 0:nt],
                                                        start=(k == 0), stop=(k == KD - 1)),
                 reads=[sq.k, C.onesb.k], writes=[b.k])
        S.op("act", lambda e, b=b, nt=nt: e.activation(out=rstd.ap[:, 0:nt], in_=b.ap[:, 0:nt], func=AF.Sqrt,
                                                       bias=C.epsb.ap, scale=1.0 / D),
             reads=[b.k, C.epsb.k], writes=[rstd.k])
        S.op("dve", lambda e, nt=nt: e.reciprocal(rstd.ap[:, 0:nt], rstd.ap[:, 0:nt]), reads=[rstd.k], writes=[rstd.k])
        S.op("dve", lambda e, nt=nt: e.tensor_tensor(
            xT.ap[:, :, 0:nt], xT.ap[:, :, 0:nt], rstd.ap[:, 0:nt].unsqueeze(1).broadcast_to([128, KD, nt]), ALU.mult),
            reads=[xT.k, rstd.k], writes=[xT.k])
        S.op("dve", lambda e, nt=nt: e.tensor_tensor(
            xT.ap[:, :, 0:nt], xT.ap[:, :, 0:nt], gv.unsqueeze(2).broadcast_to([128, KD, nt]), ALU.mult),
            reads=[xT.k, C.vec.k], writes=[xT.k])
        for (c0, n) in splits(nt, 128):
            y = yo[cnt % 2]
            cnt += 1
            for g in range(4):
                b = bank(C)
                for j in range(4):
                    k = g * 4 + j
                    S.op("pe", lambda e, k=k, j=j, b=b, c0=c0, n=n: e.transpose(
                        b.ap[0:n, j * 128:(j + 1) * 128], xT.ap[:, k, c0:c0 + n], C.ident.ap),
                        reads=[xT.k, C.ident.k], writes=[b.k])
                if g % 2 == 0:
                    S.op("act", lambda e, g=g, b=b, n=n, y=y: e.copy(y.ap[0:n, g * 512:(g + 1) * 512], b.ap[0:n, :]),
                         reads=[b.k], writes=[y.k])
                else:
                    S.op("dve", lambda e, g=g, b=b, n=n, y=y: e.tensor_copy(y.ap[0:n, g * 512:(g + 1) * 512], b.ap[0:n, :]),
                         reads=[b.k], writes=[y.k])
            S.dma("sp", C.O.y[t0 + c0:t0 + c0 + n, :], y.ap[0:n, :], reads=[y.k])


def make_inputs(inp, core):
    b = core % 4
    f = lambda a: np.ascontiguousarray(np.asarray(a, dtype=np.float32))
    m = {}
    xs = f(inp["x_sample"])[core * NS:(core + 1) * NS].reshape(TS, D)
    m["xin"] = np.concatenate([f(inp["x_prompt"])[b], xs], axis=0)
    m["cond"] = np.concatenate([f(inp["c_prompt"])[b:b + 1], f(inp["c_sample"])[core * NS:(core + 1) * NS]], axis=0)
    sl = slice(core * NS, (core + 1) * NS)
    m["swa_cache"] = f(inp["cache_swa_kv"])[0, sl]
    m["dil_c0"] = f(inp["cache_dil_kv_w128"])[0, sl]
    m["dil_c1"] = f(inp["cache_dil_kv_w512"])[0, sl]
    m["dil_c2"] = f(inp["cache_dil_kv_w2048"])[0, sl]
    m["ssd_conv_in"] = f(inp["state_ssd_conv"])[0, sl].reshape(NS * 3, 6144)
    m["ssd_h0"] = f(inp["state_ssd"])[0, sl].reshape(NS, 32, 128, 128)
    m["lru_conv_in"] = f(inp["state_lru_conv"])[0, sl].reshape(NS * 3, D)
    m["lru_h0"] = f(inp["state_lru"])[0, sl]
    return m


def shared_inputs(inp):
    f = lambda a: np.ascontiguousarray(np.asarray(a, dtype=np.float32))
    m = {}
    m["ident"] = np.eye(128, dtype=np.float32)
    vecs = np.zeros((NVEC, 128), np.float32)
    vecs[V_BADA:V_BADA + DEPTH * 96] = f(inp["b_ada"]).reshape(DEPTH * 96, 128)
    vecs[V_GMIX:V_GMIX + DEPTH * 16] = f(inp["g_mix"]).reshape(DEPTH * 16, 128)
    vecs[V_GFFN:V_GFFN + DEPTH * 16] = f(inp["g_ffn"]).reshape(DEPTH * 16, 128)
    vecs[V_GFIN:V_GFIN + 16] = f(inp["g_final"]).reshape(16, 128)
    vecs[V_SCW:V_SCW + 192] = f(inp["ssd_conv_w"]).reshape(192, 128)
    vecs[V_SCB:V_SCB + 48] = f(inp["ssd_conv_b"]).reshape(48, 128)
    vecs[V_SNORM:V_SNORM + 32] = f(inp["ssd_norm"]).reshape(32, 128)
    vecs[V_SD:V_SD + 32] = np.repeat(f(inp["ssd_d"])[0], 64).reshape(32, 128)
    vecs[V_SDTB, 0:64] = f(inp["ssd_dt_bias"])[0]
    vecs[V_SALOG, 0:64] = f(inp["ssd_a_log"])[0]
    vecs[V_LBIN:V_LBIN + 32] = f(inp["lru_b_in"]).reshape(32, 128)
    vecs[V_LCW:V_LCW + 64] = f(inp["lru_conv_w"]).reshape(64, 128)
    vecs[V_LCB:V_LCB + 16] = f(inp["lru_conv_b"]).reshape(16, 128)
    vecs[V_LBR:V_LBR + 16] = f(inp["lru_b_r"]).reshape(16, 128)
    vecs[V_LBI:V_LBI + 16] = f(inp["lru_b_i"]).reshape(16, 128)
    vecs[V_LLAM:V_LLAM + 16] = f(inp["lru_lam"]).reshape(16, 128)
    m["vecs"] = vecs
    pos = np.concatenate([np.arange(LP), PAST + (np.arange(TS) % LS)]).astype(np.float32)
    for hd in (64, 128):
        half = hd // 2
        inv = (10000.0 ** (-np.arange(half, dtype=np.float32) / np.float32(half))).astype(np.float32)
        ang = (pos[None, :] * inv[:, None]).astype(np.float32)
        p = np.arange(128)
        dd = p % hd
        cosT = np.cos(ang)[dd % half].astype(np.float32)
        sinT = np.sin(ang)[dd % half].astype(np.float32) * np.where(dd < half, -1.0, 1.0).astype(np.float32)[:, None]
        m["rope%d" % hd] = np.ascontiguousarray(np.stack([cosT, sinT]))
        perm = np.zeros((128, 128), np.float32)
        partner = (p - dd) + (dd + half) % hd
        perm[partner, p] = 1.0
        m["perm%d" % hd] = perm
    qi = np.arange(128)[:, None]
    si = np.arange(256)[None, :]
    m["pmask"] = np.where((si >= qi) & (si <= qi + 128), 0.0, MASKV).astype(np.float32)
    sm = np.full((3, 32, 2176), MASKV, np.float32)
    tt = (np.arange(32) % LS)[:, None]
    for g, (w, d) in enumerate(((128, 1), (512, 4), (2048, 16))):
        c = np.arange(w)[None, :]
        sm[g, :, :w] = np.where((c % d == tt % d) & (c >= tt), 0.0, MASKV)
        tn = np.arange(LS)[None, :]
        sm[g, :, w:w + LS] = np.where((tn <= tt) & (tn % d == tt % d), 0.0, MASKV)
    m["smask"] = sm
    sk = f(inp["swa_sinks"])[0]
    m["swa_sinks"] = sk.reshape(1, 32)
    rr_ = np.arange(32) // LS
    m["swa_sinkS"] = np.stack([sk[(u // 2) * 8 + 2 * rr_ + (u % 2)] for u in range(8)], axis=1).astype(np.float32)
    wqkv = f(inp["swa_w_qkv"][0])
    m["swa_wq"] = tile_w(wqkv[:, 0:2048], 512).reshape(4, 128, KD * 512)
    wk = wqkv[:, 2048:2304].reshape(D, 4, 64)
    wks = np.zeros((D, 4, 2, 2, 64), np.float32)
    wks[:, :, 0, 0, :] = wk
    wks[:, :, 1, 1, :] = wk
    m["swa_wk"] = tile_w(wks.reshape(D, 1024), 512).reshape(2, 128, KD * 512)
    m["swa_wv"] = tile_w(wqkv[:, 2304:2560], 256).reshape(1, 128, KD * 256)

    def rot_cols(w2, hd_):
        K_, N_ = w2.shape
        w3 = w2.reshape(K_, N_ // hd_, hd_)
        return np.ascontiguousarray(np.concatenate([w3[:, :, hd_ // 2:], w3[:, :, :hd_ // 2]], axis=2)).reshape(K_, N_)

    m["swa_wqp"] = tile_w(rot_cols(wqkv[:, 0:2048], 64), 512).reshape(4, 128, KD * 512)
    m["swa_wkp"] = tile_w(rot_cols(wks.reshape(D, 1024), 64), 512).reshape(2, 128, KD * 512)
    m["swa_wo"] = tile_w(f(inp["swa_w_o"][0]), 512).reshape(4, 128, KD * 512)
    dq = f(inp["dil_w_qkv"][0])
    m["dil_wq"] = tile_w(dq[:, 0:6144], 512).reshape(12, 128, KD * 512)
    m["dil_wk"] = tile_w(dq[:, 6144:7680], 512).reshape(3, 128, KD * 512)
    m["dil_wv"] = tile_w(dq[:, 7680:9216], 512).reshape(3, 128, KD * 512)
    m["dil_wqp"] = tile_w(rot_cols(dq[:, 0:6144], 128), 512).reshape(12, 128, KD * 512)
    m["dil_wkp"] = tile_w(rot_cols(dq[:, 6144:7680], 128), 512).reshape(3, 128, KD * 512)
    m["dil_wo"] = tile_w(f(inp["dil_w_o"][0]), 512).reshape(4, 128, KD * 512)
    win = f(inp["ssd_w_in"][0])
    m["ssd_wz"] = tile_w(win[:, 0:4096], 512).reshape(8, 128, KD * 512)
    m["ssd_wx"] = tile_w(win[:, 4096:10240], 512).reshape(12, 128, KD * 512)
    wdt = np.zeros((D, 128), np.float32)
    wdt[:, 0:64] = win[:, 10240:10304]
    m["ssd_wdt"] = tile_w(wdt, 128).reshape(1, 128, KD * 128)
    m["ssd_wout"] = tile_w(f(inp["ssd_w_out"][0]), 256).reshape(8, 128, 32 * 256)
    m["lru_win"] = tile_w(f(inp["lru_w_in"][0]), 512).reshape(8, 128, KD * 512)
    m["lru_wout"] = tile_w(f(inp["lru_w_out"][0]), 512).reshape(4, 128, KD * 512)
    for nm, key in (("lru_wr", "lru_w_r"), ("lru_wi", "lru_w_i")):
        w = f(inp[key][0])
        m[nm] = np.ascontiguousarray(w.reshape(8, 2, 128, 256).transpose(2, 0, 1, 3)).reshape(128, 4096)
    m["w_ada"] = np.stack([tile_w(f(inp["w_ada"][l]), 512) for l in range(DEPTH)]).reshape(DEPTH, 24, 128, KD * 512)
    m["w_ff1"] = np.stack([tile_w(f(inp["w_ff1"][l]), 512) for l in range(DEPTH)]).reshape(DEPTH, 16, 128, KD * 512)
    m["w_ff2"] = np.stack([tile_w(f(inp["w_ff2"][l]), 128) for l in range(DEPTH)]).reshape(DEPTH, 16, 128, 64 * 128)
    return m


_NC_CACHE = {}


def run(inp, dbg=None, trace=False, cores=None):
    cores = list(range(NCORES)) if cores is None else cores
    key = tuple(sorted((dbg or {}).items()))
    if key not in _NC_CACHE:
        _NC_CACHE[key] = build(dbg)
    nc = _NC_CACHE[key]
    sh = shared_inputs(inp)
    in_maps = []
    for c in cores:
        m = dict(sh)
        m.update(make_inputs(inp, c))
        in_maps.append(m)
    res = run_bass_kernel_spmd(nc, in_maps, core_ids=list(range(len(cores))), trace=trace)
    return res


def kernel(**inp):
    res = run(inp)
    R = res.results
    f32 = np.float32

    def pr(name, shape):
        return np.stack([np.asarray(R[b][name], f32).reshape(shape) for b in range(4)])[None]

    def sa(name, shape):
        return np.concatenate([np.asarray(R[c][name], f32).reshape((NS,) + shape) for c in range(NCORES)], axis=0)[None]

    y_prompt = np.stack([np.asarray(R[b]["y"], f32)[:LP] for b in range(4)])
    y_sample = np.concatenate([np.asarray(R[c]["y"], f32)[LP:].reshape(NS, LS, D) for c in range(NCORES)], axis=0)
    outs = [y_prompt, y_sample,
            pr("swa_kv_p", (128, 2, 4, 64)), sa("swa_kv_s", (LS, 2, 4, 64)),
            pr("ssd_conv_p", (3, 6144)), sa("ssd_conv_s", (3, 6144)),
            pr("ssd_p", (64, 64, 128)), sa("ssd_s", (64, 64, 128))]
    for g, w in enumerate((128, 512, 2048)):
        outs.append(pr("dil_kv%d_p" % g, (w, 2, 4, 128)))
        outs.append(sa("dil_kv%d_s" % g, (LS, 2, 4, 128)))
    outs += [pr("lru_conv_p", (3, D)), sa("lru_conv_s", (3, D)), pr("lru_p", (D,)), sa("lru_s", (D,))]
    return tuple(outs)
```

```python
import contextlib
import math
import numpy as np
import concourse.bass as bass
import concourse.mybir as mybir
from concourse.bass_utils import run_bass_kernel_spmd

F32 = mybir.dt.float32
BF16 = mybir.dt.bfloat16
AF = mybir.ActivationFunctionType
ALU = mybir.AluOpType
AX = mybir.AxisListType

NCORES = 8
D = 2048
KD = D // 128
DFF = 8192
DEPTH = 4
LP = 2048
NS = 4
LS = 8
TS = NS * LS
T = LP + TS
NCOND = 1 + NS
PAST = 16384
EPS = 1e-6
TILES = [(0, 512), (512, 512), (1024, 512), (1536, 544)]
TT = 544
WSLAB = 8192
NEG = -1e30
MASKV = -30000.0
NA = 49 * 1024

DEBUG = {}


class Trk:
    __slots__ = ("w", "r")

    def __init__(self):
        self.w = None
        self.r = {}


class Sched:
    def __init__(self, nc, es, n_dma_sems=56):
        self.nc = nc
        self.es = es
        self.sems = []
        self.engs = {}
        for name in ("pe", "act", "dve", "pool", "sp"):
            sem = es.enter_context(nc.semaphore("s_" + name))
            self.sems.append(sem)
            self.engs[name] = dict(semi=len(self.sems) - 1, cnt=0, prog=[], seen={})
        self.dslots = []
        for i in range(n_dma_sems):
            sem = es.enter_context(nc.semaphore("d_%d" % i))
            self.sems.append(sem)
            self.dslots.append([len(self.sems) - 1, 0])
        self.drr = 0
        self.same_engine_sync = True

    def _waits(self, ename, raw, war):
        E = self.engs[ename]
        need = {}
        for (semi, val) in raw:
            if semi == E["semi"] and (ename == "pe" or not self.same_engine_sync):
                continue
            if need.get(semi, 0) < val:
                need[semi] = val
        for (semi, val) in war:
            if semi == E["semi"]:
                continue
            if need.get(semi, 0) < val:
                need[semi] = val
        out = []
        for semi, val in need.items():
            if E["seen"].get(semi, 0) >= val:
                continue
            E["seen"][semi] = val
            out.append((semi, val))
        return out

    def op(self, ename, fn, reads=(), writes=()):
        E = self.engs[ename]
        raw = [t.w for t in reads if t.w] + [t.w for t in writes if t.w]
        war = [ev for t in writes for ev in t.r.values()]
        waits = self._waits(ename, raw, war)
        E["cnt"] += 1
        ev = (E["semi"], E["cnt"])
        for t in reads:
            t.r[ename] = ev
        for t in writes:
            t.w = ev
            t.r = {}
        E["prog"].append((waits, fn, True))

    def dma(self, qname, out, in_, reads=(), writes=(), **kw):
        Q = self.engs[qname]
        slot = self.dslots[self.drr]
        self.drr = (self.drr + 1) % len(self.dslots)
        raw = [t.w for t in reads if t.w] + [t.w for t in writes if t.w]
        if slot[1] > 0:
            raw.append((slot[0], 16 * slot[1]))
        war = [ev for t in writes for ev in t.r.values()]
        waits = self._waits(qname, raw, war)
        slot[1] += 1
        ev = (slot[0], 16 * slot[1])
        for t in reads:
            t.r[("d", slot[0])] = ev
        for t in writes:
            t.w = ev
            t.r = {}
        sem = self.sems[slot[0]]
        Q["prog"].append((waits, lambda e: e.dma_start(out=out, in_=in_, **kw).then_inc(sem, 16), False))

    def barrier(self):
        for ename, E in self.engs.items():
            raw = [(F["semi"], F["cnt"]) for fn, F in self.engs.items() if fn != ename and F["cnt"] > 0]
            raw += [(s[0], 16 * s[1]) for s in self.dslots if s[1] > 0]
            waits = self._waits(ename, [], raw)
            if waits:
                E["prog"].append((waits, None, False))

    def emit(self):
        sems = self.sems

        def mk(ename):
            E = self.engs[ename]

            def body(e):
                for waits, fn, inc in E["prog"]:
                    for (semi, val) in waits:
                        e.wait_ge(sems[semi], val)
                    if fn is not None:
                        ins = fn(e)
                        if inc:
                            ins.then_inc(sems[E["semi"]], 1)
            return body

        with self.nc.Block() as block:
            block.tensor(mk("pe"))
            block.scalar(mk("act"))
            block.vector(mk("dve"))
            block.gpsimd(mk("pool"))
            block.sync(mk("sp"))


class Tile:
    def __init__(self, ap, nsub=0, trk=None, sub=None):
        self.ap = ap
        self.k = trk if trk is not None else Trk()
        self.sub = sub if sub is not None else [Trk() for _ in range(nsub)]

    def __getitem__(self, key):
        return self.ap[key]

    @property
    def all(self):
        return [self.k] + self.sub


class Ctx:
    pass


def splits(n, m=512):
    out = []
    c = 0
    while c < n:
        out.append((c, min(m, n - c)))
        c += m
    return out


def col_segments(t0, nt):
    segs = []
    t1 = t0 + nt
    if t0 < LP:
        segs.append((0, min(t1, LP) - t0, "p", 0, 1))
    if t1 > LP:
        s0 = max(t0, LP)
        assert (s0 - LP) % LS == 0 and (t1 - LP) % LS == 0
        segs.append((s0 - t0, t1 - s0, "s", 1 + (s0 - LP) // LS, (t1 - s0) // LS))
    return segs


def tile_w(W, ncols):
    K, N = W.shape
    assert K % 128 == 0 and N % ncols == 0
    return np.ascontiguousarray(W.reshape(K // 128, 128, N // ncols, ncols).transpose(2, 1, 0, 3))


def build(dbg=None):
    dbg = dbg or {}
    nc = bass.Bass("TRN2", target_bir_lowering=False)
    es = contextlib.ExitStack()
    C = Ctx()
    C.nc = nc
    C.dbg = dbg
    S = Sched(nc, es)
    C.S = S

    def din(name, shape, dt=F32):
        return nc.dram_tensor(name, list(shape), dt, kind="ExternalInput").ap()

    def dout(name, shape, dt=F32):
        return nc.dram_tensor(name, list(shape), dt, kind="ExternalOutput").ap()

    def dscr(name, shape, dt=F32):
        if dbg.get("dump_" + name):
            return nc.dram_tensor(name, list(shape), dt, kind="ExternalOutput").ap()
        return nc.dram_tensor(name, list(shape), dt, kind="Internal").ap()

    C.din, C.dout, C.dscr = din, dout, dscr

    I = Ctx()
    C.I = I
    I.xin = din("xin", [T, D])
    I.cond = din("cond", [NCOND, D])
    I.ident = din("ident", [128, 128])
    I.vecs = din("vecs", [NVEC, 128])
    I.w_ada = din("w_ada", [DEPTH, 24, 128, KD * 512])
    I.w_ff1 = din("w_ff1", [DEPTH, 16, 128, KD * 512])
    I.w_ff2 = din("w_ff2", [DEPTH, 16, 128, 64 * 128])
    I.lru_conv_in = din("lru_conv_in", [NS * 3, D])
    I.lru_h0 = din("lru_h0", [NS, D])
    I.lru_win = din("lru_win", [8, 128, KD * 512])
    I.lru_wr = din("lru_wr", [128, 4096])
    I.lru_wi = din("lru_wi", [128, 4096])
    I.lru_wout = din("lru_wout", [4, 128, KD * 512])
    I.rope64 = din("rope64", [2, 128, T])
    I.rope128 = din("rope128", [2, 128, T])
    I.perm64 = din("perm64", [128, 128])
    I.perm128 = din("perm128", [128, 128])
    I.pmask = din("pmask", [128, 256])
    I.smask = din("smask", [3, 32, 2176])
    I.swa_sinks = din("swa_sinks", [1, 32])
    I.swa_sinkS = din("swa_sinkS", [32, 8])
    I.swa_cache = din("swa_cache", [NS, 128, 2, 4, 64])
    I.swa_wq = din("swa_wq", [4, 128, KD * 512])
    I.swa_wk = din("swa_wk", [2, 128, KD * 512])
    I.swa_wv = din("swa_wv", [1, 128, KD * 256])
    I.swa_wqp = din("swa_wqp", [4, 128, KD * 512])
    I.swa_wkp = din("swa_wkp", [2, 128, KD * 512])
    I.dil_wqp = din("dil_wqp", [12, 128, KD * 512])
    I.dil_wkp = din("dil_wkp", [3, 128, KD * 512])
    I.swa_wo = din("swa_wo", [4, 128, KD * 512])
    I.dil_c0 = din("dil_c0", [NS, 128, 2, 4, 128])
    I.dil_c1 = din("dil_c1", [NS, 512, 2, 4, 128])
    I.dil_c2 = din("dil_c2", [NS, 2048, 2, 4, 128])
    I.dil_wq = din("dil_wq", [12, 128, KD * 512])
    I.dil_wk = din("dil_wk", [3, 128, KD * 512])
    I.dil_wv = din("dil_wv", [3, 128, KD * 512])
    I.dil_wo = din("dil_wo", [4, 128, KD * 512])
    I.ssd_conv_in = din("ssd_conv_in", [NS * 3, 6144])
    I.ssd_h0 = din("ssd_h0", [NS, 32, 128, 128])
    I.ssd_wz = din("ssd_wz", [8, 128, KD * 512])
    I.ssd_wx = din("ssd_wx", [12, 128, KD * 512])
    I.ssd_wdt = din("ssd_wdt", [1, 128, KD * 128])
    I.ssd_wout = din("ssd_wout", [8, 128, 32 * 256])
    O = Ctx()
    C.O = O
    O.ssd_conv_p = dout("ssd_conv_p", [3, 6144])
    O.ssd_conv_s = dout("ssd_conv_s", [NS * 3, 6144])
    O.ssd_p = dout("ssd_p", [32, 128, 128])
    O.ssd_s = dout("ssd_s", [NS, 32, 128, 128])
    C.zT = dscr("zT", [32, 128, T], BF16)
    C.xsT = dscr("xsT", [32, 128, T], F32)
    C.bcT = dscr("bcT", [16, 128, T], BF16)
    C.ygT = dscr("ygT", [32, 128, T], F32)
    O.swa_kv_p = dout("swa_kv_p", [128, 512])
    O.swa_kv_s = dout("swa_kv_s", [TS, 512])
    for g, w in enumerate((128, 512, 2048)):
        setattr(O, "dil_kv%d_p" % g, dout("dil_kv%d_p" % g, [w, 1024]))
        setattr(O, "dil_kv%d_s" % g, dout("dil_kv%d_s" % g, [TS, 1024]))
    C.Oh = dscr("Oh", [3, T, D])
    C.lseh = dscr("lseh", [3, T, 32])
    O.y = dout("y", [T, D])
    O.lru_conv_p = dout("lru_conv_p", [3, D])
    O.lru_conv_s = dout("lru_conv_s", [NS * 3, D])
    O.lru_p = dout("lru_p", [1, D])
    O.lru_s = dout("lru_s", [NS, D])
    C.oT = dscr("oT", [32, 128, T], BF16)
    C.gT = dscr("gT", [KD, 128, T], BF16)
    C.xres = dscr("xres", [KD, 128, T])

    arena = es.enter_context(nc.sbuf_tensor("arena", [128, NA], F32))
    C.arena = arena
    consts = es.enter_context(nc.sbuf_tensor("consts", [128, 128 + 64 + 128 + NVEC], F32))
    C.ident = Tile(consts[:, 0:128])
    C.identb = Tile(consts[:, 128:192].bitcast(BF16))
    C.onesb = Tile(consts[:, 192:256].bitcast(BF16))
    C.onesf = Tile(consts[:, 256:320])
    C.vec = Tile(consts[:, 320:320 + NVEC])
    psall = es.enter_context(nc.psum_tensor("psall", [128, 4096], F32))
    C.psall = psall
    C.banks = [Tile(psall[:, i * 512:(i + 1) * 512]) for i in range(8)]
    C.brr = 0

    prologue(C)
    for l in range(DEPTH):
        layer(C, l)
    final(C)
    S.barrier()
    S.emit()
    es.close()
    return nc


def bank(C):
    b = C.banks[C.brr]
    C.brr = (C.brr + 1) % 8
    return b


def mbank(C, nb):
    if C.brr + nb > 8:
        C.brr = 0
    i0 = C.brr
    C.brr = (C.brr + nb) % 8
    t = Tile(C.psall[:, i0 * 512:(i0 + nb) * 512], trk=C.banks[i0].k, sub=[C.banks[i].k for i in range(i0 + 1, i0 + nb)])
    t.banks = [C.banks[i] for i in range(i0, i0 + nb)]
    return t


class Arena:
    def __init__(self, C):
        self.C = C
        self.off = 0

    def f32(self, n, shape=None, nsub=0):
        ap = self.C.arena[:, self.off:self.off + n]
        self.off += n
        assert self.off <= getattr(self.C, "ptop", NA), "arena overflow %d" % self.off
        if shape:
            ap = ap.rearrange(shape[0], **shape[1])
        return Tile(ap, nsub)

    def bf16(self, n, shape=None, nsub=0):
        n32 = (n + 1) // 2
        ap = self.C.arena[:, self.off:self.off + n32].bitcast(BF16)
        self.off += n32
        assert self.off <= getattr(self.C, "ptop", NA), "arena overflow %d" % self.off
        if shape:
            ap = ap.rearrange(shape[0], **shape[1])
        return Tile(ap, nsub)


V_BADA = 0
V_GMIX = V_BADA + DEPTH * 96
V_GFFN = V_GMIX + DEPTH * 16
V_GFIN = V_GFFN + DEPTH * 16
V_LBIN = V_GFIN + 16
V_LCW = V_LBIN + 32
V_LCB = V_LCW + 64
V_LBR = V_LCB + 16
V_LBI = V_LBR + 16
V_LLAM = V_LBI + 16
V_SCW = V_LLAM + 16
V_SCB = V_SCW + 192
V_SNORM = V_SCB + 48
V_SD = V_SNORM + 32
V_SDTB = V_SD + 32
V_SALOG = V_SDTB + 1
NVEC = V_SALOG + 1


def prologue(C):
    S, I = C.S, C.I
    A = Arena(C)
    S.dma("sp", C.ident.ap, I.ident, writes=[C.ident.k])
    S.op("dve", lambda e: e.tensor_copy(C.identb.ap, C.ident.ap), reads=[C.ident.k], writes=[C.identb.k])
    S.op("dve", lambda e: e.memset(C.onesb.ap, 1.0), writes=[C.onesb.k])
    S.op("dve", lambda e: e.memset(C.onesf.ap, 1.0), writes=[C.onesf.k])
    stg = A.f32(128, nsub=0)
    r0 = 0
    while r0 < NVEC:
        n = min(128, NVEC - r0)
        S.dma("sp", stg.ap[0:n, :], I.vecs[r0:r0 + n, :], writes=[stg.k])
        b = bank(C)
        S.op("pe", lambda e, n=n, b=b: e.transpose(b.ap[:, 0:n], stg.ap[0:n, :], C.ident.ap[0:n, 0:n]),
             reads=[stg.k, C.ident.k], writes=[b.k])
        S.op("dve", lambda e, n=n, b=b, r0=r0: e.tensor_copy(C.vec.ap[:, r0:r0 + n], b.ap[:, 0:n]),
             reads=[b.k], writes=[C.vec.k])
        r0 += n
    cst = A.f32(D)
    S.dma("sp", cst.ap[0:NCOND, :], I.cond, writes=[cst.k])
    S.op("act", lambda e: e.activation(out=cst.ap[0:NCOND, :], in_=cst.ap[0:NCOND, :], func=AF.Silu),
         reads=[cst.k], writes=[cst.k])
    scT = es_persist(C, "scT", KD * NCOND, BF16)
    C.scT = scT
    b = bank(C)
    for k in range(KD):
        S.op("pe", lambda e, k=k: e.transpose(b.ap[:, k * NCOND:(k + 1) * NCOND], cst.ap[0:NCOND, k * 128:(k + 1) * 128],
                                              C.ident.ap[0:NCOND, 0:NCOND]),
             reads=[cst.k, C.ident.k], writes=[b.k])
    S.op("dve", lambda e: e.tensor_copy(scT.ap, b.ap[:, 0:KD * NCOND]), reads=[b.k], writes=[scT.k])
    xin_t = [A.f32(D) for _ in range(2)]
    xo_t = [A.f32(KD * 128, ("p (k t) -> p k t", dict(k=KD))) for _ in range(2)]
    nchunk = (T + 127) // 128
    for c in range(nchunk):
        t0 = c * 128
        n = min(128, T - t0)
        xi = xin_t[c % 2]
        xo = xo_t[c % 2]
        S.dma("sp", xi.ap[0:n, :], I.xin[t0:t0 + n, :], writes=[xi.k])
        for g in range(4):
            b = bank(C)
            for j in range(4):
                k = g * 4 + j
                S.op("pe", lambda e, k=k, j=j, b=b, n=n, xi=xi: e.transpose(
                    b.ap[:, j * 128:j * 128 + n], xi.ap[0:n, k * 128:(k + 1) * 128], C.ident.ap[0:n, 0:n]),
                    reads=[xi.k, C.ident.k], writes=[b.k])
            eng = "act" if g % 2 == 0 else "dve"
            if eng == "act":
                S.op("act", lambda e, g=g, b=b, n=n, xo=xo: e.copy(
                    xo.ap[:, g * 4:(g + 1) * 4, 0:n], b.ap.rearrange("p (j t) -> p j t", j=4)[:, :, 0:n]),
                    reads=[b.k], writes=[xo.k])
            else:
                S.op("dve", lambda e, g=g, b=b, n=n, xo=xo: e.tensor_copy(
                    xo.ap[:, g * 4:(g + 1) * 4, 0:n], b.ap.rearrange("p (j t) -> p j t", j=4)[:, :, 0:n]),
                    reads=[b.k], writes=[xo.k])
        S.dma("sp", C.xres[:, :, t0:t0 + n].rearrange("k p t -> p k t"), xo.ap[:, :, 0:n], reads=[xo.k])
    S.barrier()


def es_persist(C, name, n, dt):
    if not hasattr(C, "ptop"):
        C.ptop = NA
    n32 = n if dt == F32 else (n + 1) // 2
    C.ptop -= n32
    ap = C.arena[:, C.ptop:C.ptop + n32]
    if dt == BF16:
        ap = ap.bitcast(BF16)
    return Tile(ap)


def wslab_load(C, wb, src, n):
    S = C.S
    assert n % 2048 == 0 or n <= 2048
    if n > 2048:
        S.dma("pool", wb.ap[:, 0:n].rearrange("p (a b) -> p a b", b=2048),
              src.rearrange("p (a b) -> p a b", b=2048), writes=[wb.k])
    else:
        S.dma("pool", wb.ap[:, 0:n], src, writes=[wb.k])


def ada_phase(C, l, A):
    S, I = C.S, C.I
    wbs = C.wbs
    modT = C.modT
    b = bank(C)
    for s in range(24):
        wb = wbs[C.wrr % len(wbs)]
        C.wrr += 1
        wslab_load(C, wb, I.w_ada[l, s], KD * 512)
        wv = wb.ap[:, 0:KD * 512].rearrange("p (k c) -> p k c", k=KD)
        for j in range(4):
            n = s * 4 + j
            for k in range(KD):
                S.op("pe", lambda e, n=n, j=j, k=k, wv=wv: e.matmul(
                    b.ap[:, n * NCOND:(n + 1) * NCOND], wv[:, k, j * 128:(j + 1) * 128], C.scT.ap[:, k * NCOND:(k + 1) * NCOND],
                    start=(k == 0), stop=(k == KD - 1)),
                    reads=[wb.k, C.scT.k], writes=[b.k])
    bv = C.vec.ap[:, V_BADA + l * 96:V_BADA + (l + 1) * 96]
    S.op("dve", lambda e: e.tensor_tensor(
        modT.ap, b.ap[:, 0:96 * NCOND].rearrange("p (n c) -> p n c", c=NCOND),
        bv.unsqueeze(2).broadcast_to([128, 96, NCOND]), ALU.add),
        reads=[b.k, C.vec.k], writes=[modT.k])
    for (dst, sc0, g0) in ((C.Amix, 16, V_GMIX + l * 16), (C.Affn, 64, V_GFFN + l * 16)):
        gv = C.vec.ap[:, g0:g0 + 16]
        S.op("dve", lambda e, dst=dst, sc0=sc0, gv=gv: e.scalar_tensor_tensor(
            out=dst.ap, in0=modT.ap[:, sc0:sc0 + 16, :], scalar=1.0,
            in1=gv.unsqueeze(2).broadcast_to([128, 16, NCOND]), op0=ALU.add, op1=ALU.mult),
            reads=[modT.k, C.vec.k], writes=[dst.k])


def mod_cols(ap3, seg, K=KD):
    c0, n, kind, ci, nseq = seg
    if kind == "p":
        return ap3[:, :, 0:1].broadcast_to([128, K, n])
    return ap3[:, :, ci:ci + nseq].unsqueeze(3).broadcast_to([128, K, nseq, LS])


def seg_view(ap3, seg):
    c0, n, kind, ci, nseq = seg
    v = ap3[:, :, c0:c0 + n]
    if kind == "s":
        v = v.rearrange("p k (s t) -> p k s t", t=LS)
    return v


def norm_mod(C, xT, hT, sq, rstd, tmp, nt, t0, Amod, shift0):
    S = C.S
    S.op("act", lambda e: e.activation(out=sq.ap[:, :, 0:nt], in_=xT.ap[:, :, 0:nt], func=AF.Square),
         reads=xT.all, writes=[sq.k])
    for (c0, cn) in splits(nt):
        b = bank(C)
        for k in range(KD):
            S.op("pe", lambda e, k=k, b=b, c0=c0, cn=cn: e.matmul(
                b.ap[:, 0:cn], C.onesb.ap, sq.ap[:, k, c0:c0 + cn], start=(k == 0), stop=(k == KD - 1)),
                reads=[sq.k, C.onesb.k], writes=[b.k])
        S.op("act", lambda e, b=b, c0=c0, cn=cn: e.activation(
            out=rstd.ap[:, c0:c0 + cn], in_=b.ap[:, 0:cn], func=AF.Sqrt, bias=C.epsb.ap, scale=1.0 / D),
            reads=[b.k, C.epsb.k], writes=[rstd.k])
    S.op("dve", lambda e: e.reciprocal(rstd.ap[:, 0:nt], rstd.ap[:, 0:nt]), reads=[rstd.k], writes=[rstd.k])
    Bmod = C.modT.ap[:, shift0:shift0 + 16, :]
    HK = KD // 2
    for kh in range(2):
        ks = slice(kh * HK, (kh + 1) * HK)
        for seg in col_segments(t0, nt):
            c0, n, kind, ci, nseq = seg
            if kind == "p":
                rv = rstd.ap[:, c0:c0 + n].unsqueeze(1).broadcast_to([128, HK, n])
            else:
                rv = rstd.ap[:, c0:c0 + n].rearrange("p (s t) -> p s t", t=LS).unsqueeze(1).broadcast_to([128, HK, nseq, LS])
            tv = seg_view(tmp.ap, seg)
            S.op("dve", lambda e, seg=seg, rv=rv, ks=ks, tv=tv: e.tensor_tensor(tv, seg_view(xT.ap[:, ks], seg), rv, ALU.mult),
                 reads=xT.all + [rstd.k], writes=[tmp.k])
            S.op("dve", lambda e, seg=seg, ks=ks, tv=tv: e.tensor_tensor(tv, tv, mod_cols(Amod.ap[:, ks], seg, HK), ALU.mult),
                 reads=[tmp.k, Amod.k], writes=[tmp.k])
            S.op("dve", lambda e, seg=seg, ks=ks, tv=tv: e.tensor_tensor(seg_view(hT.ap[:, ks], seg), tv, mod_cols(Bmod[:, ks], seg, HK), ALU.add),
                 reads=[tmp.k, C.modT.k], writes=[hT.k])


def resid_update(C, xT, k, bk, c0, cn, t0, gate0):
    S = C.S
    for seg in col_segments(t0 + c0, cn):
        s0, n, kind, ci, nseq = seg
        lo = c0 + s0
        if kind == "p":
            S.op("dve", lambda e, lo=lo, n=n, s0=s0: e.scalar_tensor_tensor(
                out=xT.ap[:, k, lo:lo + n], in0=bk.ap[:, s0:s0 + n], scalar=C.modT.ap[:, gate0 + k, 0:1],
                in1=xT.ap[:, k, lo:lo + n], op0=ALU.mult, op1=ALU.add),
                reads=[bk.k, C.modT.k, xT.sub[k]], writes=[xT.sub[k]])
        else:
            for q in range(nseq):
                S.op("dve", lambda e, lo=lo, s0=s0, q=q, ci=ci: e.scalar_tensor_tensor(
                    out=xT.ap[:, k, lo + q * LS:lo + (q + 1) * LS], in0=bk.ap[:, s0 + q * LS:s0 + (q + 1) * LS],
                    scalar=C.modT.ap[:, gate0 + k, ci + q:ci + q + 1],
                    in1=xT.ap[:, k, lo + q * LS:lo + (q + 1) * LS], op0=ALU.mult, op1=ALU.add),
                    reads=[bk.k, C.modT.k, xT.sub[k]], writes=[xT.sub[k]])


def ffn_phase(C, l, A, mixer_out=None):
    S, I = C.S, C.I
    xT = A.f32(KD * TT, ("p (k t) -> p k t", dict(k=KD)), nsub=KD)
    sq = A.bf16(KD * TT, ("p (k t) -> p k t", dict(k=KD)))
    hT = A.bf16(KD * TT, ("p (k t) -> p k t", dict(k=KD)))
    tmp = A.f32((KD // 2) * TT, ("p (k t) -> p k t", dict(k=KD // 2)))
    uT = A.bf16(64 * TT, ("p (k t) -> p k t", dict(k=64)), nsub=64)
    rstd = A.f32(TT)
    rl = [A.f32(512) for _ in range(2)]
    wbs = C.wbs
    for (t0, nt) in TILES:
        sp = splits(nt)
        S.dma("sp", xT.ap[:, :, 0:nt], C.xres[:, :, t0:t0 + nt].rearrange("k p t -> p k t"), writes=xT.all)
        if mixer_out is not None:
            mixer_out(C, l, xT, uT, t0, nt)
        norm_mod(C, xT, hT, sq, rstd, tmp, nt, t0, C.Affn, 48)
        for s in range(16):
            wb = wbs[C.wrr % len(wbs)]
            C.wrr += 1
            wslab_load(C, wb, I.w_ff1[l, s], KD * 512)
            wv = wb.ap[:, 0:KD * 512].rearrange("p (k c) -> p k c", k=KD)
            for j in range(4):
                n = s * 4 + j
                for (c0, cn) in sp:
                    b = bank(C)
                    for k in range(KD):
                        S.op("pe", lambda e, b=b, k=k, j=j, c0=c0, cn=cn, wv=wv: e.matmul(
                            b.ap[:, 0:cn], wv[:, k, j * 128:(j + 1) * 128], hT.ap[:, k, c0:c0 + cn],
                            start=(k == 0), stop=(k == KD - 1)),
                            reads=[wb.k, hT.k], writes=[b.k])
                    r = rl[C.rrr % 2]
                    C.rrr += 1
                    S.op("act", lambda e, b=b, r=r, cn=cn: e.activation(out=r.ap[:, 0:cn], in_=b.ap[:, 0:cn], func=AF.Relu),
                         reads=[b.k], writes=[r.k])
                    S.op("dve", lambda e, r=r, n=n, c0=c0, cn=cn: e.tensor_tensor(
                        uT.ap[:, n, c0:c0 + cn], r.ap[:, 0:cn], r.ap[:, 0:cn], ALU.mult),
                        reads=[r.k], writes=[uT.sub[n]])
        for s in range(16):
            wb = wbs[C.wrr % len(wbs)]
            C.wrr += 1
            wslab_load(C, wb, I.w_ff2[l, s], 64 * 128)
            wv = wb.ap[:, 0:64 * 128].rearrange("p (k c) -> p k c", k=64)
            for (c0, cn) in sp:
                b = bank(C)
                for n in range(64):
                    S.op("pe", lambda e, b=b, n=n, c0=c0, cn=cn, wv=wv: e.matmul(
                        b.ap[:, 0:cn], wv[:, n, :], uT.ap[:, n, c0:c0 + cn], start=(n == 0), stop=(n == 63)),
                        reads=[wb.k, uT.sub[n]], writes=[b.k])
                resid_update(C, xT, s, b, c0, cn, t0, 80)
        S.dma("sp", C.xres[:, :, t0:t0 + nt].rearrange("k p t -> p k t"), xT.ap[:, :, 0:nt], reads=xT.all)


TSPL = splits(T)


def hT_all_phase(C, l, A):
    S = C.S
    hTa = A.bf16(KD * T, ("p (k t) -> p k t", dict(k=KD)))
    mark = A.off
    xT = A.f32(KD * TT, ("p (k t) -> p k t", dict(k=KD)))
    sq = A.bf16(KD * TT, ("p (k t) -> p k t", dict(k=KD)))
    tmp = A.f32((KD // 2) * TT, ("p (k t) -> p k t", dict(k=KD // 2)))
    rstd = A.f32(TT)
    for (t0, nt) in TILES:
        S.dma("sp", xT.ap[:, :, 0:nt], C.xres[:, :, t0:t0 + nt].rearrange("k p t -> p k t"), writes=xT.all)
        hv = Tile(hTa.ap[:, :, t0:t0 + nt], trk=hTa.k)
        norm_mod(C, xT, hv, sq, rstd, tmp, nt, t0, C.Amix, 0)
    S.barrier()
    A.off = mark
    return hTa


def proj_fm(C, hTa, wsrc, nslab, ncols, epi, K=KD, after_chunk=None):
    S = C.S
    npc = ncols // 128
    for s in range(nslab):
        wb = C.wbs[C.wrr % len(C.wbs)]
        C.wrr += 1
        wslab_load(C, wb, wsrc[s], K * ncols)
        wv = wb.ap[:, 0:K * ncols].rearrange("p (k c) -> p k c", k=K)
        for j in range(npc):
            n = s * npc + j
            for (c0, cn) in TSPL:
                b = bank(C)
                for k in range(K):
                    S.op("pe", lambda e, b=b, k=k, j=j, c0=c0, cn=cn, wv=wv: e.matmul(
                        b.ap[:, 0:cn], wv[:, k, j * 128:(j + 1) * 128], hTa.ap[:, k, c0:c0 + cn],
                        start=(k == 0), stop=(k == K - 1)),
                        reads=[wb.k, hTa.k], writes=[b.k])
                epi(n, c0, cn, b)
            if after_chunk is not None:
                after_chunk(n)


def proj_fm2(C, hTa, wsrcA, wsrcB, nslab, ncols, epi2, K=KD):
    S = C.S
    npc = ncols // 128
    for s in range(nslab):
        wA, wB = C.wbs[0], C.wbs[1]
        wslab_load(C, wA, wsrcA[s], K * ncols)
        wslab_load(C, wB, wsrcB[s], K * ncols)
        wvs = [w.ap[:, 0:K * ncols].rearrange("p (k c) -> p k c", k=K) for w in (wA, wB)]
        for j in range(npc):
            n = s * npc + j
            for (c0, cn) in TSPL:
                bs = []
                for wi, (wb, wv) in enumerate(zip((wA, wB), wvs)):
                    b = bank(C)
                    for k in range(K):
                        S.op("pe", lambda e, b=b, k=k, j=j, c0=c0, cn=cn, wv=wv: e.matmul(
                            b.ap[:, 0:cn], wv[:, k, j * 128:(j + 1) * 128], hTa.ap[:, k, c0:c0 + cn],
                            start=(k == 0), stop=(k == K - 1)),
                            reads=[wb.k, hTa.k], writes=[b.k])
                    bs.append(b)
                epi2(n, c0, cn, bs[0], bs[1])


def make_mixer_out(Kc, wsrc_fn):
    def f(C, l, xT, uT, t0, nt):
        S = C.S
        oTt = Tile(uT.ap.rearrange("p k t -> p (k t)")[:, 0:Kc * TT].rearrange("p (k t) -> p k t", k=Kc), trk=uT.k, sub=uT.sub)
        S.dma("sp", oTt.ap[:, :, 0:nt], C.oT[0:Kc, :, t0:t0 + nt].rearrange("k p t -> p k t"), writes=oTt.all)
        ncols = WSLAB // Kc
        npc = ncols // 128
        wsrc = wsrc_fn(C)
        for s in range(D // ncols):
            wb = C.wbs[C.wrr % len(C.wbs)]
            C.wrr += 1
            wslab_load(C, wb, wsrc[s], Kc * ncols)
            wv = wb.ap[:, 0:Kc * ncols].rearrange("p (k c) -> p k c", k=Kc)
            for j in range(npc):
                dch = s * npc + j
                for (c0, cn) in splits(nt):
                    b = bank(C)
                    for k in range(Kc):
                        S.op("pe", lambda e, b=b, k=k, j=j, c0=c0, cn=cn, wv=wv: e.matmul(
                            b.ap[:, 0:cn], wv[:, k, j * 128:(j + 1) * 128], oTt.ap[:, k, c0:c0 + cn],
                            start=(k == 0), stop=(k == Kc - 1)),
                            reads=[wb.k] + oTt.all, writes=[b.k])
                    resid_update(C, xT, dch, b, c0, cn, t0, 32)
    return f


def fm_to_tm(C, src_ap, nk, ncol, stage, dsts):
    S = C.S
    for g0 in range(0, nk, 4):
        b = bank(C)
        ng = min(4, nk - g0)
        for j in range(ng):
            k = g0 + j
            S.op("pe", lambda e, k=k, j=j, b=b: e.transpose(b.ap[0:ncol, j * 128:(j + 1) * 128], src_ap[:, k, :], C.ident.ap),
                 reads=stage.src_trk + [C.ident.k], writes=[b.k])
        S.op("dve", lambda e, g0=g0, ng=ng, b=b: e.tensor_copy(stage.ap[0:ncol, g0 * 128:(g0 + ng) * 128], b.ap[0:ncol, 0:ng * 128]),
             reads=[b.k], writes=[stage.k])
    for (dap, r0, nr) in dsts:
        S.dma("sp", dap, stage.ap[r0:r0 + nr, 0:nk * 128], reads=[stage.k])


def lru_phase(C, l, A):
    S, I, O = C.S, C.I, C.O
    hTa = hT_all_phase(C, l, A)
    V = C.vec.ap
    one = C.onesf.ap[:, 0:1]
    pst = A.f32(D)
    S.dma("sp", pst.ap[0:12, :], I.lru_conv_in, writes=[pst.k])
    S.dma("sp", pst.ap[12:16, :], I.lru_h0, writes=[pst.k])
    preT = A.f32(KD * 16, ("p (k r) -> p k r", dict(k=KD)))
    b = bank(C)
    for k in range(KD):
        S.op("pe", lambda e, k=k: e.transpose(b.ap[:, k * 16:(k + 1) * 16], pst.ap[0:16, k * 128:(k + 1) * 128], C.ident.ap[0:16, 0:16]),
             reads=[pst.k, C.ident.k], writes=[b.k])
    S.op("dve", lambda e: e.tensor_copy(preT.ap, b.ap[:, 0:KD * 16].rearrange("p (k r) -> p k r", k=KD)), reads=[b.k], writes=[preT.k])
    cf = A.f32(KD)
    S.op("act", lambda e: e.activation(out=cf.ap, in_=V[:, V_LLAM:V_LLAM + KD], func=AF.Exp, scale=-1.0), reads=[C.vec.k], writes=[cf.k])
    S.op("act", lambda e: e.activation(out=cf.ap, in_=cf.ap, func=AF.Ln, bias=one, scale=1.0), reads=[cf.k, C.onesf.k], writes=[cf.k])
    S.op("dve", lambda e: e.tensor_scalar(cf.ap, cf.ap, -8.0, None, ALU.mult), reads=[cf.k], writes=[cf.k])
    wr = A.bf16(4096, ("p (b i j) -> p b i j", dict(b=8, i=2)))
    wi = A.bf16(4096, ("p (b i j) -> p b i j", dict(b=8, i=2)))
    S.dma("pool", wr.ap.rearrange("p b i j -> p (b i) j"), I.lru_wr.rearrange("p (a j) -> p a j", j=256), writes=[wr.k])
    S.dma("pool", wi.ap.rearrange("p b i j -> p (b i) j"), I.lru_wi.rearrange("p (a j) -> p a j", j=256), writes=[wi.k])
    mark_g = A.off
    gx = A.f32(512)
    g2 = A.f32(512)
    gout = A.bf16(T)
    XW = 3 + LP + NS * (3 + LS)
    def epi_gate(n, c0, cn, bk):
        S.op("act", lambda e: e.activation(out=gx.ap[:, 0:cn], in_=bk.ap[:, 0:cn], func=AF.Identity,
                                           bias=V[:, V_LBIN + n:V_LBIN + n + 1], scale=1.0),
             reads=[bk.k, C.vec.k], writes=[gx.k])
        S.op("dve", lambda e: e.tensor_tensor(g2.ap[:, 0:cn], gx.ap[:, 0:cn], gx.ap[:, 0:cn], ALU.mult), reads=[gx.k], writes=[g2.k])
        S.op("dve", lambda e: e.tensor_scalar(g2.ap[:, 0:cn], g2.ap[:, 0:cn], 0.044715, 1.0, ALU.mult, ALU.add), reads=[g2.k], writes=[g2.k])
        S.op("dve", lambda e: e.tensor_tensor(g2.ap[:, 0:cn], g2.ap[:, 0:cn], gx.ap[:, 0:cn], ALU.mult), reads=[g2.k, gx.k], writes=[g2.k])
        S.op("act", lambda e: e.activation(out=g2.ap[:, 0:cn], in_=g2.ap[:, 0:cn], func=AF.Sigmoid, scale=1.5957691216057308),
             reads=[g2.k], writes=[g2.k])
        S.op("dve", lambda e: e.tensor_tensor(gout.ap[:, c0:c0 + cn], g2.ap[:, 0:cn], gx.ap[:, 0:cn], ALU.mult),
             reads=[g2.k, gx.k], writes=[gout.k])

    def after_gate(n):
        S.dma("sp", C.gT[n], gout.ap, reads=[gout.k])

    proj_fm(C, hTa, I.lru_win[0:4], 4, 512, epi_gate, after_chunk=after_gate)

    S.barrier()
    A.off = mark_g
    xpre = A.f32(2 * XW, ("p (c t) -> p c t", dict(c=2)))
    xc = A.f32(2 * T, ("p (c t) -> p c t", dict(c=2)))
    xcb = A.bf16(2 * T, ("p (c t) -> p c t", dict(c=2)))
    rr = A.f32(T)
    ii = A.f32(T)
    gt = A.bf16(2 * T, ("p (c t) -> p c t", dict(c=2)))
    stT = A.f32(KD * 20, ("p (k r) -> p k r", dict(k=KD)))
    S.op("dve", lambda e: e.memset(xpre.ap[:, :, 0:3], 0.0), writes=[xpre.k])

    def samp(ap2, w):
        return ap2.rearrange("p (s t) -> p s t", t=w)

    def epi_xb(n, c0, cn, bk):
        kk = n - 16
        c = kk % 2
        bias = V[:, V_LBIN + n:V_LBIN + n + 1]
        if c0 < LP:
            S.op("act", lambda e: e.activation(out=xpre.ap[:, c, 3 + c0:3 + c0 + cn], in_=bk.ap[:, 0:cn], func=AF.Identity, bias=bias, scale=1.0),
                 reads=[bk.k, C.vec.k], writes=[xpre.k])
        else:
            S.op("act", lambda e: e.activation(out=samp(xpre.ap[:, c, 3 + LP:XW], 3 + LS)[:, :, 3:3 + LS], in_=samp(bk.ap[:, 0:TS], LS),
                                               func=AF.Identity, bias=bias, scale=1.0),
                 reads=[bk.k, C.vec.k], writes=[xpre.k])

    def after_xb(n):
        kk = n - 16
        if kk % 2 == 0:
            return
        blk = kk // 2
        S.dma("sp", gt.ap, C.gT[2 * blk:2 * blk + 2].rearrange("c p t -> p c t"), writes=[gt.k])
        for c in range(2):
            k = 2 * blk + c
            xs_pre = samp(xpre.ap[:, c, 3 + LP:XW], 3 + LS)
            S.op("dve", lambda e, c=c, k=k, xs_pre=xs_pre: e.tensor_copy(xs_pre[:, :, 0:3], preT.ap[:, k, 0:12].rearrange("p (s r) -> p s r", r=3)),
                 reads=[preT.k], writes=[xpre.k])
            w = [V[:, V_LCW + i * KD + k:V_LCW + i * KD + k + 1] for i in range(4)]
            cb = V[:, V_LCB + k:V_LCB + k + 1]
            for (dst, src) in ((xc.ap[:, c, 0:LP], lambda i, c=c: xpre.ap[:, c, i:i + LP]),
                               (samp(xc.ap[:, c, LP:T], LS), lambda i, xs_pre=xs_pre: xs_pre[:, :, i:i + LS])):
                S.op("dve", lambda e, dst=dst, src=src, w=w, cb=cb: e.tensor_scalar(dst, src(0), w[0], cb, ALU.mult, ALU.add),
                     reads=[xpre.k, C.vec.k], writes=[xc.k])
                for i in range(1, 4):
                    S.op("dve", lambda e, dst=dst, src=src, w=w, i=i: e.scalar_tensor_tensor(out=dst, in0=src(i), scalar=w[i], in1=dst, op0=ALU.mult, op1=ALU.add),
                         reads=[xpre.k, C.vec.k, xc.k], writes=[xc.k])
            S.op("act", lambda e, c=c, k=k: e.copy(stT.ap[:, k, 0:3], xpre.ap[:, c, LP:LP + 3]), reads=[xpre.k], writes=[stT.k])
            S.op("act", lambda e, c=c, k=k, xs_pre=xs_pre: e.copy(stT.ap[:, k, 3:15].rearrange("p (s r) -> p s r", r=3), xs_pre[:, :, LS:LS + 3]),
                 reads=[xpre.k], writes=[stT.k])
        S.op("act", lambda e: e.copy(xcb.ap, xc.ap), reads=[xc.k], writes=[xcb.k])
        for c in range(2):
            k = 2 * blk + c
            for (wt, dstt, bvec) in ((wr, rr, V_LBR), (wi, ii, V_LBI)):
                for (c0, cn) in TSPL:
                    bk = bank(C)
                    for ic in range(2):
                        S.op("pe", lambda e, bk=bk, ic=ic, c=c, c0=c0, cn=cn, wt=wt: e.matmul(
                            bk.ap[:, 0:cn], wt.ap[:, blk, ic, c * 128:(c + 1) * 128], xcb.ap[:, ic, c0:c0 + cn], start=(ic == 0), stop=(ic == 1)),
                            reads=[wt.k, xcb.k], writes=[bk.k])
                    S.op("act", lambda e, bk=bk, c0=c0, cn=cn, dstt=dstt, bvec=bvec, k=k: e.activation(
                        out=dstt.ap[:, c0:c0 + cn], in_=bk.ap[:, 0:cn], func=AF.Sigmoid, bias=V[:, bvec + k:bvec + k + 1], scale=1.0),
                        reads=[bk.k, C.vec.k], writes=[dstt.k])
            S.op("act", lambda e, k=k: e.activation(out=rr.ap, in_=rr.ap, func=AF.Exp, scale=cf.ap[:, k:k + 1]),
                 reads=[rr.k, cf.k], writes=[rr.k])
            S.op("dve", lambda e, c=c: e.tensor_tensor(ii.ap, ii.ap, xc.ap[:, c, :], ALU.mult), reads=[ii.k, xc.k], writes=[ii.k])
            S.op("dve", lambda e, c=c: e.tensor_tensor(xc.ap[:, c, :], rr.ap, rr.ap, ALU.mult), reads=[rr.k, xc.k], writes=[xc.k])
            S.op("act", lambda e, c=c: e.activation(out=xc.ap[:, c, :], in_=xc.ap[:, c, :], func=AF.Sqrt, bias=one, scale=-1.0),
                 reads=[xc.k, C.onesf.k], writes=[xc.k])
            S.op("dve", lambda e, c=c: e.tensor_tensor(ii.ap, ii.ap, xc.ap[:, c, :], ALU.mult), reads=[ii.k, xc.k], writes=[ii.k])
            S.op("dve", lambda e, c=c: e.tensor_tensor_scan(xc.ap[:, c, 0:LP], rr.ap[:, 0:LP], ii.ap[:, 0:LP], 0.0, ALU.mult, ALU.add),
                 reads=[rr.k, ii.k, xc.k], writes=[xc.k])
            for q in range(NS):
                cs = slice(LP + q * LS, LP + (q + 1) * LS)
                S.op("dve", lambda e, c=c, cs=cs, q=q, k=k: e.tensor_tensor_scan(xc.ap[:, c, cs], rr.ap[:, cs], ii.ap[:, cs],
                                                                              preT.ap[:, k, 12 + q:13 + q], ALU.mult, ALU.add),
                     reads=[rr.k, ii.k, xc.k, preT.k], writes=[xc.k])
            S.op("act", lambda e, c=c, k=k: e.copy(stT.ap[:, k, 15:16], xc.ap[:, c, LP - 1:LP]), reads=[xc.k], writes=[stT.k])
            S.op("act", lambda e, c=c, k=k: e.copy(stT.ap[:, k, 16:20], samp(xc.ap[:, c, LP:T], LS)[:, :, LS - 1]), reads=[xc.k], writes=[stT.k])
            S.op("dve", lambda e, c=c: e.tensor_tensor(gt.ap[:, c, :], xc.ap[:, c, :], gt.ap[:, c, :], ALU.mult), reads=[xc.k, gt.k], writes=[gt.k])
        S.dma("sp", C.oT[2 * blk:2 * blk + 2].rearrange("c p t -> p c t"), gt.ap, reads=[gt.k])

    proj_fm(C, hTa, I.lru_win[4:8], 4, 512, lambda n, c0, cn, bk: epi_xb(n + 16, c0, cn, bk), after_chunk=lambda n: after_xb(n + 16))
    stage = pst
    stage.src_trk = [stT.k]
    fm_to_tm(C, stT.ap, KD, 20, stage, [(O.lru_conv_p, 0, 3), (O.lru_conv_s, 3, 12), (O.lru_p, 15, 1), (O.lru_s, 16, 4)])
    S.barrier()


def attn_unit(C, B_, R, nB, hd, q_aps, q_trk, sgroups, pvblocks, mask_ap, mask_trk, scale, sink_ap, out_cb):
    S = C.S
    NK = max(off + max(n, mn) for (off, n, _, _, mn) in sgroups)
    NKP = 256 if nB > 1 else ((NK + 511) // 512) * 512
    nb = (nB * NKP + 511) // 512
    sc = mbank(C, nb)
    scv = sc.ap[0:R, 0:nB * NKP].rearrange("p (b n) -> p b n", b=nB)
    for b in range(nB):
        for (off, n, kf, ktrk, mn) in sgroups:
            bt = sc.banks[(b * NKP + off) // 512].k
            S.op("pe", lambda e, b=b, off=off, n=n, kf=kf: e.matmul(scv[:, b, off:off + n], q_aps[b], kf(b), start=True, stop=False),
                 reads=q_trk + ktrk, writes=[bt])
            S.op("pe", lambda e, b=b, off=off, mn=mn: e.matmul(scv[:, b, off:off + mn], C.identb.ap[:, 0:R], mask_ap[:, off:off + mn], start=False, stop=True),
                 reads=[C.identb.k] + mask_trk, writes=[bt])
    mx, negm, rs, es, lse = B_.mx, B_.negm, B_.rs, B_.es, B_.lse
    S.op("dve", lambda e: e.tensor_reduce(mx.ap[0:R, 0:nB], scv[:, :, 0:NK], AX.X, ALU.max), reads=sc.all, writes=[mx.k])
    S.op("dve", lambda e: e.tensor_scalar(mx.ap[0:R, 0:nB], mx.ap[0:R, 0:nB], scale, None, ALU.mult), reads=[mx.k], writes=[mx.k])
    if sink_ap is not None:
        S.op("dve", lambda e: e.tensor_tensor(mx.ap[0:R, 0:nB], mx.ap[0:R, 0:nB], sink_ap, ALU.max), reads=[mx.k, B_.sink_trk], writes=[mx.k])
    S.op("dve", lambda e: e.tensor_scalar(negm.ap[0:R, 0:nB], mx.ap[0:R, 0:nB], -1.0, None, ALU.mult), reads=[mx.k], writes=[negm.k])
    S.op("dve", lambda e: e.memset(rs.ap[0:R, 0:nB], 0.0), writes=[rs.k])
    pb = B_.pb
    pbv = pb.ap[0:R, 0:nB * NK].rearrange("p (b n) -> p b n", b=nB)
    for b in range(nB):
        S.op("act", lambda e, b=b: e.activation(out=pbv[:, b, :], in_=scv[:, b, 0:NK], func=AF.Exp, bias=negm.ap[0:R, b:b + 1], scale=scale,
                                                accum_out=rs.ap[0:R, b:b + 1]),
             reads=sc.all + [negm.k, rs.k], writes=[pb.k, rs.k])
    if sink_ap is not None:
        S.op("dve", lambda e: e.tensor_tensor(es.ap[0:R, 0:nB], sink_ap, mx.ap[0:R, 0:nB], ALU.subtract), reads=[mx.k, B_.sink_trk], writes=[es.k])
        S.op("act", lambda e: e.activation(out=es.ap[0:R, 0:nB], in_=es.ap[0:R, 0:nB], func=AF.Exp), reads=[es.k], writes=[es.k])
        S.op("dve", lambda e: e.tensor_tensor(rs.ap[0:R, 0:nB], rs.ap[0:R, 0:nB], es.ap[0:R, 0:nB], ALU.add), reads=[rs.k, es.k], writes=[rs.k])
    S.op("act", lambda e: e.activation(out=lse.ap[0:R, 0:nB], in_=rs.ap[0:R, 0:nB], func=AF.Ln), reads=[rs.k], writes=[lse.k])
    S.op("dve", lambda e: e.tensor_tensor(lse.ap[0:R, 0:nB], lse.ap[0:R, 0:nB], mx.ap[0:R, 0:nB], ALU.add), reads=[lse.k, mx.k], writes=[lse.k])
    S.op("dve", lambda e: e.reciprocal(rs.ap[0:R, 0:nB], rs.ap[0:R, 0:nB]), reads=[rs.k], writes=[rs.k])
    S.op("dve", lambda e: e.tensor_tensor(pbv, pbv, rs.ap[0:R, 0:nB].unsqueeze(2).broadcast_to([R, nB, NK]), ALU.mult), reads=[pb.k, rs.k], writes=[pb.k])
    pT = B_.pT
    per_bank = 1024 // R
    nidx = nB * len(pvblocks)
    idx = 0
    tb = None
    filled = []
    for b in range(nB):
        for (off, n, vf, vtrk) in pvblocks:
            if idx % per_bank == 0:
                tb = bank(C)
                filled.append((tb, idx))
            j = idx % per_bank
            tbv = tb.ap.bitcast(BF16)
            S.op("pe", lambda e, b=b, off=off, n=n, j=j, tbv=tbv: e.transpose(tbv[0:n, j * R:(j + 1) * R], pbv[:, b, off:off + n], C.identb.ap[0:R, 0:R]),
                 reads=[pb.k, C.identb.k], writes=[tb.k])
            idx += 1
    for ci, (tb, i0) in enumerate(filled):
        cnt = min(per_bank, nidx - i0)
        tbv = tb.ap.bitcast(BF16)
        if ci % 2 == 0:
            S.op("act", lambda e, tbv=tbv, i0=i0, cnt=cnt: e.copy(pT.ap[:, i0 * R:(i0 + cnt) * R], tbv[:, 0:cnt * R]), reads=[tb.k], writes=[pT.k])
        else:
            S.op("dve", lambda e, tbv=tbv, i0=i0, cnt=cnt: e.tensor_copy(pT.ap[:, i0 * R:(i0 + cnt) * R], tbv[:, 0:cnt * R]), reads=[tb.k], writes=[pT.k])
    ob = bank(C)
    idx = 0
    for b in range(nB):
        for bi, (off, n, vf, vtrk) in enumerate(pvblocks):
            S.op("pe", lambda e, b=b, n=n, vf=vf, idx=idx, bi=bi: e.matmul(ob.ap[0:R, b * hd:(b + 1) * hd], pT.ap[0:n, idx * R:(idx + 1) * R], vf(b),
                                                                         start=(bi == 0), stop=(bi == len(pvblocks) - 1)),
                 reads=[pT.k] + vtrk, writes=[ob.k])
            idx += 1
    Osb = B_.Osb
    S.op("act", lambda e: e.copy(Osb.ap[0:R, 0:nB * hd], ob.ap[0:R, 0:nB * hd]), reads=[ob.k], writes=[Osb.k])
    out_cb(Osb, lse)


def attn_phase(C, l, A, cfg):
    S, I, O = C.S, C.I, C.O
    hd, G, HQ = cfg["hd"], cfg["G"], cfg["HQ"]
    NKV = 4
    HPC = 128 // hd
    QC = HQ // HPC
    scale = hd ** -0.5
    hTa = hT_all_phase(C, l, A)
    if C.dbg.get("rope_f32", cfg["name"] == "swa"):
        cosT = A.f32(T)
        sinT = A.f32(T)
        S.dma("sp", cosT.ap, cfg["rope"][0], writes=[cosT.k])
        S.dma("sp", sinT.ap, cfg["rope"][1], writes=[sinT.k])
    else:
        cosT = A.bf16(T)
        sinT = A.bf16(T)
        for (c0, cn) in splits(T, 1040):
            S.dma("pool", cosT.ap[:, c0:c0 + cn], cfg["rope"][0][:, c0:c0 + cn], writes=[cosT.k])
            S.dma("pool", sinT.ap[:, c0:c0 + cn], cfg["rope"][1][:, c0:c0 + cn], writes=[sinT.k])
    perm = A.bf16(128)
    S.dma("pool", perm.ap, cfg["perm"], writes=[perm.k])
    pmask = A.bf16(256)
    S.dma("pool", pmask.ap, I.pmask, writes=[pmask.k])
    smask = A.bf16(max(cfg["wins"]) + 128)
    S.op("dve", lambda e: e.memset(smask.ap, 0.0), writes=[smask.k])
    QT = A.bf16(QC * T, ("p (k t) -> p k t", dict(k=QC)))
    NKC = NKV * HPC
    KT = A.bf16(NKC * T, ("p (k t) -> p k t", dict(k=NKC)))
    VW = NKV * hd
    Vg = A.bf16(16 * VW, ("p (u c) -> p u c", dict(u=16)))
    Vs = A.bf16(NS * VW, ("p (s c) -> p s c", dict(s=NS)))
    S.op("dve", lambda e: e.memset(Vs.ap, 0.0), writes=[Vs.k])
    qb = A.bf16(512)
    t1 = A.f32(512)
    t2 = A.f32(512)
    B_ = Ctx()
    B_.pb = A.bf16(max(2048, max(cfg["wins"]) + 128))
    B_.pT = A.bf16(2048)
    B_.Osb = A.f32(512)
    B_.mx, B_.negm, B_.rs, B_.es, B_.lse = [A.f32(8) for _ in range(5)]
    wmax = max(cfg["wins"])
    kcb = A.bf16(HPC * wmax, ("p (a b) -> p a b", dict(b=128)))
    S.op("dve", lambda e: e.memset(kcb.ap, 0.0), writes=[kcb.k])
    Vc = A.bf16((wmax // 128) * hd, ("p (a b) -> p a b", dict(b=hd)))
    KcT = A.bf16(HPC * wmax)
    qs = A.bf16(32)
    sinkp = A.f32(32)
    sinks = A.f32(8)
    B_.sink_trk = sinkp.k
    if cfg["name"] == "swa":
        srow = A.f32(32)
        orow = A.f32(128)
        S.op("dve", lambda e: e.memset(orow.ap[0:1, :], 1.0), writes=[orow.k])
        S.dma("sp", srow.ap[0:1, :], I.swa_sinks, writes=[srow.k])
        sb_ = bank(C)
        S.op("pe", lambda e: e.matmul(sb_.ap[:, 0:32], orow.ap[0:1, :], srow.ap[0:1, :], start=True, stop=True), reads=[orow.k, srow.k], writes=[sb_.k])
        S.op("dve", lambda e: e.tensor_copy(sinkp.ap, sb_.ap[:, 0:32]), reads=[sb_.k], writes=[sinkp.k])
        S.dma("sp", sinks.ap[0:32, :], I.swa_sinkS, writes=[sinkp.k])
    stop = C.dbg.get("attn_stop", 99)
    if stop <= 1:
        S.barrier()
        return

    def rope_epi(dst, dst_trk, kout):
        def epi(n, c0, cn, bk, b2):
            S.op("act", lambda e: e.activation(out=t1.ap[:, 0:cn], in_=bk.ap[:, 0:cn], func=AF.Identity, scale=1.0), reads=[bk.k], writes=[t1.k])
            S.op("act", lambda e: e.activation(out=t2.ap[:, 0:cn], in_=b2.ap[:, 0:cn], func=AF.Identity, scale=1.0), reads=[b2.k], writes=[t2.k])
            S.op("dve", lambda e: e.tensor_tensor(t1.ap[:, 0:cn], t1.ap[:, 0:cn], cosT.ap[:, c0:c0 + cn], ALU.mult), reads=[t1.k, cosT.k], writes=[t1.k])
            S.op("dve", lambda e: e.tensor_tensor(t2.ap[:, 0:cn], t2.ap[:, 0:cn], sinT.ap[:, c0:c0 + cn], ALU.mult), reads=[t2.k, sinT.k], writes=[t2.k])
            if kout is None:
                S.op("dve", lambda e: e.tensor_tensor(dst[:, n, c0:c0 + cn], t1.ap[:, 0:cn], t2.ap[:, 0:cn], ALU.add), reads=[t1.k, t2.k], writes=[dst_trk])
            else:
                S.op("dve", lambda e: e.tensor_tensor(t1.ap[:, 0:cn], t1.ap[:, 0:cn], t2.ap[:, 0:cn], ALU.add), reads=[t1.k, t2.k], writes=[t1.k])
                S.op("dve", lambda e: e.tensor_copy(dst[:, n, c0:c0 + cn], t1.ap[:, 0:cn]), reads=[t1.k], writes=[dst_trk])
                kout(n, c0, cn)
        return epi

    Ost = A.f32(512)
    for g in range(G):
        d, w = cfg["dils"][g], cfg["wins"][g]
        nun = 16
        kvp, kvs = cfg["kv_out"][g]
        if cfg["name"] == "dil" or g == 0:
            for (c0, cn) in splits(w + 128, 1088):
                S.dma("pool", smask.ap[0:32, c0:c0 + cn], I.smask[g, :, c0:c0 + cn], writes=[smask.k])

        def kout(n, c0, cn, g=g, d=d, w=w, kvp=kvp, kvs=kvs):
            if n % HPC != 0 or C.dbg.get("no_kout"):
                return
            kv_i = n // HPC
            for (o, nn) in splits(cn, 128):
                tok = c0 + o
                if tok < LP and tok < LP - w:
                    continue
                tb = bank(C)
                S.op("pe", lambda e, o=o, nn=nn, tb=tb: e.transpose(tb.ap[0:nn, 0:128], t1.ap[:, o:o + nn], C.ident.ap),
                     reads=[t1.k, C.ident.k], writes=[tb.k])
                S.op("act", lambda e, nn=nn, tb=tb: e.copy(Ost.ap[0:nn, 0:hd], tb.ap[0:nn, 0:hd]), reads=[tb.k], writes=[Ost.k])
                if tok < LP:
                    S.dma("sp", kvp[tok - (LP - w):tok - (LP - w) + nn, kv_i * hd:(kv_i + 1) * hd], Ost.ap[0:nn, 0:hd], reads=[Ost.k])
                else:
                    S.dma("sp", kvs[:, kv_i * hd:(kv_i + 1) * hd], Ost.ap[0:nn, 0:hd], reads=[Ost.k])

        nks = NKC // 4
        proj_fm2(C, hTa, cfg["wk"][g * nks:(g + 1) * nks], cfg["wkp"][g * nks:(g + 1) * nks], nks, 512, rope_epi(KT.ap, KT.k, kout))
        if stop <= 2:
            S.barrier()
            return
        wb = C.wbs[C.wrr % len(C.wbs)]
        C.wrr += 1
        wslab_load(C, wb, cfg["wv"][g], KD * VW)
        wv = wb.ap[:, 0:KD * VW].rearrange("p (k c) -> p k c", k=KD)
        for u in range(nun):
            r, blk = u % d, u // d
            tk0 = r + d * 128 * blk
            tb = bank(C)
            for k in range(KD):
                lhs_ = hTa.ap[:, k, tk0:tk0 + d * 127 + 1:d]
                rhs_ = wv[:, k, :]
                S.op("pe", lambda e, k=k, tb=tb, lhs_=lhs_, rhs_=rhs_: e.matmul(tb.ap[:, 0:VW], lhs_, rhs_, start=(k == 0), stop=(k == KD - 1)),
                     reads=[hTa.k, wb.k], writes=[tb.k])
            S.op("act", lambda e, u=u, tb=tb: e.copy(Vg.ap[:, u, :], tb.ap[:, 0:VW]), reads=[tb.k], writes=[Vg.k])
            if d * 128 * blk >= LP - w:
                S.op("act", lambda e, tb=tb: e.copy(Ost.ap[:, 0:VW], tb.ap[:, 0:VW]), reads=[tb.k], writes=[Ost.k])
                r0 = tk0 - (LP - w)
                S.dma("sp", kvp[r0:r0 + d * 127 + 1:d, VW:2 * VW], Ost.ap[:, 0:VW], reads=[Ost.k])
        for q in range(NS):
            tb = bank(C)
            for k in range(KD):
                rhs_ = wv[:, k, :]
                S.op("pe", lambda e, k=k, tb=tb, q=q, rhs_=rhs_: e.matmul(tb.ap[0:LS, 0:VW], hTa.ap[:, k, LP + q * LS:LP + (q + 1) * LS], rhs_, start=(k == 0), stop=(k == KD - 1)),
                     reads=[hTa.k, wb.k], writes=[tb.k])
            S.op("act", lambda e, q=q, tb=tb: e.copy(Vs.ap[0:LS, q, :], tb.ap[0:LS, 0:VW]), reads=[tb.k], writes=[Vs.k])
            S.op("act", lambda e, tb=tb: e.copy(Ost.ap[0:LS, 0:VW], tb.ap[0:LS, 0:VW]), reads=[tb.k], writes=[Ost.k])
            S.dma("sp", kvs[q * LS:(q + 1) * LS, VW:2 * VW], Ost.ap[0:LS, 0:VW], reads=[Ost.k])
        if stop <= 3:
            S.barrier()
            return
        for kvh in range(NKV):
            proj_fm2(C, hTa, cfg["wq"][g * NKV + kvh:g * NKV + kvh + 1], cfg["wqp"][g * NKV + kvh:g * NKV + kvh + 1], 1, 512, rope_epi(QT.ap, QT.k, None))
            if stop <= 4:
                continue
            for u in range(nun if not C.dbg.get("skip_punits") else 0):
                r, blk = u % d, u // d
                tq0 = r + d * 128 * blk
                qsl = slice(tq0, tq0 + d * 127 + 1, d)
                if blk == 0:
                    ksl, nk, moff = qsl, 128, 128
                else:
                    tk0 = tq0 - d * 128
                    ksl, nk, moff = slice(tk0, tk0 + d * 255 + 1, d), 256, 0
                q_aps, kfs = [], None
                for b in range(HQ):
                    q_aps.append(QT.ap[:, b // HPC, qsl])
                kf = lambda b, ksl=ksl, kvh=kvh: KT.ap[:, kvh * HPC + (b % HPC), ksl]
                sg = [(0, nk, kf, [KT.k], nk)]
                pv = []
                if blk > 0:
                    pv.append((0, 128, lambda b, u=u, d=d, kvh=kvh: Vg.ap[:, u - d, kvh * hd:(kvh + 1) * hd], [Vg.k]))
                pv.append((nk - 128, 128, lambda b, u=u, kvh=kvh: Vg.ap[:, u, kvh * hd:(kvh + 1) * hd], [Vg.k]))
                sink_ap = sinkp.ap[:, kvh * HQ:(kvh + 1) * HQ] if cfg["name"] == "swa" else None

                def ocb(Osb, lse, g=g, kvh=kvh, tq0=tq0, d=d):
                    S.dma("sp", C.Oh[g, tq0:tq0 + d * 127 + 1:d, kvh * HQ * hd:(kvh + 1) * HQ * hd], Osb.ap[:, 0:HQ * hd], reads=[Osb.k])
                    if G > 1:
                        S.dma("sp", C.lseh[g, tq0:tq0 + d * 127 + 1:d, kvh * HQ:(kvh + 1) * HQ], lse.ap[:, 0:HQ], reads=[lse.k])
                attn_unit(C, B_, 128, HQ, hd, q_aps, [QT.k], sg, pv, pmask.ap[:, moff:moff + nk],
                          [pmask.k], scale, sink_ap, ocb)
            for q in range(NS if not C.dbg.get("skip_sunits") else 0):
                nblk = w // 128
                csrc = cfg["cache"][g]
                for par in range(HPC):
                    S.dma("pool", kcb.ap[:, par * nblk:(par + 1) * nblk, par * hd:(par + 1) * hd],
                          csrc[q, :, 0, kvh, :].rearrange("(a p) c -> p a c", p=128), writes=[kcb.k])
                S.dma("pool", Vc.ap[:, 0:nblk, :], csrc[q, :, 1, kvh, :].rearrange("(a p) c -> p a c", p=128), writes=[Vc.k])
                KcTv = KcT.ap[:, 0:HPC * w].rearrange("p (r c) -> p r c", r=HPC)
                for par in range(HPC):
                    for a0 in range(0, nblk, 8):
                        na = min(8, nblk - a0)
                        tb = bank(C)
                        tbv = tb.ap.bitcast(BF16)
                        for a in range(na):
                            S.op("pe", lambda e, a=a, a0=a0, tbv=tbv, par=par, nblk=nblk: e.transpose(tbv[:, a * 128:(a + 1) * 128], kcb.ap[:, par * nblk + a0 + a, :], C.identb.ap) if True else None,
                                 reads=[kcb.k, C.identb.k], writes=[tb.k])
                        S.op("act", lambda e, a0=a0, na=na, tbv=tbv, par=par, KcTv=KcTv: e.copy(KcTv[:, par, a0 * 128:(a0 + na) * 128], tbv[:, 0:na * 128]), reads=[tb.k], writes=[KcT.k])
                for par in range(HPC):
                    tsl = slice(LP + q * LS, LP + (q + 1) * LS)
                    S.op("dve", lambda e, tsl=tsl: e.tensor_copy(qs.ap[:, 0:QC * LS].rearrange("p (c t) -> p c t", t=LS), QT.ap[:, :, tsl]),
                         reads=[QT.k], writes=[qs.k])
                    q_aps = [qs.ap[:, 0:QC * LS]]
                    sg = []
                    for (o, nn) in splits(w, 512):
                        sg.append((o, nn, lambda b, o=o, nn=nn, par=par, KcTv=KcTv: KcTv[:, par, o:o + nn], [KcT.k], nn))
                    sg.append((w, LS, lambda b, par=par, tsl=tsl, kvh=kvh: KT.ap[:, kvh * HPC + par, tsl], [KT.k], 128))
                    pv = [(a * 128, 128, lambda b, a=a: Vc.ap[:, a, :], [Vc.k]) for a in range(nblk)]
                    pv.append((w, 128, lambda b, q=q, kvh=kvh: Vs.ap[:, q, kvh * hd:(kvh + 1) * hd], [Vs.k]))
                    sink_ap = sinks.ap[0:32, kvh * HPC + par:kvh * HPC + par + 1] if cfg["name"] == "swa" else None

                    def ocb(Osb, lse, g=g, kvh=kvh, q=q, par=par):
                        for hh in range(QC):
                            head = kvh * HQ + hh * HPC + par
                            S.dma("sp", C.Oh[g, LP + q * LS:LP + (q + 1) * LS, head * hd:(head + 1) * hd], Osb.ap[hh * LS:(hh + 1) * LS, 0:hd], reads=[Osb.k])
                            if G > 1:
                                S.dma("sp", C.lseh[g, LP + q * LS:LP + (q + 1) * LS, head:head + 1], lse.ap[hh * LS:(hh + 1) * LS, 0:1], reads=[lse.k], allow_slow_non_contiguous=True)
                    attn_unit(C, B_, QC * LS, 1, hd, q_aps, [qs.k], sg, pv, smask.ap[:, 0:w + 128], [smask.k], scale, sink_ap, ocb)
    S.barrier()
    if stop <= 6:
        return
    A.off = 0
    NH = NKV * HQ
    Og = [A.f32(D) for _ in range(G)]
    acc = A.f32(D)
    lt = A.f32(G * 16, ("p (g h) -> p g h", dict(g=G)))
    mxl = A.f32(16)
    sml = A.f32(16)
    obf = A.bf16(D)
    oTs = A.bf16(KD * 128, ("p (k t) -> p k t", dict(k=KD)))
    for (t0, n) in splits(T, 128):
        for g in range(G):
            S.dma("sp", Og[g].ap[0:n, :], C.Oh[g, t0:t0 + n, :], writes=[Og[g].k])
        if G > 1:
            S.dma("sp", lt.ap[0:n], C.lseh[0:G, t0:t0 + n, 0:16].rearrange("g t h -> t g h"), writes=[lt.k])
            S.op("dve", lambda e, n=n: e.tensor_tensor(mxl.ap[0:n], lt.ap[0:n, 0, :], lt.ap[0:n, 1, :], ALU.max), reads=[lt.k], writes=[mxl.k])
            S.op("dve", lambda e, n=n: e.tensor_tensor(mxl.ap[0:n], mxl.ap[0:n], lt.ap[0:n, 2, :], ALU.max), reads=[lt.k, mxl.k], writes=[mxl.k])
            S.op("dve", lambda e, n=n: e.tensor_tensor(lt.ap[0:n], lt.ap[0:n], mxl.ap[0:n].unsqueeze(1).broadcast_to([n, G, 16]), ALU.subtract), reads=[lt.k, mxl.k], writes=[lt.k])
            S.op("act", lambda e, n=n: e.activation(out=lt.ap[0:n], in_=lt.ap[0:n], func=AF.Exp), reads=[lt.k], writes=[lt.k])
            S.op("dve", lambda e, n=n: e.tensor_tensor(sml.ap[0:n], lt.ap[0:n, 0, :], lt.ap[0:n, 1, :], ALU.add), reads=[lt.k], writes=[sml.k])
            S.op("dve", lambda e, n=n: e.tensor_tensor(sml.ap[0:n], sml.ap[0:n], lt.ap[0:n, 2, :], ALU.add), reads=[lt.k, sml.k], writes=[sml.k])
            S.op("dve", lambda e, n=n: e.reciprocal(sml.ap[0:n], sml.ap[0:n]), reads=[sml.k], writes=[sml.k])
            S.op("dve", lambda e, n=n: e.tensor_tensor(lt.ap[0:n], lt.ap[0:n], sml.ap[0:n].unsqueeze(1).broadcast_to([n, G, 16]), ALU.mult), reads=[lt.k, sml.k], writes=[lt.k])
            for g in range(G):
                ov = Og[g].ap[0:n, :].rearrange("p (h d) -> p h d", h=16)
                wv_ = lt.ap[0:n, g, :].unsqueeze(2).broadcast_to([n, 16, 128])
                S.op("dve", lambda e, ov=ov, wv_=wv_: e.tensor_tensor(ov, ov, wv_, ALU.mult), reads=[Og[g].k, lt.k], writes=[Og[g].k])
            S.op("dve", lambda e, n=n: e.tensor_tensor(acc.ap[0:n], Og[0].ap[0:n], Og[1].ap[0:n], ALU.add), reads=[Og[0].k, Og[1].k], writes=[acc.k])
            S.op("dve", lambda e, n=n: e.tensor_tensor(obf.ap[0:n], acc.ap[0:n], Og[2].ap[0:n], ALU.add), reads=[acc.k, Og[2].k], writes=[obf.k])
        else:
            S.op("act", lambda e, n=n: e.copy(obf.ap[0:n], Og[0].ap[0:n]), reads=[Og[0].k], writes=[obf.k])
        for h2 in range(2):
            tb = bank(C)
            tbv = tb.ap.bitcast(BF16)
            for j in range(8):
                k = h2 * 8 + j
                S.op("pe", lambda e, k=k, j=j, n=n, tbv=tbv: e.transpose(tbv[:, j * 128:j * 128 + n], obf.ap[0:n, k * 128:(k + 1) * 128], C.identb.ap[0:n, 0:n]),
                     reads=[obf.k, C.identb.k], writes=[tb.k])
            src = tbv.rearrange("p (j t) -> p j t", j=8)[:, :, 0:n]
            dst = oTs.ap[:, h2 * 8:(h2 + 1) * 8, 0:n]
            if h2 == 0:
                S.op("act", lambda e, src=src, dst=dst: e.copy(dst, src), reads=[tb.k], writes=[oTs.k])
            else:
                S.op("dve", lambda e, src=src, dst=dst: e.tensor_copy(dst, src), reads=[tb.k], writes=[oTs.k])
        S.dma("sp", C.oT[0:KD, :, t0:t0 + n].rearrange("k p t -> p k t"), oTs.ap[:, :, 0:n], reads=[oTs.k])
    S.barrier()


def ssd_phase(C, l, A):
    S, I, O = C.S, C.I, C.O
    V = C.vec.ap
    one = C.onesf.ap[:, 0:1]
    XW = 3 + LP + NS * (3 + LS)

    def samp(ap2, w):
        return ap2.rearrange("p (s t) -> p s t", t=w)

    dtT = A.f32(T)
    aT = A.f32(T)
    stT = A.f32(48 * 15, ("p (k r) -> p k r", dict(k=48)))
    preT = A.f32(48 * 12, ("p (k r) -> p k r", dict(k=48)))
    Ah = A.f32(1)
    keep = A.off
    hTa = hT_all_phase(C, l, A)
    pst = A.f32(D)
    for pc in range(3):
        S.dma("sp", pst.ap[0:12, :], I.ssd_conv_in[:, pc * D:(pc + 1) * D], writes=[pst.k])
        b = bank(C)
        for k in range(KD):
            S.op("pe", lambda e, k=k, b=b: e.transpose(b.ap[:, k * 12:(k + 1) * 12], pst.ap[0:12, k * 128:(k + 1) * 128], C.ident.ap[0:12, 0:12]),
                 reads=[pst.k, C.ident.k], writes=[b.k])
        S.op("dve", lambda e, pc=pc, b=b: e.tensor_copy(preT.ap[:, pc * KD:(pc + 1) * KD, :], b.ap[:, 0:KD * 12].rearrange("p (k r) -> p k r", k=KD)),
             reads=[b.k], writes=[preT.k])
    tx = A.f32(512)
    ty = A.f32(512)

    def epi_dt(n, c0, cn, bk):
        S.op("act", lambda e: e.activation(out=tx.ap[:, 0:cn], in_=bk.ap[:, 0:cn], func=AF.Identity, bias=V[:, V_SDTB:V_SDTB + 1], scale=1.0),
             reads=[bk.k, C.vec.k], writes=[tx.k])
        S.op("act", lambda e: e.activation(out=ty.ap[:, 0:cn], in_=tx.ap[:, 0:cn], func=AF.Abs), reads=[tx.k], writes=[ty.k])
        S.op("act", lambda e: e.activation(out=ty.ap[:, 0:cn], in_=ty.ap[:, 0:cn], func=AF.Exp, scale=-1.0), reads=[ty.k], writes=[ty.k])
        S.op("act", lambda e: e.activation(out=ty.ap[:, 0:cn], in_=ty.ap[:, 0:cn], func=AF.Ln, bias=one, scale=1.0), reads=[ty.k, C.onesf.k], writes=[ty.k])
        S.op("dve", lambda e: e.tensor_scalar(tx.ap[:, 0:cn], tx.ap[:, 0:cn], 0.0, None, ALU.max), reads=[tx.k], writes=[tx.k])
        S.op("dve", lambda e: e.tensor_tensor(dtT.ap[:, c0:c0 + cn], tx.ap[:, 0:cn], ty.ap[:, 0:cn], ALU.add), reads=[tx.k, ty.k], writes=[dtT.k])

    proj_fm(C, hTa, I.ssd_wdt, 1, 128, epi_dt)
    S.op("act", lambda e: e.activation(out=Ah.ap, in_=V[:, V_SALOG:V_SALOG + 1], func=AF.Exp), reads=[C.vec.k], writes=[Ah.k])
    S.op("dve", lambda e: e.tensor_scalar(Ah.ap, Ah.ap, -1.0, None, ALU.mult), reads=[Ah.k], writes=[Ah.k])
    S.op("act", lambda e: e.activation(out=aT.ap, in_=dtT.ap, func=AF.Exp, scale=Ah.ap), reads=[dtT.k, Ah.k], writes=[aT.k])
    zo = A.bf16(T)

    def epi_z(n, c0, cn, bk):
        S.op("act", lambda e: e.activation(out=zo.ap[:, c0:c0 + cn], in_=bk.ap[:, 0:cn], func=AF.Silu), reads=[bk.k], writes=[zo.k])

    proj_fm(C, hTa, I.ssd_wz, 8, 512, epi_z, after_chunk=lambda n: S.dma("sp", C.zT[n], zo.ap, reads=[zo.k]))
    xpre = A.f32(XW)
    xc = A.f32(T)
    xcb = A.bf16(T)
    S.op("dve", lambda e: e.memset(xpre.ap[:, 0:3], 0.0), writes=[xpre.k])

    def epi_x(n, c0, cn, bk):
        if c0 < LP:
            S.op("act", lambda e: e.copy(xpre.ap[:, 3 + c0:3 + c0 + cn], bk.ap[:, 0:cn]), reads=[bk.k], writes=[xpre.k])
        else:
            S.op("act", lambda e: e.copy(samp(xpre.ap[:, 3 + LP:XW], 3 + LS)[:, :, 3:3 + LS], samp(bk.ap[:, 0:TS], LS)), reads=[bk.k], writes=[xpre.k])

    def after_x(n):
        xs_pre = samp(xpre.ap[:, 3 + LP:XW], 3 + LS)
        S.op("dve", lambda e: e.tensor_copy(xs_pre[:, :, 0:3], preT.ap[:, n, :].rearrange("p (s r) -> p s r", r=3)), reads=[preT.k], writes=[xpre.k])
        w = [V[:, V_SCW + i * 48 + n:V_SCW + i * 48 + n + 1] for i in range(4)]
        cb = V[:, V_SCB + n:V_SCB + n + 1]
        for (dst, src) in ((xc.ap[:, 0:LP], lambda i: xpre.ap[:, i:i + LP]), (samp(xc.ap[:, LP:T], LS), lambda i: xs_pre[:, :, i:i + LS])):
            S.op("dve", lambda e, dst=dst, src=src: e.tensor_scalar(dst, src(0), w[0], cb, ALU.mult, ALU.add), reads=[xpre.k, C.vec.k], writes=[xc.k])
            for i in range(1, 4):
                S.op("dve", lambda e, dst=dst, src=src, i=i: e.scalar_tensor_tensor(out=dst, in0=src(i), scalar=w[i], in1=dst, op0=ALU.mult, op1=ALU.add),
                     reads=[xpre.k, C.vec.k, xc.k], writes=[xc.k])
        S.op("act", lambda e: e.copy(stT.ap[:, n, 0:3], xpre.ap[:, LP:LP + 3]), reads=[xpre.k], writes=[stT.k])
        S.op("act", lambda e: e.copy(stT.ap[:, n, 3:15].rearrange("p (s r) -> p s r", r=3), xs_pre[:, :, LS:LS + 3]), reads=[xpre.k], writes=[stT.k])
        if n < 32:
            S.op("act", lambda e: e.activation(out=xc.ap, in_=xc.ap, func=AF.Silu), reads=[xc.k], writes=[xc.k])
            S.dma("sp", C.xsT[n], xc.ap, reads=[xc.k])
        else:
            S.op("act", lambda e: e.activation(out=xcb.ap, in_=xc.ap, func=AF.Silu), reads=[xc.k], writes=[xcb.k])
            S.dma("sp", C.bcT[n - 32], xcb.ap, reads=[xcb.k])

    proj_fm(C, hTa, I.ssd_wx, 12, 512, epi_x, after_chunk=after_x)
    stage = pst
    for pc in range(3):
        stage.src_trk = [stT.k]
        fm_to_tm(C, stT.ap[:, pc * KD:(pc + 1) * KD, :], KD, 15, stage,
                 [(O.ssd_conv_p[:, pc * D:(pc + 1) * D], 0, 3), (O.ssd_conv_s[:, pc * D:(pc + 1) * D], 3, 12)])
    S.barrier()
    A.off = keep
    NCP = 2
    xs = A.f32(NCP * T, ("p (c t) -> p c t", dict(c=NCP)))
    abc = A.f32(NCP * T, ("p (c t) -> p c t", dict(c=NCP)))
    yacc = A.f32(NCP * T, ("p (c t) -> p c t", dict(c=NCP)))
    H0 = A.f32(NCP * NS * 128, ("p (c q s) -> p c q s", dict(c=NCP, q=NS)))
    stF = A.f32(NCP * NCOND * 128, ("p (c q s) -> p c q s", dict(c=NCP, q=NCOND)))
    BTg = A.bf16(T)
    CTg = A.bf16(T)
    Bsb = A.f32(T)
    Csb = A.f32(T)
    d1s = [A.f32(T) for _ in range(2)]
    Hss = [A.f32(T) for _ in range(2)]
    tms = [A.f32(T) for _ in range(2)]
    selt = A.f32(128)
    selb = A.bf16(128)
    zt = A.bf16(NCP * T, ("p (c t) -> p c t", dict(c=NCP)))
    rot = 0
    for ps_ in range(32 // NCP):
        g = (ps_ * NCP) // 4
        c_lo = ps_ * NCP
        S.dma("sp", xs.ap, C.xsT[c_lo:c_lo + NCP].rearrange("c p t -> p c t"), writes=[xs.k])
        for c in range(NCP):
            S.dma("sp", H0.ap[:, c], I.ssd_h0[:, c_lo + c].rearrange("q p s -> p q s"), writes=[H0.k])
        S.dma("sp", BTg.ap, C.bcT[g], writes=[BTg.k])
        S.dma("sp", CTg.ap, C.bcT[8 + g], writes=[CTg.k])
        for c in range(NCP):
            cg = c_lo + c
            for hl in range(2):
                S.op("dve", lambda e, hl=hl, cg=cg: e.tensor_copy(selt.ap[:, hl * 64:(hl + 1) * 64], C.ident.ap[:, 2 * cg + hl:2 * cg + hl + 1].broadcast_to([128, 64])),
                     reads=[C.ident.k], writes=[selt.k])
            for (c0, cn) in TSPL:
                b1 = bank(C)
                S.op("pe", lambda e, b1=b1, c0=c0, cn=cn: e.matmul(b1.ap[:, 0:cn], selt.ap, aT.ap[:, c0:c0 + cn], start=True, stop=True),
                     reads=[selt.k, aT.k], writes=[b1.k])
                S.op("act", lambda e, b1=b1, c=c, c0=c0, cn=cn: e.copy(abc.ap[:, c, c0:c0 + cn], b1.ap[:, 0:cn]), reads=[b1.k], writes=[abc.k])
                b2 = bank(C)
                S.op("pe", lambda e, b2=b2, c0=c0, cn=cn: e.matmul(b2.ap[:, 0:cn], selt.ap, dtT.ap[:, c0:c0 + cn], start=True, stop=True),
                     reads=[selt.k, dtT.k], writes=[b2.k])
                S.op("dve", lambda e, c=c, cg=cg, c0=c0, cn=cn: e.tensor_scalar(yacc.ap[:, c, c0:c0 + cn], xs.ap[:, c, c0:c0 + cn], V[:, V_SD + cg:V_SD + cg + 1], None, ALU.mult),
                     reads=[xs.k, C.vec.k], writes=[yacc.k])
                S.op("dve", lambda e, b2=b2, c=c, c0=c0, cn=cn: e.tensor_tensor(xs.ap[:, c, c0:c0 + cn], xs.ap[:, c, c0:c0 + cn], b2.ap[:, 0:cn], ALU.mult),
                     reads=[xs.k, b2.k, yacc.k], writes=[xs.k])
        for s_ in range(128):
            S.op("dve", lambda e, s_=s_: e.tensor_copy(selb.ap, C.identb.ap[:, s_:s_ + 1].broadcast_to([128, 128])), reads=[C.identb.k], writes=[selb.k])
            for (src, dst) in ((BTg, Bsb), (CTg, Csb)):
                for (c0, cn) in TSPL:
                    b1 = bank(C)
                    S.op("pe", lambda e, b1=b1, c0=c0, cn=cn, src=src: e.matmul(b1.ap[:, 0:cn], selb.ap, src.ap[:, c0:c0 + cn], start=True, stop=True),
                         reads=[selb.k, src.k], writes=[b1.k])
                    S.op("act", lambda e, b1=b1, c0=c0, cn=cn, dst=dst: e.copy(dst.ap[:, c0:c0 + cn], b1.ap[:, 0:cn]), reads=[b1.k], writes=[dst.k])
            for c in range(NCP):
                d1, Hs, tm = d1s[rot % 2], Hss[rot % 2], tms[rot % 2]
                rot += 1
                S.op("dve", lambda e, c=c, d1=d1: e.tensor_tensor(d1.ap, xs.ap[:, c, :], Bsb.ap, ALU.mult), reads=[xs.k, Bsb.k], writes=[d1.k])
                S.op("dve", lambda e, c=c, d1=d1, Hs=Hs: e.tensor_tensor_scan(Hs.ap[:, 0:LP], abc.ap[:, c, 0:LP], d1.ap[:, 0:LP], 0.0, ALU.mult, ALU.add),
                     reads=[abc.k, d1.k], writes=[Hs.k])
                for q in range(NS):
                    cs = slice(LP + q * LS, LP + (q + 1) * LS)
                    S.op("dve", lambda e, c=c, cs=cs, q=q, s_=s_, d1=d1, Hs=Hs: e.tensor_tensor_scan(Hs.ap[:, cs], abc.ap[:, c, cs], d1.ap[:, cs], H0.ap[:, c, q, s_:s_ + 1], ALU.mult, ALU.add),
                         reads=[abc.k, d1.k, H0.k], writes=[Hs.k])
                S.op("act", lambda e, c=c, s_=s_, Hs=Hs: e.copy(stF.ap[:, c, 0, s_:s_ + 1], Hs.ap[:, LP - 1:LP]), reads=[Hs.k], writes=[stF.k])
                S.op("act", lambda e, c=c, s_=s_, Hs=Hs: e.copy(stF.ap[:, c, 1:NCOND, s_], samp(Hs.ap[:, LP:T], LS)[:, :, LS - 1]), reads=[Hs.k], writes=[stF.k])
                S.op("pool", lambda e, Hs=Hs, tm=tm: e.tensor_tensor(tm.ap, Hs.ap, Csb.ap, ALU.mult), reads=[Hs.k, Csb.k], writes=[tm.k])
                S.op("dve", lambda e, c=c, tm=tm: e.tensor_tensor(yacc.ap[:, c, :], yacc.ap[:, c, :], tm.ap, ALU.add), reads=[yacc.k, tm.k], writes=[yacc.k])
        S.dma("sp", zt.ap, C.zT[c_lo:c_lo + NCP].rearrange("c p t -> p c t"), writes=[zt.k])
        S.op("dve", lambda e: e.tensor_tensor(yacc.ap, yacc.ap, zt.ap, ALU.mult), reads=[yacc.k, zt.k], writes=[yacc.k])
        S.dma("sp", C.ygT[c_lo:c_lo + NCP].rearrange("c p t -> p c t"), yacc.ap, reads=[yacc.k])
        for c in range(NCP):
            S.dma("sp", O.ssd_p[c_lo + c], stF.ap[:, c, 0, :], reads=[stF.k])
            S.dma("sp", O.ssd_s[:, c_lo + c].rearrange("q p s -> p q s"), stF.ap[:, c, 1:NCOND, :], reads=[stF.k])
    S.barrier()
    A.off = keep
    yg = A.f32(4 * T, ("p (c t) -> p c t", dict(c=4)))
    sq = A.bf16(4 * T, ("p (c t) -> p c t", dict(c=4)))
    rstd = A.f32(T)
    yo = A.bf16(4 * T, ("p (c t) -> p c t", dict(c=4)))
    for g in range(8):
        S.dma("sp", yg.ap, C.ygT[4 * g:4 * g + 4].rearrange("c p t -> p c t"), writes=[yg.k])
        S.op("act", lambda e: e.activation(out=sq.ap, in_=yg.ap, func=AF.Square), reads=[yg.k], writes=[sq.k])
        for (c0, cn) in TSPL:
            b = bank(C)
            for c in range(4):
                S.op("pe", lambda e, b=b, c=c, c0=c0, cn=cn: e.matmul(b.ap[:, 0:cn], C.onesb.ap, sq.ap[:, c, c0:c0 + cn], start=(c == 0), stop=(c == 3)),
                     reads=[sq.k, C.onesb.k], writes=[b.k])
            S.op("act", lambda e, b=b, c0=c0, cn=cn: e.activation(out=rstd.ap[:, c0:c0 + cn], in_=b.ap[:, 0:cn], func=AF.Sqrt, bias=C.epsb.ap, scale=1.0 / 512),
                 reads=[b.k, C.epsb.k], writes=[rstd.k])
        S.op("dve", lambda e: e.reciprocal(rstd.ap, rstd.ap), reads=[rstd.k], writes=[rstd.k])
        for c in range(4):
            S.op("dve", lambda e, c=c, g=g: e.scalar_tensor_tensor(out=yo.ap[:, c, :], in0=yg.ap[:, c, :], scalar=V[:, V_SNORM + 4 * g + c:V_SNORM + 4 * g + c + 1],
                                                                 in1=rstd.ap, op0=ALU.mult, op1=ALU.mult),
                 reads=[yg.k, rstd.k, C.vec.k], writes=[yo.k])
        S.dma("sp", C.oT[4 * g:4 * g + 4].rearrange("c p t -> p c t"), yo.ap, reads=[yo.k])
    S.barrier()


MIXERS = {}


def layer(C, l):
    S = C.S
    A = Arena(C)
    if l == 0:
        C.modT = es_persist(C, "modT", 96 * NCOND, F32)
        C.modT.ap = C.modT.ap.rearrange("p (n c) -> p n c", c=NCOND)
        C.Amix = es_persist(C, "Amix", 16 * NCOND, F32)
        C.Amix.ap = C.Amix.ap.rearrange("p (n c) -> p n c", c=NCOND)
        C.Affn = es_persist(C, "Affn", 16 * NCOND, F32)
        C.Affn.ap = C.Affn.ap.rearrange("p (n c) -> p n c", c=NCOND)
        C.epsb = es_persist(C, "epsb", 1, F32)
        S.op("dve", lambda e: e.memset(C.epsb.ap, EPS), writes=[C.epsb.k])
        C.wbs = [es_persist(C, "wb%d" % i, WSLAB, BF16) for i in range(2)]
        C.wrr = 0
        C.rrr = 0
    ada_phase(C, l, A)
    S.barrier()
    kind = l % 4
    mo = None
    en = C.dbg.get("mixers", (0, 1, 2, 3))
    I, O = C.I, C.O
    if kind == 3 and 3 in en:
        lru_phase(C, l, A)
        mo = make_mixer_out(16, lambda C: C.I.lru_wout)
    if kind == 1 and 1 in en:
        ssd_phase(C, l, A)
        mo = make_mixer_out(32, lambda C: C.I.ssd_wout)
    if kind == 0 and 0 in en:
        cfg = dict(name="swa", hd=64, G=1, HQ=8, dils=[1], wins=[128], rope=I.rope64, perm=I.perm64,
                   wq=I.swa_wq, wk=I.swa_wk, wv=I.swa_wv, wqp=I.swa_wqp, wkp=I.swa_wkp, cache=[I.swa_cache], kv_out=[(O.swa_kv_p, O.swa_kv_s)])
        attn_phase(C, l, A, cfg)
        mo = make_mixer_out(16, lambda C: C.I.swa_wo)
    if kind == 2 and 2 in en:
        cfg = dict(name="dil", hd=128, G=3, HQ=4, dils=[1, 4, 16], wins=[128, 512, 2048], rope=I.rope128, perm=I.perm128,
                   wq=I.dil_wq, wk=I.dil_wk, wv=I.dil_wv, wqp=I.dil_wqp, wkp=I.dil_wkp, cache=[I.dil_c0, I.dil_c1, I.dil_c2],
                   kv_out=[(O.dil_kv0_p, O.dil_kv0_s), (O.dil_kv1_p, O.dil_kv1_s), (O.dil_kv2_p, O.dil_kv2_s)])
        attn_phase(C, l, A, cfg)
        mo = make_mixer_out(16, lambda C: C.I.dil_wo)
    A.off = 0
    ffn_phase(C, l, A, mo)
    S.barrier()


def final(C):
    S = C.S
    A = Arena(C)
    NT = 512
    xT = A.f32(KD * NT, ("p (k t) -> p k t", dict(k=KD)))
    sq = A.bf16(KD * NT, ("p (k t) -> p k t", dict(k=KD)))
    rstd = A.f32(NT)
    yo = [A.f32(D) for _ in range(2)]
    gv = C.vec.ap[:, V_GFIN:V_GFIN + 16]
    cnt = 0
    for (t0, nt) in splits(T, NT):
        S.dma("sp", xT.ap[:, :, 0:nt], C.xres[:, :, t0:t0 + nt].rearrange("k p t -> p k t"), writes=[xT.k])
        S.op("act", lambda e, nt=nt: e.activation(out=sq.ap[:, :, 0:nt], in_=xT.ap[:, :, 0:nt], func=AF.Square),
             reads=[xT.k], writes=[sq.k])
        b = bank(C)
        for k in range(KD):
            S.op("pe", lambda e, k=k, b=b, nt=nt: e.matmul(b.ap[:, 0:nt], C.onesb.ap, sq.ap[:, k, 0:nt],
                                                        start=(k == 0), stop=(k == KD - 1)),
                 reads=[sq.k, C.onesb.k], writes=[b.k])
        S.op("act", lambda e, b=b, nt=nt: e.activation(out=rstd.ap[:, 0:nt], in_=b.ap[:, 0:nt], func=AF.Sqrt,
                                                       bias=C.epsb.ap, scale=1.0 / D),
             reads=[b.k, C.epsb.k], writes=[rstd.k])
        S.op("dve", lambda e, nt=nt: e.reciprocal(rstd.ap[:, 0:nt], rstd.ap[:, 0:nt]), reads=[rstd.k], writes=[rstd.k])
        S.op("dve", lambda e, nt=nt: e.tensor_tensor(
            xT.ap[:, :, 0:nt], xT.ap[:, :, 0:nt], rstd.ap[:, 0:nt].unsqueeze(1).broadcast_to([128, KD, nt]), ALU.mult),
            reads=[xT.k, rstd.k], writes=[xT.k])
        S.op("dve", lambda e, nt=nt: e.tensor_tensor(
            xT.ap[:, :, 0:nt], xT.ap[:, :, 0:nt], gv.unsqueeze(2).broadcast_to([128, KD, nt]), ALU.mult),
            reads=[xT.k, C.vec.k], writes=[xT.k])
        for (c0, n) in splits(nt, 128):
            y = yo[cnt % 2]
            cnt += 1
            for g in range(4):
                b = bank(C)
                for j in range(4):
                    k = g * 4 + j
                    S.op("pe", lambda e, k=k, j=j, b=b, c0=c0, n=n: e.transpose(
                        b.ap[0:n, j * 128:(j + 1) * 128], xT.ap[:, k, c0:c0 + n], C.ident.ap),
                        reads=[xT.k, C.ident.k], writes=[b.k])
                if g % 2 == 0:
                    S.op("act", lambda e, g=g, b=b, n=n, y=y: e.copy(y.ap[0:n, g * 512:(g + 1) * 512], b.ap[0:n, :]),
                         reads=[b.k], writes=[y.k])
                else:
                    S.op("dve", lambda e, g=g, b=b, n=n, y=y: e.tensor_copy(y.ap[0:n, g * 512:(g + 1) * 512], b.ap[0:n, :]),
                         reads=[b.k], writes=[y.k])
            S.dma("sp", C.O.y[t0 + c0:t0 + c0 + n, :], y.ap[0:n, :], reads=[y.k])


def make_inputs(inp, core):
    b = core % 4
    f = lambda a: np.ascontiguousarray(np.asarray(a, dtype=np.float32))
    m = {}
    xs = f(inp["x_sample"])[core * NS:(core + 1) * NS].reshape(TS, D)
    m["xin"] = np.concatenate([f(inp["x_prompt"])[b], xs], axis=0)
    m["cond"] = np.concatenate([f(inp["c_prompt"])[b:b + 1], f(inp["c_sample"])[core * NS:(core + 1) * NS]], axis=0)
    sl = slice(core * NS, (core + 1) * NS)
    m["swa_cache"] = f(inp["cache_swa_kv"])[0, sl]
    m["dil_c0"] = f(inp["cache_dil_kv_w128"])[0, sl]
    m["dil_c1"] = f(inp["cache_dil_kv_w512"])[0, sl]
    m["dil_c2"] = f(inp["cache_dil_kv_w2048"])[0, sl]
    m["ssd_conv_in"] = f(inp["state_ssd_conv"])[0, sl].reshape(NS * 3, 6144)
    m["ssd_h0"] = f(inp["state_ssd"])[0, sl].reshape(NS, 32, 128, 128)
    m["lru_conv_in"] = f(inp["state_lru_conv"])[0, sl].reshape(NS * 3, D)
    m["lru_h0"] = f(inp["state_lru"])[0, sl]
    return m


def shared_inputs(inp):
    f = lambda a: np.ascontiguousarray(np.asarray(a, dtype=np.float32))
    m = {}
    m["ident"] = np.eye(128, dtype=np.float32)
    vecs = np.zeros((NVEC, 128), np.float32)
    vecs[V_BADA:V_BADA + DEPTH * 96] = f(inp["b_ada"]).reshape(DEPTH * 96, 128)
    vecs[V_GMIX:V_GMIX + DEPTH * 16] = f(inp["g_mix"]).reshape(DEPTH * 16, 128)
    vecs[V_GFFN:V_GFFN + DEPTH * 16] = f(inp["g_ffn"]).reshape(DEPTH * 16, 128)
    vecs[V_GFIN:V_GFIN + 16] = f(inp["g_final"]).reshape(16, 128)
    vecs[V_SCW:V_SCW + 192] = f(inp["ssd_conv_w"]).reshape(192, 128)
    vecs[V_SCB:V_SCB + 48] = f(inp["ssd_conv_b"]).reshape(48, 128)
    vecs[V_SNORM:V_SNORM + 32] = f(inp["ssd_norm"]).reshape(32, 128)
    vecs[V_SD:V_SD + 32] = np.repeat(f(inp["ssd_d"])[0], 64).reshape(32, 128)
    vecs[V_SDTB, 0:64] = f(inp["ssd_dt_bias"])[0]
    vecs[V_SALOG, 0:64] = f(inp["ssd_a_log"])[0]
    vecs[V_LBIN:V_LBIN + 32] = f(inp["lru_b_in"]).reshape(32, 128)
    vecs[V_LCW:V_LCW + 64] = f(inp["lru_conv_w"]).reshape(64, 128)
    vecs[V_LCB:V_LCB + 16] = f(inp["lru_conv_b"]).reshape(16, 128)
    vecs[V_LBR:V_LBR + 16] = f(inp["lru_b_r"]).reshape(16, 128)
    vecs[V_LBI:V_LBI + 16] = f(inp["lru_b_i"]).reshape(16, 128)
    vecs[V_LLAM:V_LLAM + 16] = f(inp["lru_lam"]).reshape(16, 128)
    m["vecs"] = vecs
    pos = np.concatenate([np.arange(LP), PAST + (np.arange(TS) % LS)]).astype(np.float32)
    for hd in (64, 128):
        half = hd // 2
        inv = (10000.0 ** (-np.arange(half, dtype=np.float32) / np.float32(half))).astype(np.float32)
        ang = (pos[None, :] * inv[:, None]).astype(np.float32)
        p = np.arange(128)
        dd = p % hd
        cosT = np.cos(ang)[dd % half].astype(np.float32)
        sinT = np.sin(ang)[dd % half].astype(np.float32) * np.where(dd < half, -1.0, 1.0).astype(np.float32)[:, None]
        m["rope%d" % hd] = np.ascontiguousarray(np.stack([cosT, sinT]))
        perm = np.zeros((128, 128), np.float32)
        partner = (p - dd) + (dd + half) % hd
        perm[partner, p] = 1.0
        m["perm%d" % hd] = perm
    qi = np.arange(128)[:, None]
    si = np.arange(256)[None, :]
    m["pmask"] = np.where((si >= qi) & (si <= qi + 128), 0.0, MASKV).astype(np.float32)
    sm = np.full((3, 32, 2176), MASKV, np.float32)
    tt = (np.arange(32) % LS)[:, None]
    for g, (w, d) in enumerate(((128, 1), (512, 4), (2048, 16))):
        c = np.arange(w)[None, :]
        sm[g, :, :w] = np.where((c % d == tt % d) & (c >= tt), 0.0, MASKV)
        tn = np.arange(LS)[None, :]
        sm[g, :, w:w + LS] = np.where((tn <= tt) & (tn % d == tt % d), 0.0, MASKV)
    m["smask"] = sm
    sk = f(inp["swa_sinks"])[0]
    m["swa_sinks"] = sk.reshape(1, 32)
    rr_ = np.arange(32) // LS
    m["swa_sinkS"] = np.stack([sk[(u // 2) * 8 + 2 * rr_ + (u % 2)] for u in range(8)], axis=1).astype(np.float32)
    wqkv = f(inp["swa_w_qkv"][0])
    m["swa_wq"] = tile_w(wqkv[:, 0:2048], 512).reshape(4, 128, KD * 512)
    wk = wqkv[:, 2048:2304].reshape(D, 4, 64)
    wks = np.zeros((D, 4, 2, 2, 64), np.float32)
    wks[:, :, 0, 0, :] = wk
    wks[:, :, 1, 1, :] = wk
    m["swa_wk"] = tile_w(wks.reshape(D, 1024), 512).reshape(2, 128, KD * 512)
    m["swa_wv"] = tile_w(wqkv[:, 2304:2560], 256).reshape(1, 128, KD * 256)

    def rot_cols(w2, hd_):
        K_, N_ = w2.shape
        w3 = w2.reshape(K_, N_ // hd_, hd_)
        return np.ascontiguousarray(np.concatenate([w3[:, :, hd_ // 2:], w3[:, :, :hd_ // 2]], axis=2)).reshape(K_, N_)

    m["swa_wqp"] = tile_w(rot_cols(wqkv[:, 0:2048], 64), 512).reshape(4, 128, KD * 512)
    m["swa_wkp"] = tile_w(rot_cols(wks.reshape(D, 1024), 64), 512).reshape(2, 128, KD * 512)
    m["swa_wo"] = tile_w(f(inp["swa_w_o"][0]), 512).reshape(4, 128, KD * 512)
    dq = f(inp["dil_w_qkv"][0])
    m["dil_wq"] = tile_w(dq[:, 0:6144], 512).reshape(12, 128, KD * 512)
    m["dil_wk"] = tile_w(dq[:, 6144:7680], 512).reshape(3, 128, KD * 512)
    m["dil_wv"] = tile_w(dq[:, 7680:9216], 512).reshape(3, 128, KD * 512)
    m["dil_wqp"] = tile_w(rot_cols(dq[:, 0:6144], 128), 512).reshape(12, 128, KD * 512)
    m["dil_wkp"] = tile_w(rot_cols(dq[:, 6144:7680], 128), 512).reshape(3, 128, KD * 512)
    m["dil_wo"] = tile_w(f(inp["dil_w_o"][0]), 512).reshape(4, 128, KD * 512)
    win = f(inp["ssd_w_in"][0])
    m["ssd_wz"] = tile_w(win[:, 0:4096], 512).reshape(8, 128, KD * 512)
    m["ssd_wx"] = tile_w(win[:, 4096:10240], 512).reshape(12, 128, KD * 512)
    wdt = np.zeros((D, 128), np.float32)
    wdt[:, 0:64] = win[:, 10240:10304]
    m["ssd_wdt"] = tile_w(wdt, 128).reshape(1, 128, KD * 128)
    m["ssd_wout"] = tile_w(f(inp["ssd_w_out"][0]), 256).reshape(8, 128, 32 * 256)
    m["lru_win"] = tile_w(f(inp["lru_w_in"][0]), 512).reshape(8, 128, KD * 512)
    m["lru_wout"] = tile_w(f(inp["lru_w_out"][0]), 512).reshape(4, 128, KD * 512)
    for nm, key in (("lru_wr", "lru_w_r"), ("lru_wi", "lru_w_i")):
        w = f(inp[key][0])
        m[nm] = np.ascontiguousarray(w.reshape(8, 2, 128, 256).transpose(2, 0, 1, 3)).reshape(128, 4096)
    m["w_ada"] = np.stack([tile_w(f(inp["w_ada"][l]), 512) for l in range(DEPTH)]).reshape(DEPTH, 24, 128, KD * 512)
    m["w_ff1"] = np.stack([tile_w(f(inp["w_ff1"][l]), 512) for l in range(DEPTH)]).reshape(DEPTH, 16, 128, KD * 512)
    m["w_ff2"] = np.stack([tile_w(f(inp["w_ff2"][l]), 128) for l in range(DEPTH)]).reshape(DEPTH, 16, 128, 64 * 128)
    return m


_NC_CACHE = {}


def run(inp, dbg=None, trace=False, cores=None):
    cores = list(range(NCORES)) if cores is None else cores
    key = tuple(sorted((dbg or {}).items()))
    if key not in _NC_CACHE:
        _NC_CACHE[key] = build(dbg)
    nc = _NC_CACHE[key]
    sh = shared_inputs(inp)
    in_maps = []
    for c in cores:
        m = dict(sh)
        m.update(make_inputs(inp, c))
        in_maps.append(m)
    res = run_bass_kernel_spmd(nc, in_maps, core_ids=list(range(len(cores))), trace=trace)
    return res


def kernel(**inp):
    res = run(inp)
    R = res.results
    f32 = np.float32

    def pr(name, shape):
        return np.stack([np.asarray(R[b][name], f32).reshape(shape) for b in range(4)])[None]

    def sa(name, shape):
        return np.concatenate([np.asarray(R[c][name], f32).reshape((NS,) + shape) for c in range(NCORES)], axis=0)[None]

    y_prompt = np.stack([np.asarray(R[b]["y"], f32)[:LP] for b in range(4)])
    y_sample = np.concatenate([np.asarray(R[c]["y"], f32)[LP:].reshape(NS, LS, D) for c in range(NCORES)], axis=0)
    outs = [y_prompt, y_sample,
            pr("swa_kv_p", (128, 2, 4, 64)), sa("swa_kv_s", (LS, 2, 4, 64)),
            pr("ssd_conv_p", (3, 6144)), sa("ssd_conv_s", (3, 6144)),
            pr("ssd_p", (64, 64, 128)), sa("ssd_s", (64, 64, 128))]
    for g, w in enumerate((128, 512, 2048)):
        outs.append(pr("dil_kv%d_p" % g, (w, 2, 4, 128)))
        outs.append(sa("dil_kv%d_s" % g, (LS, 2, 4, 128)))
    outs += [pr("lru_conv_p", (3, D)), sa("lru_conv_s", (3, D)), pr("lru_p", (D,)), sa("lru_s", (D,))]
    return tuple(outs)
```

```python
import contextlib
import math
import numpy as np
import concourse.bass as bass
import concourse.mybir as mybir
from concourse.bass_utils import run_bass_kernel_spmd

F32 = mybir.dt.float32
BF16 = mybir.dt.bfloat16
AF = mybir.ActivationFunctionType
ALU = mybir.AluOpType
AX = mybir.AxisListType

NCORES = 8
D = 2048
KD = D // 128
DFF = 8192
DEPTH = 4
LP = 2048
NS = 4
LS = 8
TS = NS * LS
T = LP + TS
NCOND = 1 + NS
PAST = 16384
EPS = 1e-6
TILES = [(0, 512), (512, 512), (1024, 512), (1536, 544)]
TT = 544
WSLAB = 8192
NEG = -1e30
MASKV = -30000.0
NA = 49 * 1024

DEBUG = {}


class Trk:
    __slots__ = ("w", "r")

    def __init__(self):
        self.w = None
        self.r = {}


class Sched:
    def __init__(self, nc, es, n_dma_sems=56):
        self.nc = nc
        self.es = es
        self.sems = []
        self.engs = {}
        for name in ("pe", "act", "dve", "pool", "sp"):
            sem = es.enter_context(nc.semaphore("s_" + name))
            self.sems.append(sem)
            self.engs[name] = dict(semi=len(self.sems) - 1, cnt=0, prog=[], seen={})
        self.dslots = []
        for i in range(n_dma_sems):
            sem = es.enter_context(nc.semaphore("d_%d" % i))
            self.sems.append(sem)
            self.dslots.append([len(self.sems) - 1, 0])
        self.drr = 0
        self.same_engine_sync = True

    def _waits(self, ename, raw, war):
        E = self.engs[ename]
        need = {}
        for (semi, val) in raw:
            if semi == E["semi"] and (ename == "pe" or not self.same_engine_sync):
                continue
            if need.get(semi, 0) < val:
                need[semi] = val
        for (semi, val) in war:
            if semi == E["semi"]:
                continue
            if need.get(semi, 0) < val:
                need[semi] = val
        out = []
        for semi, val in need.items():
            if E["seen"].get(semi, 0) >= val:
                continue
            E["seen"][semi] = val
            out.append((semi, val))
        return out

    def op(self, ename, fn, reads=(), writes=()):
        E = self.engs[ename]
        raw = [t.w for t in reads if t.w] + [t.w for t in writes if t.w]
        war = [ev for t in writes for ev in t.r.values()]
        waits = self._waits(ename, raw, war)
        E["cnt"] += 1
        ev = (E["semi"], E["cnt"])
        for t in reads:
            t.r[ename] = ev
        for t in writes:
            t.w = ev
            t.r = {}
        E["prog"].append((waits, fn, True))

    def dma(self, qname, out, in_, reads=(), writes=(), **kw):
        Q = self.engs[qname]
        slot = self.dslots[self.drr]
        self.drr = (self.drr + 1) % len(self.dslots)
        raw = [t.w for t in reads if t.w] + [t.w for t in writes if t.w]
        if slot[1] > 0:
            raw.append((slot[0], 16 * slot[1]))
        war = [ev for t in writes for ev in t.r.values()]
        waits = self._waits(qname, raw, war)
        slot[1] += 1
        ev = (slot[0], 16 * slot[1])
        for t in reads:
            t.r[("d", slot[0])] = ev
        for t in writes:
            t.w = ev
            t.r = {}
        sem = self.sems[slot[0]]
        Q["prog"].append((waits, lambda e: e.dma_start(out=out, in_=in_, **kw).then_inc(sem, 16), False))

    def barrier(self):
        for ename, E in self.engs.items():
            raw = [(F["semi"], F["cnt"]) for fn, F in self.engs.items() if fn != ename and F["cnt"] > 0]
            raw += [(s[0], 16 * s[1]) for s in self.dslots if s[1] > 0]
            waits = self._waits(ename, [], raw)
            if waits:
                E["prog"].append((waits, None, False))

    def emit(self):
        sems = self.sems

        def mk(ename):
            E = self.engs[ename]

            def body(e):
                for waits, fn, inc in E["prog"]:
                    for (semi, val) in waits:
                        e.wait_ge(sems[semi], val)
                    if fn is not None:
                        ins = fn(e)
                        if inc:
                            ins.then_inc(sems[E["semi"]], 1)
            return body

        with self.nc.Block() as block:
            block.tensor(mk("pe"))
            block.scalar(mk("act"))
            block.vector(mk("dve"))
            block.gpsimd(mk("pool"))
            block.sync(mk("sp"))


class Tile:
    def __init__(self, ap, nsub=0, trk=None, sub=None):
        self.ap = ap
        self.k = trk if trk is not None else Trk()
        self.sub = sub if sub is not None else [Trk() for _ in range(nsub)]

    def __getitem__(self, key):
        return self.ap[key]

    @property
    def all(self):
        return [self.k] + self.sub


class Ctx:
    pass


def splits(n, m=512):
    out = []
    c = 0
    while c < n:
        out.append((c, min(m, n - c)))
        c += m
    return out


def col_segments(t0, nt):
    segs = []
    t1 = t0 + nt
    if t0 < LP:
        segs.append((0, min(t1, LP) - t0, "p", 0, 1))
    if t1 > LP:
        s0 = max(t0, LP)
        assert (s0 - LP) % LS == 0 and (t1 - LP) % LS == 0
        segs.append((s0 - t0, t1 - s0, "s", 1 + (s0 - LP) // LS, (t1 - s0) // LS))
    return segs


def tile_w(W, ncols):
    K, N = W.shape
    assert K % 128 == 0 and N % ncols == 0
    return np.ascontiguousarray(W.reshape(K // 128, 128, N // ncols, ncols).transpose(2, 1, 0, 3))


def build(dbg=None):
    dbg = dbg or {}
    nc = bass.Bass("TRN2", target_bir_lowering=False)
    es = contextlib.ExitStack()
    C = Ctx()
    C.nc = nc
    C.dbg = dbg
    S = Sched(nc, es)
    C.S = S

    def din(name, shape, dt=F32):
        return nc.dram_tensor(name, list(shape), dt, kind="ExternalInput").ap()

    def dout(name, shape, dt=F32):
        return nc.dram_tensor(name, list(shape), dt, kind="ExternalOutput").ap()

    def dscr(name, shape, dt=F32):
        if dbg.get("dump_" + name):
            return nc.dram_tensor(name, list(shape), dt, kind="ExternalOutput").ap()
        return nc.dram_tensor(name, list(shape), dt, kind="Internal").ap()

    C.din, C.dout, C.dscr = din, dout, dscr

    I = Ctx()
    C.I = I
    I.xin = din("xin", [T, D])
    I.cond = din("cond", [NCOND, D])
    I.ident = din("ident", [128, 128])
    I.vecs = din("vecs", [NVEC, 128])
    I.w_ada = din("w_ada", [DEPTH, 24, 128, KD * 512])
    I.w_ff1 = din("w_ff1", [DEPTH, 16, 128, KD * 512])
    I.w_ff2 = din("w_ff2", [DEPTH, 16, 128, 64 * 128])
    I.lru_conv_in = din("lru_conv_in", [NS * 3, D])
    I.lru_h0 = din("lru_h0", [NS, D])
    I.lru_win = din("lru_win", [8, 128, KD * 512])
    I.lru_wr = din("lru_wr", [128, 4096])
    I.lru_wi = din("lru_wi", [128, 4096])
    I.lru_wout = din("lru_wout", [4, 128, KD * 512])
    I.rope64 = din("rope64", [2, 128, T])
    I.rope128 = din("rope128", [2, 128, T])
    I.perm64 = din("perm64", [128, 128])
    I.perm128 = din("perm128", [128, 128])
    I.pmask = din("pmask", [128, 256])
    I.smask = din("smask", [3, 32, 2176])
    I.swa_sinks = din("swa_sinks", [1, 32])
    I.swa_sinkS = din("swa_sinkS", [32, 8])
    I.swa_cache = din("swa_cache", [NS, 128, 2, 4, 64])
    I.swa_wq = din("swa_wq", [4, 128, KD * 512])
    I.swa_wk = din("swa_wk", [2, 128, KD * 512])
    I.swa_wv = din("swa_wv", [1, 128, KD * 256])
    I.swa_wqp = din("swa_wqp", [4, 128, KD * 512])
    I.swa_wkp = din("swa_wkp", [2, 128, KD * 512])
    I.dil_wqp = din("dil_wqp", [12, 128, KD * 512])
    I.dil_wkp = din("dil_wkp", [3, 128, KD * 512])
    I.swa_wo = din("swa_wo", [4, 128, KD * 512])
    I.dil_c0 = din("dil_c0", [NS, 128, 2, 4, 128])
    I.dil_c1 = din("dil_c1", [NS, 512, 2, 4, 128])
    I.dil_c2 = din("dil_c2", [NS, 2048, 2, 4, 128])
    I.dil_wq = din("dil_wq", [12, 128, KD * 512])
    I.dil_wk = din("dil_wk", [3, 128, KD * 512])
    I.dil_wv = din("dil_wv", [3, 128, KD * 512])
    I.dil_wo = din("dil_wo", [4, 128, KD * 512])
    I.ssd_conv_in = din("ssd_conv_in", [NS * 3, 6144])
    I.ssd_h0 = din("ssd_h0", [NS, 32, 128, 128])
    I.ssd_wz = din("ssd_wz", [8, 128, KD * 512])
    I.ssd_wx = din("ssd_wx", [12, 128, KD * 512])
    I.ssd_wdt = din("ssd_wdt", [1, 128, KD * 128])
    I.ssd_wout = din("ssd_wout", [8, 128, 32 * 256])
    O = Ctx()
    C.O = O
    O.ssd_conv_p = dout("ssd_conv_p", [3, 6144])
    O.ssd_conv_s = dout("ssd_conv_s", [NS * 3, 6144])
    O.ssd_p = dout("ssd_p", [32, 128, 128])
    O.ssd_s = dout("ssd_s", [NS, 32, 128, 128])
    C.zT = dscr("zT", [32, 128, T], BF16)
    C.xsT = dscr("xsT", [32, 128, T], F32)
    C.bcT = dscr("bcT", [16, 128, T], BF16)
    C.ygT = dscr("ygT", [32, 128, T], F32)
    O.swa_kv_p = dout("swa_kv_p", [128, 512])
    O.swa_kv_s = dout("swa_kv_s", [TS, 512])
    for g, w in enumerate((128, 512, 2048)):
        setattr(O, "dil_kv%d_p" % g, dout("dil_kv%d_p" % g, [w, 1024]))
        setattr(O, "dil_kv%d_s" % g, dout("dil_kv%d_s" % g, [TS, 1024]))
    C.Oh = dscr("Oh", [3, T, D])
    C.lseh = dscr("lseh", [3, T, 32])
    O.y = dout("y", [T, D])
    O.lru_conv_p = dout("lru_conv_p", [3, D])
    O.lru_conv_s = dout("lru_conv_s", [NS * 3, D])
    O.lru_p = dout("lru_p", [1, D])
    O.lru_s = dout("lru_s", [NS, D])
    C.oT = dscr("oT", [32, 128, T], BF16)
    C.gT = dscr("gT", [KD, 128, T], BF16)
    C.xres = dscr("xres", [KD, 128, T])

    arena = es.enter_context(nc.sbuf_tensor("arena", [128, NA], F32))
    C.arena = arena
    consts = es.enter_context(nc.sbuf_tensor("consts", [128, 128 + 64 + 128 + NVEC], F32))
    C.ident = Tile(consts[:, 0:128])
    C.identb = Tile(consts[:, 128:192].bitcast(BF16))
    C.onesb = Tile(consts[:, 192:256].bitcast(BF16))
    C.onesf = Tile(consts[:, 256:320])
    C.vec = Tile(consts[:, 320:320 + NVEC])
    psall = es.enter_context(nc.psum_tensor("psall", [128, 4096], F32))
    C.psall = psall
    C.banks = [Tile(psall[:, i * 512:(i + 1) * 512]) for i in range(8)]
    C.brr = 0

    prologue(C)
    for l in range(DEPTH):
        layer(C, l)
    final(C)
    S.barrier()
    S.emit()
    es.close()
    return nc


def bank(C):
    b = C.banks[C.brr]
    C.brr = (C.brr + 1) % 8
    return b


def mbank(C, nb):
    if C.brr + nb > 8:
        C.brr = 0
    i0 = C.brr
    C.brr = (C.brr + nb) % 8
    t = Tile(C.psall[:, i0 * 512:(i0 + nb) * 512], trk=C.banks[i0].k, sub=[C.banks[i].k for i in range(i0 + 1, i0 + nb)])
    t.banks = [C.banks[i] for i in range(i0, i0 + nb)]
    return t


class Arena:
    def __init__(self, C):
        self.C = C
        self.off = 0

    def f32(self, n, shape=None, nsub=0):
        ap = self.C.arena[:, self.off:self.off + n]
        self.off += n
        assert self.off <= getattr(self.C, "ptop", NA), "arena overflow %d" % self.off
        if shape:
            ap = ap.rearrange(shape[0], **shape[1])
        return Tile(ap, nsub)

    def bf16(self, n, shape=None, nsub=0):
        n32 = (n + 1) // 2
        ap = self.C.arena[:, self.off:self.off + n32].bitcast(BF16)
        self.off += n32
        assert self.off <= getattr(self.C, "ptop", NA), "arena overflow %d" % self.off
        if shape:
            ap = ap.rearrange(shape[0], **shape[1])
        return Tile(ap, nsub)


V_BADA = 0
V_GMIX = V_BADA + DEPTH * 96
V_GFFN = V_GMIX + DEPTH * 16
V_GFIN = V_GFFN + DEPTH * 16
V_LBIN = V_GFIN + 16
V_LCW = V_LBIN + 32
V_LCB = V_LCW + 64
V_LBR = V_LCB + 16
V_LBI = V_LBR + 16
V_LLAM = V_LBI + 16
V_SCW = V_LLAM + 16
V_SCB = V_SCW + 192
V_SNORM = V_SCB + 48
V_SD = V_SNORM + 32
V_SDTB = V_SD + 32
V_SALOG = V_SDTB + 1
NVEC = V_SALOG + 1


def prologue(C):
    S, I = C.S, C.I
    A = Arena(C)
    S.dma("sp", C.ident.ap, I.ident, writes=[C.ident.k])
    S.op("dve", lambda e: e.tensor_copy(C.identb.ap, C.ident.ap), reads=[C.ident.k], writes=[C.identb.k])
    S.op("dve", lambda e: e.memset(C.onesb.ap, 1.0), writes=[C.onesb.k])
    S.op("dve", lambda e: e.memset(C.onesf.ap, 1.0), writes=[C.onesf.k])
    stg = A.f32(128, nsub=0)
    r0 = 0
    while r0 < NVEC:
        n = min(128, NVEC - r0)
        S.dma("sp", stg.ap[0:n, :], I.vecs[r0:r0 + n, :], writes=[stg.k])
        b = bank(C)
        S.op("pe", lambda e, n=n, b=b: e.transpose(b.ap[:, 0:n], stg.ap[0:n, :], C.ident.ap[0:n, 0:n]),
             reads=[stg.k, C.ident.k], writes=[b.k])
        S.op("dve", lambda e, n=n, b=b, r0=r0: e.tensor_copy(C.vec.ap[:, r0:r0 + n], b.ap[:, 0:n]),
             reads=[b.k], writes=[C.vec.k])
        r0 += n
    cst = A.f32(D)
    S.dma("sp", cst.ap[0:NCOND, :], I.cond, writes=[cst.k])
    S.op("act", lambda e: e.activation(out=cst.ap[0:NCOND, :], in_=cst.ap[0:NCOND, :], func=AF.Silu),
         reads=[cst.k], writes=[cst.k])
    scT = es_persist(C, "scT", KD * NCOND, BF16)
    C.scT = scT
    b = bank(C)
    for k in range(KD):
        S.op("pe", lambda e, k=k: e.transpose(b.ap[:, k * NCOND:(k + 1) * NCOND], cst.ap[0:NCOND, k * 128:(k + 1) * 128],
                                              C.ident.ap[0:NCOND, 0:NCOND]),
             reads=[cst.k, C.ident.k], writes=[b.k])
    S.op("dve", lambda e: e.tensor_copy(scT.ap, b.ap[:, 0:KD * NCOND]), reads=[b.k], writes=[scT.k])
    xin_t = [A.f32(D) for _ in range(2)]
    xo_t = [A.f32(KD * 128, ("p (k t) -> p k t", dict(k=KD))) for _ in range(2)]
    nchunk = (T + 127) // 128
    for c in range(nchunk):
        t0 = c * 128
        n = min(128, T - t0)
        xi = xin_t[c % 2]
        xo = xo_t[c % 2]
        S.dma("sp", xi.ap[0:n, :], I.xin[t0:t0 + n, :], writes=[xi.k])
        for g in range(4):
            b = bank(C)
            for j in range(4):
                k = g * 4 + j
                S.op("pe", lambda e, k=k, j=j, b=b, n=n, xi=xi: e.transpose(
                    b.ap[:, j * 128:j * 128 + n], xi.ap[0:n, k * 128:(k + 1) * 128], C.ident.ap[0:n, 0:n]),
                    reads=[xi.k, C.ident.k], writes=[b.k])
            eng = "act" if g % 2 == 0 else "dve"
            if eng == "act":
                S.op("act", lambda e, g=g, b=b, n=n, xo=xo: e.copy(
                    xo.ap[:, g * 4:(g + 1) * 4, 0:n], b.ap.rearrange("p (j t) -> p j t", j=4)[:, :, 0:n]),
                    reads=[b.k], writes=[xo.k])
            else:
                S.op("dve", lambda e, g=g, b=b, n=n, xo=xo: e.tensor_copy(
                    xo.ap[:, g * 4:(g + 1) * 4, 0:n], b.ap.rearrange("p (j t) -> p j t", j=4)[:, :, 0:n]),
                    reads=[b.k], writes=[xo.k])
        S.dma("sp", C.xres[:, :, t0:t0 + n].rearrange("k p t -> p k t"), xo.ap[:, :, 0:n], reads=[xo.k])
    S.barrier()


def es_persist(C, name, n, dt):
    if not hasattr(C, "ptop"):
        C.ptop = NA
    n32 = n if dt == F32 else (n + 1) // 2
    C.ptop -= n32
    ap = C.arena[:, C.ptop:C.ptop + n32]
    if dt == BF16:
        ap = ap.bitcast(BF16)
    return Tile(ap)


def wslab_load(C, wb, src, n):
    S = C.S
    assert n % 2048 == 0 or n <= 2048
    if n > 2048:
        S.dma("pool", wb.ap[:, 0:n].rearrange("p (a b) -> p a b", b=2048),
              src.rearrange("p (a b) -> p a b", b=2048), writes=[wb.k])
    else:
        S.dma("pool", wb.ap[:, 0:n], src, writes=[wb.k])


def ada_phase(C, l, A):
    S, I = C.S, C.I
    wbs = C.wbs
    modT = C.modT
    b = bank(C)
    for s in range(24):
        wb = wbs[C.wrr % len(wbs)]
        C.wrr += 1
        wslab_load(C, wb, I.w_ada[l, s], KD * 512)
        wv = wb.ap[:, 0:KD * 512].rearrange("p (k c) -> p k c", k=KD)
        for j in range(4):
            n = s * 4 + j
            for k in range(KD):
                S.op("pe", lambda e, n=n, j=j, k=k, wv=wv: e.matmul(
                    b.ap[:, n * NCOND:(n + 1) * NCOND], wv[:, k, j * 128:(j + 1) * 128], C.scT.ap[:, k * NCOND:(k + 1) * NCOND],
                    start=(k == 0), stop=(k == KD - 1)),
                    reads=[wb.k, C.scT.k], writes=[b.k])
    bv = C.vec.ap[:, V_BADA + l * 96:V_BADA + (l + 1) * 96]
    S.op("dve", lambda e: e.tensor_tensor(
        modT.ap, b.ap[:, 0:96 * NCOND].rearrange("p (n c) -> p n c", c=NCOND),
        bv.unsqueeze(2).broadcast_to([128, 96, NCOND]), ALU.add),
        reads=[b.k, C.vec.k], writes=[modT.k])
    for (dst, sc0, g0) in ((C.Amix, 16, V_GMIX + l * 16), (C.Affn, 64, V_GFFN + l * 16)):
        gv = C.vec.ap[:, g0:g0 + 16]
        S.op("dve", lambda e, dst=dst, sc0=sc0, gv=gv: e.scalar_tensor_tensor(
            out=dst.ap, in0=modT.ap[:, sc0:sc0 + 16, :], scalar=1.0,
            in1=gv.unsqueeze(2).broadcast_to([128, 16, NCOND]), op0=ALU.add, op1=ALU.mult),
            reads=[modT.k, C.vec.k], writes=[dst.k])


def mod_cols(ap3, seg, K=KD):
    c0, n, kind, ci, nseq = seg
    if kind == "p":
        return ap3[:, :, 0:1].broadcast_to([128, K, n])
    return ap3[:, :, ci:ci + nseq].unsqueeze(3).broadcast_to([128, K, nseq, LS])


def seg_view(ap3, seg):
    c0, n, kind, ci, nseq = seg
    v = ap3[:, :, c0:c0 + n]
    if kind == "s":
        v = v.rearrange("p k (s t) -> p k s t", t=LS)
    return v


def norm_mod(C, xT, hT, sq, rstd, tmp, nt, t0, Amod, shift0):
    S = C.S
    S.op("act", lambda e: e.activation(out=sq.ap[:, :, 0:nt], in_=xT.ap[:, :, 0:nt], func=AF.Square),
         reads=xT.all, writes=[sq.k])
    for (c0, cn) in splits(nt):
        b = bank(C)
        for k in range(KD):
            S.op("pe", lambda e, k=k, b=b, c0=c0, cn=cn: e.matmul(
                b.ap[:, 0:cn], C.onesb.ap, sq.ap[:, k, c0:c0 + cn], start=(k == 0), stop=(k == KD - 1)),
                reads=[sq.k, C.onesb.k], writes=[b.k])
        S.op("act", lambda e, b=b, c0=c0, cn=cn: e.activation(
            out=rstd.ap[:, c0:c0 + cn], in_=b.ap[:, 0:cn], func=AF.Sqrt, bias=C.epsb.ap, scale=1.0 / D),
            reads=[b.k, C.epsb.k], writes=[rstd.k])
    S.op("dve", lambda e: e.reciprocal(rstd.ap[:, 0:nt], rstd.ap[:, 0:nt]), reads=[rstd.k], writes=[rstd.k])
    Bmod = C.modT.ap[:, shift0:shift0 + 16, :]
    HK = KD // 2
    for kh in range(2):
        ks = slice(kh * HK, (kh + 1) * HK)
        for seg in col_segments(t0, nt):
            c0, n, kind, ci, nseq = seg
            if kind == "p":
                rv = rstd.ap[:, c0:c0 + n].unsqueeze(1).broadcast_to([128, HK, n])
            else:
                rv = rstd.ap[:, c0:c0 + n].rearrange("p (s t) -> p s t", t=LS).unsqueeze(1).broadcast_to([128, HK, nseq, LS])
            tv = seg_view(tmp.ap, seg)
            S.op("dve", lambda e, seg=seg, rv=rv, ks=ks, tv=tv: e.tensor_tensor(tv, seg_view(xT.ap[:, ks], seg), rv, ALU.mult),
                 reads=xT.all + [rstd.k], writes=[tmp.k])
            S.op("dve", lambda e, seg=seg, ks=ks, tv=tv: e.tensor_tensor(tv, tv, mod_cols(Amod.ap[:, ks], seg, HK), ALU.mult),
                 reads=[tmp.k, Amod.k], writes=[tmp.k])
            S.op("dve", lambda e, seg=seg, ks=ks, tv=tv: e.tensor_tensor(seg_view(hT.ap[:, ks], seg), tv, mod_cols(Bmod[:, ks], seg, HK), ALU.add),
                 reads=[tmp.k, C.modT.k], writes=[hT.k])


def resid_update(C, xT, k, bk, c0, cn, t0, gate0):
    S = C.S
    for seg in col_segments(t0 + c0, cn):
        s0, n, kind, ci, nseq = seg
        lo = c0 + s0
        if kind == "p":
            S.op("dve", lambda e, lo=lo, n=n, s0=s0: e.scalar_tensor_tensor(
                out=xT.ap[:, k, lo:lo + n], in0=bk.ap[:, s0:s0 + n], scalar=C.modT.ap[:, gate0 + k, 0:1],
                in1=xT.ap[:, k, lo:lo + n], op0=ALU.mult, op1=ALU.add),
                reads=[bk.k, C.modT.k, xT.sub[k]], writes=[xT.sub[k]])
        else:
            for q in range(nseq):
                S.op("dve", lambda e, lo=lo, s0=s0, q=q, ci=ci: e.scalar_tensor_tensor(
                    out=xT.ap[:, k, lo + q * LS:lo + (q + 1) * LS], in0=bk.ap[:, s0 + q * LS:s0 + (q + 1) * LS],
                    scalar=C.modT.ap[:, gate0 + k, ci + q:ci + q + 1],
                    in1=xT.ap[:, k, lo + q * LS:lo + (q + 1) * LS], op0=ALU.mult, op1=ALU.add),
                    reads=[bk.k, C.modT.k, xT.sub[k]], writes=[xT.sub[k]])


def ffn_phase(C, l, A, mixer_out=None):
    S, I = C.S, C.I
    xT = A.f32(KD * TT, ("p (k t) -> p k t", dict(k=KD)), nsub=KD)
    sq = A.bf16(KD * TT, ("p (k t) -> p k t", dict(k=KD)))
    hT = A.bf16(KD * TT, ("p (k t) -> p k t", dict(k=KD)))
    tmp = A.f32((KD // 2) * TT, ("p (k t) -> p k t", dict(k=KD // 2)))
    uT = A.bf16(64 * TT, ("p (k t) -> p k t", dict(k=64)), nsub=64)
    rstd = A.f32(TT)
    rl = [A.f32(512) for _ in range(2)]
    wbs = C.wbs
    for (t0, nt) in TILES:
        sp = splits(nt)
        S.dma("sp", xT.ap[:, :, 0:nt], C.xres[:, :, t0:t0 + nt].rearrange("k p t -> p k t"), writes=xT.all)
        if mixer_out is not None:
            mixer_out(C, l, xT, uT, t0, nt)
        norm_mod(C, xT, hT, sq, rstd, tmp, nt, t0, C.Affn, 48)
        for s in range(16):
            wb = wbs[C.wrr % len(wbs)]
            C.wrr += 1
            wslab_load(C, wb, I.w_ff1[l, s], KD * 512)
            wv = wb.ap[:, 0:KD * 512].rearrange("p (k c) -> p k c", k=KD)
            for j in range(4):
                n = s * 4 + j
                for (c0, cn) in sp:
                    b = bank(C)
                    for k in range(KD):
                        S.op("pe", lambda e, b=b, k=k, j=j, c0=c0, cn=cn, wv=wv: e.matmul(
                            b.ap[:, 0:cn], wv[:, k, j * 128:(j + 1) * 128], hT.ap[:, k, c0:c0 + cn],
                            start=(k == 0), stop=(k == KD - 1)),
                            reads=[wb.k, hT.k], writes=[b.k])
                    r = rl[C.rrr % 2]
                    C.rrr += 1
                    S.op("act", lambda e, b=b, r=r, cn=cn: e.activation(out=r.ap[:, 0:cn], in_=b.ap[:, 0:cn], func=AF.Relu),
                         reads=[b.k], writes=[r.k])
                    S.op("dve", lambda e, r=r, n=n, c0=c0, cn=cn: e.tensor_tensor(
                        uT.ap[:, n, c0:c0 + cn], r.ap[:, 0:cn], r.ap[:, 0:cn], ALU.mult),
                        reads=[r.k], writes=[uT.sub[n]])
        for s in range(16):
            wb = wbs[C.wrr % len(wbs)]
            C.wrr += 1
            wslab_load(C, wb, I.w_ff2[l, s], 64 * 128)
            wv = wb.ap[:, 0:64 * 128].rearrange("p (k c) -> p k c", k=64)
            for (c0, cn) in sp:
                b = bank(C)
                for n in range(64):
                    S.op("pe", lambda e, b=b, n=n, c0=c0, cn=cn, wv=wv: e.matmul(
                        b.ap[:, 0:cn], wv[:, n, :], uT.ap[:, n, c0:c0 + cn], start=(n == 0), stop=(n == 63)),
                        reads=[wb.k, uT.sub[n]], writes=[b.k])
                resid_update(C, xT, s, b, c0, cn, t0, 80)
        S.dma("sp", C.xres[:, :, t0:t0 + nt].rearrange("k p t -> p k t"), xT.ap[:, :, 0:nt], reads=xT.all)


TSPL = splits(T)


def hT_all_phase(C, l, A):
    S = C.S
    hTa = A.bf16(KD * T, ("p (k t) -> p k t", dict(k=KD)))
    mark = A.off
    xT = A.f32(KD * TT, ("p (k t) -> p k t", dict(k=KD)))
    sq = A.bf16(KD * TT, ("p (k t) -> p k t", dict(k=KD)))
    tmp = A.f32((KD // 2) * TT, ("p (k t) -> p k t", dict(k=KD // 2)))
    rstd = A.f32(TT)
    for (t0, nt) in TILES:
        S.dma("sp", xT.ap[:, :, 0:nt], C.xres[:, :, t0:t0 + nt].rearrange("k p t -> p k t"), writes=xT.all)
        hv = Tile(hTa.ap[:, :, t0:t0 + nt], trk=hTa.k)
        norm_mod(C, xT, hv, sq, rstd, tmp, nt, t0, C.Amix, 0)
    S.barrier()
    A.off = mark
    return hTa


def proj_fm(C, hTa, wsrc, nslab, ncols, epi, K=KD, after_chunk=None):
    S = C.S
    npc = ncols // 128
    for s in range(nslab):
        wb = C.wbs[C.wrr % len(C.wbs)]
        C.wrr += 1
        wslab_load(C, wb, wsrc[s], K * ncols)
        wv = wb.ap[:, 0:K * ncols].rearrange("p (k c) -> p k c", k=K)
        for j in range(npc):
            n = s * npc + j
            for (c0, cn) in TSPL:
                b = bank(C)
                for k in range(K):
                    S.op("pe", lambda e, b=b, k=k, j=j, c0=c0, cn=cn, wv=wv: e.matmul(
                        b.ap[:, 0:cn], wv[:, k, j * 128:(j + 1) * 128], hTa.ap[:, k, c0:c0 + cn],
                        start=(k == 0), stop=(k == K - 1)),
                        reads=[wb.k, hTa.k], writes=[b.k])
                epi(n, c0, cn, b)
            if after_chunk is not None:
                after_chunk(n)


def proj_fm2(C, hTa, wsrcA, wsrcB, nslab, ncols, epi2, K=KD):
    S = C.S
    npc = ncols // 128
    for s in range(nslab):
        wA, wB = C.wbs[0], C.wbs[1]
        wslab_load(C, wA, wsrcA[s], K * ncols)
        wslab_load(C, wB, wsrcB[s], K * ncols)
        wvs = [w.ap[:, 0:K * ncols].rearrange("p (k c) -> p k c", k=K) for w in (wA, wB)]
        for j in range(npc):
            n = s * npc + j
            for (c0, cn) in TSPL:
                bs = []
                for wi, (wb, wv) in enumerate(zip((wA, wB), wvs)):
                    b = bank(C)
                    for k in range(K):
                        S.op("pe", lambda e, b=b, k=k, j=j, c0=c0, cn=cn, wv=wv: e.matmul(
                            b.ap[:, 0:cn], wv[:, k, j * 128:(j + 1) * 128], hTa.ap[:, k, c0:c0 + cn],
                            start=(k == 0), stop=(k == K - 1)),
                            reads=[wb.k, hTa.k], writes=[b.k])
                    bs.append(b)
                epi2(n, c0, cn, bs[0], bs[1])


def make_mixer_out(Kc, wsrc_fn):
    def f(C, l, xT, uT, t0, nt):
        S = C.S
        oTt = Tile(uT.ap.rearrange("p k t -> p (k t)")[:, 0:Kc * TT].rearrange("p (k t) -> p k t", k=Kc), trk=uT.k, sub=uT.sub)
        S.dma("sp", oTt.ap[:, :, 0:nt], C.oT[0:Kc, :, t0:t0 + nt].rearrange("k p t -> p k t"), writes=oTt.all)
        ncols = WSLAB // Kc
        npc = ncols // 128
        wsrc = wsrc_fn(C)
        for s in range(D // ncols):
            wb = C.wbs[C.wrr % len(C.wbs)]
            C.wrr += 1
            wslab_load(C, wb, wsrc[s], Kc * ncols)
            wv = wb.ap[:, 0:Kc * ncols].rearrange("p (k c) -> p k c", k=Kc)
            for j in range(npc):
                dch = s * npc + j
                for (c0, cn) in splits(nt):
                    b = bank(C)
                    for k in range(Kc):
                        S.op("pe", lambda e, b=b, k=k, j=j, c0=c0, cn=cn, wv=wv: e.matmul(
                            b.ap[:, 0:cn], wv[:, k, j * 128:(j + 1) * 128], oTt.ap[:, k, c0:c0 + cn],
                            start=(k == 0), stop=(k == Kc - 1)),
                            reads=[wb.k] + oTt.all, writes=[b.k])
                    resid_update(C, xT, dch, b, c0, cn, t0, 32)
    return f


def fm_to_tm(C, src_ap, nk, ncol, stage, dsts):
    S = C.S
    for g0 in range(0, nk, 4):
        b = bank(C)
        ng = min(4, nk - g0)
        for j in range(ng):
            k = g0 + j
            S.op("pe", lambda e, k=k, j=j, b=b: e.transpose(b.ap[0:ncol, j * 128:(j + 1) * 128], src_ap[:, k, :], C.ident.ap),
                 reads=stage.src_trk + [C.ident.k], writes=[b.k])
        S.op("dve", lambda e, g0=g0, ng=ng, b=b: e.tensor_copy(stage.ap[0:ncol, g0 * 128:(g0 + ng) * 128], b.ap[0:ncol, 0:ng * 128]),
             reads=[b.k], writes=[stage.k])
    for (dap, r0, nr) in dsts:
        S.dma("sp", dap, stage.ap[r0:r0 + nr, 0:nk * 128], reads=[stage.k])


def lru_phase(C, l, A):
    S, I, O = C.S, C.I, C.O
    hTa = hT_all_phase(C, l, A)
    V = C.vec.ap
    one = C.onesf.ap[:, 0:1]
    pst = A.f32(D)
    S.dma("sp", pst.ap[0:12, :], I.lru_conv_in, writes=[pst.k])
    S.dma("sp", pst.ap[12:16, :], I.lru_h0, writes=[pst.k])
    preT = A.f32(KD * 16, ("p (k r) -> p k r", dict(k=KD)))
    b = bank(C)
    for k in range(KD):
        S.op("pe", lambda e, k=k: e.transpose(b.ap[:, k * 16:(k + 1) * 16], pst.ap[0:16, k * 128:(k + 1) * 128], C.ident.ap[0:16, 0:16]),
             reads=[pst.k, C.ident.k], writes=[b.k])
    S.op("dve", lambda e: e.tensor_copy(preT.ap, b.ap[:, 0:KD * 16].rearrange("p (k r) -> p k r", k=KD)), reads=[b.k], writes=[preT.k])
    cf = A.f32(KD)
    S.op("act", lambda e: e.activation(out=cf.ap, in_=V[:, V_LLAM:V_LLAM + KD], func=AF.Exp, scale=-1.0), reads=[C.vec.k], writes=[cf.k])
    S.op("act", lambda e: e.activation(out=cf.ap, in_=cf.ap, func=AF.Ln, bias=one, scale=1.0), reads=[cf.k, C.onesf.k], writes=[cf.k])
    S.op("dve", lambda e: e.tensor_scalar(cf.ap, cf.ap, -8.0, None, ALU.mult), reads=[cf.k], writes=[cf.k])
    wr = A.bf16(4096, ("p (b i j) -> p b i j", dict(b=8, i=2)))
    wi = A.bf16(4096, ("p (b i j) -> p b i j", dict(b=8, i=2)))
    S.dma("pool", wr.ap.rearrange("p b i j -> p (b i) j"), I.lru_wr.rearrange("p (a j) -> p a j", j=256), writes=[wr.k])
    S.dma("pool", wi.ap.rearrange("p b i j -> p (b i) j"), I.lru_wi.rearrange("p (a j) -> p a j", j=256), writes=[wi.k])
    mark_g = A.off
    gx = A.f32(512)
    g2 = A.f32(512)
    gout = A.bf16(T)
    XW = 3 + LP + NS * (3 + LS)
    def epi_gate(n, c0, cn, bk):
        S.op("act", lambda e: e.activation(out=gx.ap[:, 0:cn], in_=bk.ap[:, 0:cn], func=AF.Identity,
                                           bias=V[:, V_LBIN + n:V_LBIN + n + 1], scale=1.0),
             reads=[bk.k, C.vec.k], writes=[gx.k])
        S.op("dve", lambda e: e.tensor_tensor(g2.ap[:, 0:cn], gx.ap[:, 0:cn], gx.ap[:, 0:cn], ALU.mult), reads=[gx.k], writes=[g2.k])
        S.op("dve", lambda e: e.tensor_scalar(g2.ap[:, 0:cn], g2.ap[:, 0:cn], 0.044715, 1.0, ALU.mult, ALU.add), reads=[g2.k], writes=[g2.k])
        S.op("dve", lambda e: e.tensor_tensor(g2.ap[:, 0:cn], g2.ap[:, 0:cn], gx.ap[:, 0:cn], ALU.mult), reads=[g2.k, gx.k], writes=[g2.k])
        S.op("act", lambda e: e.activation(out=g2.ap[:, 0:cn], in_=g2.ap[:, 0:cn], func=AF.Sigmoid, scale=1.5957691216057308),
             reads=[g2.k], writes=[g2.k])
        S.op("dve", lambda e: e.tensor_tensor(gout.ap[:, c0:c0 + cn], g2.ap[:, 0:cn], gx.ap[:, 0:cn], ALU.mult),
             reads=[g2.k, gx.k], writes=[gout.k])

    def after_gate(n):
        S.dma("sp", C.gT[n], gout.ap, reads=[gout.k])

    proj_fm(C, hTa, I.lru_win[0:4], 4, 512, epi_gate, after_chunk=after_gate)

    S.barrier()
    A.off = mark_g
    xpre = A.f32(2 * XW, ("p (c t) -> p c t", dict(c=2)))
    xc = A.f32(2 * T, ("p (c t) -> p c t", dict(c=2)))
    xcb = A.bf16(2 * T, ("p (c t) -> p c t", dict(c=2)))
    rr = A.f32(T)
    ii = A.f32(T)
    gt = A.bf16(2 * T, ("p (c t) -> p c t", dict(c=2)))
    stT = A.f32(KD * 20, ("p (k r) -> p k r", dict(k=KD)))
    S.op("dve", lambda e: e.memset(xpre.ap[:, :, 0:3], 0.0), writes=[xpre.k])

    def samp(ap2, w):
        return ap2.rearrange("p (s t) -> p s t", t=w)

    def epi_xb(n, c0, cn, bk):
        kk = n - 16
        c = kk % 2
        bias = V[:, V_LBIN + n:V_LBIN + n + 1]
        if c0 < LP:
            S.op("act", lambda e: e.activation(out=xpre.ap[:, c, 3 + c0:3 + c0 + cn], in_=bk.ap[:, 0:cn], func=AF.Identity, bias=bias, scale=1.0),
                 reads=[bk.k, C.vec.k], writes=[xpre.k])
        else:
            S.op("act", lambda e: e.activation(out=samp(xpre.ap[:, c, 3 + LP:XW], 3 + LS)[:, :, 3:3 + LS], in_=samp(bk.ap[:, 0:TS], LS),
                                               func=AF.Identity, bias=bias, scale=1.0),
                 reads=[bk.k, C.vec.k], writes=[xpre.k])

    def after_xb(n):
        kk = n - 16
        if kk % 2 == 0:
            return
        blk = kk // 2
        S.dma("sp", gt.ap, C.gT[2 * blk:2 * blk + 2].rearrange("c p t -> p c t"), writes=[gt.k])
        for c in range(2):
            k = 2 * blk + c
            xs_pre = samp(xpre.ap[:, c, 3 + LP:XW], 3 + LS)
            S.op("dve", lambda e, c=c, k=k, xs_pre=xs_pre: e.tensor_copy(xs_pre[:, :, 0:3], preT.ap[:, k, 0:12].rearrange("p (s r) -> p s r", r=3)),
                 reads=[preT.k], writes=[xpre.k])
            w = [V[:, V_LCW + i * KD + k:V_LCW + i * KD + k + 1] for i in range(4)]
            cb = V[:, V_LCB + k:V_LCB + k + 1]
            for (dst, src) in ((xc.ap[:, c, 0:LP], lambda i, c=c: xpre.ap[:, c, i:i + LP]),
                               (samp(xc.ap[:, c, LP:T], LS), lambda i, xs_pre=xs_pre: xs_pre[:, :, i:i + LS])):
                S.op("dve", lambda e, dst=dst, src=src, w=w, cb=cb: e.tensor_scalar(dst, src(0), w[0], cb, ALU.mult, ALU.add),
                     reads=[xpre.k, C.vec.k], writes=[xc.k])
                for i in range(1, 4):
                    S.op("dve", lambda e, dst=dst, src=src, w=w, i=i: e.scalar_tensor_tensor(out=dst, in0=src(i), scalar=w[i], in1=dst, op0=ALU.mult, op1=ALU.add),
                         reads=[xpre.k, C.vec.k, xc.k], writes=[xc.k])
            S.op("act", lambda e, c=c, k=k: e.copy(stT.ap[:, k, 0:3], xpre.ap[:, c, LP:LP + 3]), reads=[xpre.k], writes=[stT.k])
            S.op("act", lambda e, c=c, k=k, xs_pre=xs_pre: e.copy(stT.ap[:, k, 3:15].rearrange("p (s r) -> p s r", r=3), xs_pre[:, :, LS:LS + 3]),
                 reads=[xpre.k], writes=[stT.k])
        S.op("act", lambda e: e.copy(xcb.ap, xc.ap), reads=[xc.k], writes=[xcb.k])
        for c in range(2):
            k = 2 * blk + c
            for (wt, dstt, bvec) in ((wr, rr, V_LBR), (wi, ii, V_LBI)):
                for (c0, cn) in TSPL:
                    bk = bank(C)
                    for ic in range(2):
                        S.op("pe", lambda e, bk=bk, ic=ic, c=c, c0=c0, cn=cn, wt=wt: e.matmul(
                            bk.ap[:, 0:cn], wt.ap[:, blk, ic, c * 128:(c + 1) * 128], xcb.ap[:, ic, c0:c0 + cn], start=(ic == 0), stop=(ic == 1)),
                            reads=[wt.k, xcb.k], writes=[bk.k])
                    S.op("act", lambda e, bk=bk, c0=c0, cn=cn, dstt=dstt, bvec=bvec, k=k: e.activation(
                        out=dstt.ap[:, c0:c0 + cn], in_=bk.ap[:, 0:cn], func=AF.Sigmoid, bias=V[:, bvec + k:bvec + k + 1], scale=1.0),
                        reads=[bk.k, C.vec.k], writes=[dstt.k])
            S.op("act", lambda e, k=k: e.activation(out=rr.ap, in_=rr.ap, func=AF.Exp, scale=cf.ap[:, k:k + 1]),
                 reads=[rr.k, cf.k], writes=[rr.k])
            S.op("dve", lambda e, c=c: e.tensor_tensor(ii.ap, ii.ap, xc.ap[:, c, :], ALU.mult), reads=[ii.k, xc.k], writes=[ii.k])
            S.op("dve", lambda e, c=c: e.tensor_tensor(xc.ap[:, c, :], rr.ap, rr.ap, ALU.mult), reads=[rr.k, xc.k], writes=[xc.k])
            S.op("act", lambda e, c=c: e.activation(out=xc.ap[:, c, :], in_=xc.ap[:, c, :], func=AF.Sqrt, bias=one, scale=-1.0),
                 reads=[xc.k, C.onesf.k], writes=[xc.k])
            S.op("dve", lambda e, c=c: e.tensor_tensor(ii.ap, ii.ap, xc.ap[:, c, :], ALU.mult), reads=[ii.k, xc.k], writes=[ii.k])
            S.op("dve", lambda e, c=c: e.tensor_tensor_scan(xc.ap[:, c, 0:LP], rr.ap[:, 0:LP], ii.ap[:, 0:LP], 0.0, ALU.mult, ALU.add),
                 reads=[rr.k, ii.k, xc.k], writes=[xc.k])
            for q in range(NS):
                cs = slice(LP + q * LS, LP + (q + 1) * LS)
                S.op("dve", lambda e, c=c, cs=cs, q=q, k=k: e.tensor_tensor_scan(xc.ap[:, c, cs], rr.ap[:, cs], ii.ap[:, cs],
                                                                              preT.ap[:, k, 12 + q:13 + q], ALU.mult, ALU.add),
                     reads=[rr.k, ii.k, xc.k, preT.k], writes=[xc.k])
            S.op("act", lambda e, c=c, k=k: e.copy(stT.ap[:, k, 15:16], xc.ap[:, c, LP - 1:LP]), reads=[xc.k], writes=[stT.k])
            S.op("act", lambda e, c=c, k=k: e.copy(stT.ap[:, k, 16:20], samp(xc.ap[:, c, LP:T], LS)[:, :, LS - 1]), reads=[xc.k], writes=[stT.k])
            S.op("dve", lambda e, c=c: e.tensor_tensor(gt.ap[:, c, :], xc.ap[:, c, :], gt.ap[:, c, :], ALU.mult), reads=[xc.k, gt.k], writes=[gt.k])
        S.dma("sp", C.oT[2 * blk:2 * blk + 2].rearrange("c p t -> p c t"), gt.ap, reads=[gt.k])

    proj_fm(C, hTa, I.lru_win[4:8], 4, 512, lambda n, c0, cn, bk: epi_xb(n + 16, c0, cn, bk), after_chunk=lambda n: after_xb(n + 16))
    stage = pst
    stage.src_trk = [stT.k]
    fm_to_tm(C, stT.ap, KD, 20, stage, [(O.lru_conv_p, 0, 3), (O.lru_conv_s, 3, 12), (O.lru_p, 15, 1), (O.lru_s, 16, 4)])
    S.barrier()


def attn_unit(C, B_, R, nB, hd, q_aps, q_trk, sgroups, pvblocks, mask_ap, mask_trk, scale, sink_ap, out_cb):
    S = C.S
    NK = max(off + max(n, mn) for (off, n, _, _, mn) in sgroups)
    NKP = 256 if nB > 1 else ((NK + 511) // 512) * 512
    nb = (nB * NKP + 511) // 512
    sc = mbank(C, nb)
    scv = sc.ap[0:R, 0:nB * NKP].rearrange("p (b n) -> p b n", b=nB)
    for b in range(nB):
        for (off, n, kf, ktrk, mn) in sgroups:
            bt = sc.banks[(b * NKP + off) // 512].k
            S.op("pe", lambda e, b=b, off=off, n=n, kf=kf: e.matmul(scv[:, b, off:off + n], q_aps[b], kf(b), start=True, stop=False),
                 reads=q_trk + ktrk, writes=[bt])
            S.op("pe", lambda e, b=b, off=off, mn=mn: e.matmul(scv[:, b, off:off + mn], C.identb.ap[:, 0:R], mask_ap[:, off:off + mn], start=False, stop=True),
                 reads=[C.identb.k] + mask_trk, writes=[bt])
    mx, negm, rs, es, lse = B_.mx, B_.negm, B_.rs, B_.es, B_.lse
    S.op("dve", lambda e: e.tensor_reduce(mx.ap[0:R, 0:nB], scv[:, :, 0:NK], AX.X, ALU.max), reads=sc.all, writes=[mx.k])
    S.op("dve", lambda e: e.tensor_scalar(mx.ap[0:R, 0:nB], mx.ap[0:R, 0:nB], scale, None, ALU.mult), reads=[mx.k], writes=[mx.k])
    if sink_ap is not None:
        S.op("dve", lambda e: e.tensor_tensor(mx.ap[0:R, 0:nB], mx.ap[0:R, 0:nB], sink_ap, ALU.max), reads=[mx.k, B_.sink_trk], writes=[mx.k])
    S.op("dve", lambda e: e.tensor_scalar(negm.ap[0:R, 0:nB], mx.ap[0:R, 0:nB], -1.0, None, ALU.mult), reads=[mx.k], writes=[negm.k])
    S.op("dve", lambda e: e.memset(rs.ap[0:R, 0:nB], 0.0), writes=[rs.k])
    pb = B_.pb
    pbv = pb.ap[0:R, 0:nB * NK].rearrange("p (b n) -> p b n", b=nB)
    for b in range(nB):
        S.op("act", lambda e, b=b: e.activation(out=pbv[:, b, :], in_=scv[:, b, 0:NK], func=AF.Exp, bias=negm.ap[0:R, b:b + 1], scale=scale,
                                                accum_out=rs.ap[0:R, b:b + 1]),
             reads=sc.all + [negm.k, rs.k], writes=[pb.k, rs.k])
    if sink_ap is not None:
        S.op("dve", lambda e: e.tensor_tensor(es.ap[0:R, 0:nB], sink_ap, mx.ap[0:R, 0:nB], ALU.subtract), reads=[mx.k, B_.sink_trk], writes=[es.k])
        S.op("act", lambda e: e.activation(out=es.ap[0:R, 0:nB], in_=es.ap[0:R, 0:nB], func=AF.Exp), reads=[es.k], writes=[es.k])
        S.op("dve", lambda e: e.tensor_tensor(rs.ap[0:R, 0:nB], rs.ap[0:R, 0:nB], es.ap[0:R, 0:nB], ALU.add), reads=[rs.k, es.k], writes=[rs.k])
    S.op("act", lambda e: e.activation(out=lse.ap[0:R, 0:nB], in_=rs.ap[0:R, 0:nB], func=AF.Ln), reads=[rs.k], writes=[lse.k])
    S.op("dve", lambda e: e.tensor_tensor(lse.ap[0:R, 0:nB], lse.ap[0:R, 0:nB], mx.ap[0:R, 0:nB], ALU.add), reads=[lse.k, mx.k], writes=[lse.k])
    S.op("dve", lambda e: e.reciprocal(rs.ap[0:R, 0:nB], rs.ap[0:R, 0:nB]), reads=[rs.k], writes=[rs.k])
    S.op("dve", lambda e: e.tensor_tensor(pbv, pbv, rs.ap[0:R, 0:nB].unsqueeze(2).broadcast_to([R, nB, NK]), ALU.mult), reads=[pb.k, rs.k], writes=[pb.k])
    pT = B_.pT
    per_bank = 1024 // R
    nidx = nB * len(pvblocks)
    idx = 0
    tb = None
    filled = []
    for b in range(nB):
        for (off, n, vf, vtrk) in pvblocks:
            if idx % per_bank == 0:
                tb = bank(C)
                filled.append((tb, idx))
            j = idx % per_bank
            tbv = tb.ap.bitcast(BF16)
            S.op("pe", lambda e, b=b, off=off, n=n, j=j, tbv=tbv: e.transpose(tbv[0:n, j * R:(j + 1) * R], pbv[:, b, off:off + n], C.identb.ap[0:R, 0:R]),
                 reads=[pb.k, C.identb.k], writes=[tb.k])
            idx += 1
    for ci, (tb, i0) in enumerate(filled):
        cnt = min(per_bank, nidx - i0)
        tbv = tb.ap.bitcast(BF16)
        if ci % 2 == 0:
            S.op("act", lambda e, tbv=tbv, i0=i0, cnt=cnt: e.copy(pT.ap[:, i0 * R:(i0 + cnt) * R], tbv[:, 0:cnt * R]), reads=[tb.k], writes=[pT.k])
        else:
            S.op("dve", lambda e, tbv=tbv, i0=i0, cnt=cnt: e.tensor_copy(pT.ap[:, i0 * R:(i0 + cnt) * R], tbv[:, 0:cnt * R]), reads=[tb.k], writes=[pT.k])
    ob = bank(C)
    idx = 0
    for b in range(nB):
        for bi, (off, n, vf, vtrk) in enumerate(pvblocks):
            S.op("pe", lambda e, b=b, n=n, vf=vf, idx=idx, bi=bi: e.matmul(ob.ap[0:R, b * hd:(b + 1) * hd], pT.ap[0:n, idx * R:(idx + 1) * R], vf(b),
                                                                         start=(bi == 0), stop=(bi == len(pvblocks) - 1)),
                 reads=[pT.k] + vtrk, writes=[ob.k])
            idx += 1
    Osb = B_.Osb
    S.op("act", lambda e: e.copy(Osb.ap[0:R, 0:nB * hd], ob.ap[0:R, 0:nB * hd]), reads=[ob.k], writes=[Osb.k])
    out_cb(Osb, lse)


def attn_phase(C, l, A, cfg):
    S, I, O = C.S, C.I, C.O
    hd, G, HQ = cfg["hd"], cfg["G"], cfg["HQ"]
    NKV = 4
    HPC = 128 // hd
    QC = HQ // HPC
    scale = hd ** -0.5
    hTa = hT_all_phase(C, l, A)
    if C.dbg.get("rope_f32", cfg["name"] == "swa"):
        cosT = A.f32(T)
        sinT = A.f32(T)
        S.dma("sp", cosT.ap, cfg["rope"][0], writes=[cosT.k])
        S.dma("sp", sinT.ap, cfg["rope"][1], writes=[sinT.k])
    else:
        cosT = A.bf16(T)
        sinT = A.bf16(T)
        for (c0, cn) in splits(T, 1040):
            S.dma("pool", cosT.ap[:, c0:c0 + cn], cfg["rope"][0][:, c0:c0 + cn], writes=[cosT.k])
            S.dma("pool", sinT.ap[:, c0:c0 + cn], cfg["rope"][1][:, c0:c0 + cn], writes=[sinT.k])
    perm = A.bf16(128)
    S.dma("pool", perm.ap, cfg["perm"], writes=[perm.k])
    pmask = A.bf16(256)
    S.dma("pool", pmask.ap, I.pmask, writes=[pmask.k])
    smask = A.bf16(max(cfg["wins"]) + 128)
    S.op("dve", lambda e: e.memset(smask.ap, 0.0), writes=[smask.k])
    QT = A.bf16(QC * T, ("p (k t) -> p k t", dict(k=QC)))
    NKC = NKV * HPC
    KT = A.bf16(NKC * T, ("p (k t) -> p k t", dict(k=NKC)))
    VW = NKV * hd
    Vg = A.bf16(16 * VW, ("p (u c) -> p u c", dict(u=16)))
    Vs = A.bf16(NS * VW, ("p (s c) -> p s c", dict(s=NS)))
    S.op("dve", lambda e: e.memset(Vs.ap, 0.0), writes=[Vs.k])
    qb = A.bf16(512)
    t1 = A.f32(512)
    t2 = A.f32(512)
    B_ = Ctx()
    B_.pb = A.bf16(max(2048, max(cfg["wins"]) + 128))
    B_.pT = A.bf16(2048)
    B_.Osb = A.f32(512)
    B_.mx, B_.negm, B_.rs, B_.es, B_.lse = [A.f32(8) for _ in range(5)]
    wmax = max(cfg["wins"])
    kcb = A.bf16(HPC * wmax, ("p (a b) -> p a b", dict(b=128)))
    S.op("dve", lambda e: e.memset(kcb.ap, 0.0), writes=[kcb.k])
    Vc = A.bf16((wmax // 128) * hd, ("p (a b) -> p a b", dict(b=hd)))
    KcT = A.bf16(HPC * wmax)
    qs = A.bf16(32)
    sinkp = A.f32(32)
    sinks = A.f32(8)
    B_.sink_trk = sinkp.k
    if cfg["name"] == "swa":
        srow = A.f32(32)
        orow = A.f32(128)
        S.op("dve", lambda e: e.memset(orow.ap[0:1, :], 1.0), writes=[orow.k])
        S.dma("sp", srow.ap[0:1, :], I.swa_sinks, writes=[srow.k])
        sb_ = bank(C)
        S.op("pe", lambda e: e.matmul(sb_.ap[:, 0:32], orow.ap[0:1, :], srow.ap[0:1, :], start=True, stop=True), reads=[orow.k, srow.k], writes=[sb_.k])
        S.op("dve", lambda e: e.tensor_copy(sinkp.ap, sb_.ap[:, 0:32]), reads=[sb_.k], writes=[sinkp.k])
        S.dma("sp", sinks.ap[0:32, :], I.swa_sinkS, writes=[sinkp.k])
    stop = C.dbg.get("attn_stop", 99)
    if stop <= 1:
        S.barrier()
        return

    def rope_epi(dst, dst_trk, kout):
        def epi(n, c0, cn, bk, b2):
            S.op("act", lambda e: e.activation(out=t1.ap[:, 0:cn], in_=bk.ap[:, 0:cn], func=AF.Identity, scale=1.0), reads=[bk.k], writes=[t1.k])
            S.op("act", lambda e: e.activation(out=t2.ap[:, 0:cn], in_=b2.ap[:, 0:cn], func=AF.Identity, scale=1.0), reads=[b2.k], writes=[t2.k])
            S.op("dve", lambda e: e.tensor_tensor(t1.ap[:, 0:cn], t1.ap[:, 0:cn], cosT.ap[:, c0:c0 + cn], ALU.mult), reads=[t1.k, cosT.k], writes=[t1.k])
            S.op("dve", lambda e: e.tensor_tensor(t2.ap[:, 0:cn], t2.ap[:, 0:cn], sinT.ap[:, c0:c0 + cn], ALU.mult), reads=[t2.k, sinT.k], writes=[t2.k])
            if kout is None:
                S.op("dve", lambda e: e.tensor_tensor(dst[:, n, c0:c0 + cn], t1.ap[:, 0:cn], t2.ap[:, 0:cn], ALU.add), reads=[t1.k, t2.k], writes=[dst_trk])
            else:
                S.op("dve", lambda e: e.tensor_tensor(t1.ap[:, 0:cn], t1.ap[:, 0:cn], t2.ap[:, 0:cn], ALU.add), reads=[t1.k, t2.k], writes=[t1.k])
                S.op("dve", lambda e: e.tensor_copy(dst[:, n, c0:c0 + cn], t1.ap[:, 0:cn]), reads=[t1.k], writes=[dst_trk])
                kout(n, c0, cn)
        return epi

    Ost = A.f32(512)
    for g in range(G):
        d, w = cfg["dils"][g], cfg["wins"][g]
        nun = 16
        kvp, kvs = cfg["kv_out"][g]
        if cfg["name"] == "dil" or g == 0:
            for (c0, cn) in splits(w + 128, 1088):
                S.dma("pool", smask.ap[0:32, c0:c0 + cn], I.smask[g, :, c0:c0 + cn], writes=[smask.k])

        def kout(n, c0, cn, g=g, d=d, w=w, kvp=kvp, kvs=kvs):
            if n % HPC != 0 or C.dbg.get("no_kout"):
                return
            kv_i = n // HPC
            for (o, nn) in splits(cn, 128):
                tok = c0 + o
                if tok < LP and tok < LP - w:
                    continue
                tb = bank(C)
                S.op("pe", lambda e, o=o, nn=nn, tb=tb: e.transpose(tb.ap[0:nn, 0:128], t1.ap[:, o:o + nn], C.ident.ap),
                     reads=[t1.k, C.ident.k], writes=[tb.k])
                S.op("act", lambda e, nn=nn, tb=tb: e.copy(Ost.ap[0:nn, 0:hd], tb.ap[0:nn, 0:hd]), reads=[tb.k], writes=[Ost.k])
                if tok < LP:
                    S.dma("sp", kvp[tok - (LP - w):tok - (LP - w) + nn, kv_i * hd:(kv_i + 1) * hd], Ost.ap[0:nn, 0:hd], reads=[Ost.k])
                else:
                    S.dma("sp", kvs[:, kv_i * hd:(kv_i + 1) * hd], Ost.ap[0:nn, 0:hd], reads=[Ost.k])

        nks = NKC // 4
        proj_fm2(C, hTa, cfg["wk"][g * nks:(g + 1) * nks], cfg["wkp"][g * nks:(g + 1) * nks], nks, 512, rope_epi(KT.ap, KT.k, kout))
        if stop <= 2:
            S.barrier()
            return
        wb = C.wbs[C.wrr % len(C.wbs)]
        C.wrr += 1
        wslab_load(C, wb, cfg["wv"][g], KD * VW)
        wv = wb.ap[:, 0:KD * VW].rearrange("p (k c) -> p k c", k=KD)
        for u in range(nun):
            r, blk = u % d, u // d
            tk0 = r + d * 128 * blk
            tb = bank(C)
            for k in range(KD):
                lhs_ = hTa.ap[:, k, tk0:tk0 + d * 127 + 1:d]
                rhs_ = wv[:, k, :]
                S.op("pe", lambda e, k=k, tb=tb, lhs_=lhs_, rhs_=rhs_: e.matmul(tb.ap[:, 0:VW], lhs_, rhs_, start=(k == 0), stop=(k == KD - 1)),
                     reads=[hTa.k, wb.k], writes=[tb.k])
            S.op("act", lambda e, u=u, tb=tb: e.copy(Vg.ap[:, u, :], tb.ap[:, 0:VW]), reads=[tb.k], writes=[Vg.k])
            if d * 128 * blk >= LP - w:
                S.op("act", lambda e, tb=tb: e.copy(Ost.ap[:, 0:VW], tb.ap[:, 0:VW]), reads=[tb.k], writes=[Ost.k])
                r0 = tk0 - (LP - w)
                S.dma("sp", kvp[r0:r0 + d * 127 + 1:d, VW:2 * VW], Ost.ap[:, 0:VW], reads=[Ost.k])
        for q in range(NS):
            tb = bank(C)
            for k in range(KD):
                rhs_ = wv[:, k, :]
                S.op("pe", lambda e, k=k, tb=tb, q=q, rhs_=rhs_: e.matmul(tb.ap[0:LS, 0:VW], hTa.ap[:, k, LP + q * LS:LP + (q + 1) * LS], rhs_, start=(k == 0), stop=(k == KD - 1)),
                     reads=[hTa.k, wb.k], writes=[tb.k])
            S.op("act", lambda e, q=q, tb=tb: e.copy(Vs.ap[0:LS, q, :], tb.ap[0:LS, 0:VW]), reads=[tb.k], writes=[Vs.k])
            S.op("act", lambda e, tb=tb: e.copy(Ost.ap[0:LS, 0:VW], tb.ap[0:LS, 0:VW]), reads=[tb.k], writes=[Ost.k])
            S.dma("sp", kvs[q * LS:(q + 1) * LS, VW:2 * VW], Ost.ap[0:LS, 0:VW], reads=[Ost.k])
        if stop <= 3:
            S.barrier()
            return
        for kvh in range(NKV):
            proj_fm2(C, hTa, cfg["wq"][g * NKV + kvh:g * NKV + kvh + 1], cfg["wqp"][g * NKV + kvh:g * NKV + kvh + 1], 1, 512, rope_epi(QT.ap, QT.k, None))
            if stop <= 4:
                continue
            for u in range(nun if not C.dbg.get("skip_punits") else 0):
                r, blk = u % d, u // d
                tq0 = r + d * 128 * blk
                qsl = slice(tq0, tq0 + d * 127 + 1, d)
                if blk == 0:
                    ksl, nk, moff = qsl, 128, 128
                else:
                    tk0 = tq0 - d * 128
                    ksl, nk, moff = slice(tk0, tk0 + d * 255 + 1, d), 256, 0
                q_aps, kfs = [], None
                for b in range(HQ):
                    q_aps.append(QT.ap[:, b // HPC, qsl])
                kf = lambda b, ksl=ksl, kvh=kvh: KT.ap[:, kvh * HPC + (b % HPC), ksl]
                sg = [(0, nk, kf, [KT.k], nk)]
                pv = []
                if blk > 0:
                    pv.append((0, 128, lambda b, u=u, d=d, kvh=kvh: Vg.ap[:, u - d, kvh * hd:(kvh + 1) * hd], [Vg.k]))
                pv.append((nk - 128, 128, lambda b, u=u, kvh=kvh: Vg.ap[:, u, kvh * hd:(kvh + 1) * hd], [Vg.k]))
                sink_ap = sinkp.ap[:, kvh * HQ:(kvh + 1) * HQ] if cfg["name"] == "swa" else None

                def ocb(Osb, lse, g=g, kvh=kvh, tq0=tq0, d=d):
                    S.dma("sp", C.Oh[g, tq0:tq0 + d * 127 + 1:d, kvh * HQ * hd:(kvh + 1) * HQ * hd], Osb.ap[:, 0:HQ * hd], reads=[Osb.k])
                    if G > 1:
                        S.dma("sp", C.lseh[g, tq0:tq0 + d * 127 + 1:d, kvh * HQ:(kvh + 1) * HQ], lse.ap[:, 0:HQ], reads=[lse.k])
                attn_unit(C, B_, 128, HQ, hd, q_aps, [QT.k], sg, pv, pmask.ap[:, moff:moff + nk],
                          [pmask.k], scale, sink_ap, ocb)
            for q in range(NS if not C.dbg.get("skip_sunits") else 0):
                nblk = w // 128
                csrc = cfg["cache"][g]
                for par in range(HPC):
                    S.dma("pool", kcb.ap[:, par * nblk:(par + 1) * nblk, par * hd:(par + 1) * hd],
                          csrc[q, :, 0, kvh, :].rearrange("(a p) c -> p a c", p=128), writes=[kcb.k])
                S.dma("pool", Vc.ap[:, 0:nblk, :], csrc[q, :, 1, kvh, :].rearrange("(a p) c -> p a c", p=128), writes=[Vc.k])
                KcTv = KcT.ap[:, 0:HPC * w].rearrange("p (r c) -> p r c", r=HPC)
                for par in range(HPC):
                    for a0 in range(0, nblk, 8):
                        na = min(8, nblk - a0)
                        tb = bank(C)
                        tbv = tb.ap.bitcast(BF16)
                        for a in range(na):
                            S.op("pe", lambda e, a=a, a0=a0, tbv=tbv, par=par, nblk=nblk: e.transpose(tbv[:, a * 128:(a + 1) * 128], kcb.ap[:, par * nblk + a0 + a, :], C.identb.ap) if True else None,
                                 reads=[kcb.k, C.identb.k], writes=[tb.k])
                        S.op("act", lambda e, a0=a0, na=na, tbv=tbv, par=par, KcTv=KcTv: e.copy(KcTv[:, par, a0 * 128:(a0 + na) * 128], tbv[:, 0:na * 128]), reads=[tb.k], writes=[KcT.k])
                for par in range(HPC):
                    tsl = slice(LP + q * LS, LP + (q + 1) * LS)
                    S.op("dve", lambda e, tsl=tsl: e.tensor_copy(qs.ap[:, 0:QC * LS].rearrange("p (c t) -> p c t", t=LS), QT.ap[:, :, tsl]),
                         reads=[QT.k], writes=[qs.k])
                    q_aps = [qs.ap[:, 0:QC * LS]]
                    sg = []
                    for (o, nn) in splits(w, 512):
                        sg.append((o, nn, lambda b, o=o, nn=nn, par=par, KcTv=KcTv: KcTv[:, par, o:o + nn], [KcT.k], nn))
                    sg.append((w, LS, lambda b, par=par, tsl=tsl, kvh=kvh: KT.ap[:, kvh * HPC + par, tsl], [KT.k], 128))
                    pv = [(a * 128, 128, lambda b, a=a: Vc.ap[:, a, :], [Vc.k]) for a in range(nblk)]
                    pv.append((w, 128, lambda b, q=q, kvh=kvh: Vs.ap[:, q, kvh * hd:(kvh + 1) * hd], [Vs.k]))
                    sink_ap = sinks.ap[0:32, kvh * HPC + par:kvh * HPC + par + 1] if cfg["name"] == "swa" else None

                    def ocb(Osb, lse, g=g, kvh=kvh, q=q, par=par):
                        for hh in range(QC):
                            head = kvh * HQ + hh * HPC + par
                            S.dma("sp", C.Oh[g, LP + q * LS:LP + (q + 1) * LS, head * hd:(head + 1) * hd], Osb.ap[hh * LS:(hh + 1) * LS, 0:hd], reads=[Osb.k])
                            if G > 1:
                                S.dma("sp", C.lseh[g, LP + q * LS:LP + (q + 1) * LS, head:head + 1], lse.ap[hh * LS:(hh + 1) * LS, 0:1], reads=[lse.k], allow_slow_non_contiguous=True)
                    attn_unit(C, B_, QC * LS, 1, hd, q_aps, [qs.k], sg, pv, smask.ap[:, 0:w + 128], [smask.k], scale, sink_ap, ocb)
    S.barrier()
    if stop <= 6:
        return
    A.off = 0
    NH = NKV * HQ
    Og = [A.f32(D) for _ in range(G)]
    acc = A.f32(D)
    lt = A.f32(G * 16, ("p (g h) -> p g h", dict(g=G)))
    mxl = A.f32(16)
    sml = A.f32(16)
    obf = A.bf16(D)
    oTs = A.bf16(KD * 128, ("p (k t) -> p k t", dict(k=KD)))
    for (t0, n) in splits(T, 128):
        for g in range(G):
            S.dma("sp", Og[g].ap[0:n, :], C.Oh[g, t0:t0 + n, :], writes=[Og[g].k])
        if G > 1:
            S.dma("sp", lt.ap[0:n], C.lseh[0:G, t0:t0 + n, 0:16].rearrange("g t h -> t g h"), writes=[lt.k])
            S.op("dve", lambda e, n=n: e.tensor_tensor(mxl.ap[0:n], lt.ap[0:n, 0, :], lt.ap[0:n, 1, :], ALU.max), reads=[lt.k], writes=[mxl.k])
            S.op("dve", lambda e, n=n: e.tensor_tensor(mxl.ap[0:n], mxl.ap[0:n], lt.ap[0:n, 2, :], ALU.max), reads=[lt.k, mxl.k], writes=[mxl.k])
            S.op("dve", lambda e, n=n: e.tensor_tensor(lt.ap[0:n], lt.ap[0:n], mxl.ap[0:n].unsqueeze(1).broadcast_to([n, G, 16]), ALU.subtract), reads=[lt.k, mxl.k], writes=[lt.k])
            S.op("act", lambda e, n=n: e.activation(out=lt.ap[0:n], in_=lt.ap[0:n], func=AF.Exp), reads=[lt.k], writes=[lt.k])
            S.op("dve", lambda e, n=n: e.tensor_tensor(sml.ap[0:n], lt.ap[0:n, 0, :], lt.ap[0:n, 1, :], ALU.add), reads=[lt.k], writes=[sml.k])
            S.op("dve", lambda e, n=n: e.tensor_tensor(sml.ap[0:n], sml.ap[0:n], lt.ap[0:n, 2, :], ALU.add), reads=[lt.k, sml.k], writes=[sml.k])
            S.op("dve", lambda e, n=n: e.reciprocal(sml.ap[0:n], sml.ap[0:n]), reads=[sml.k], writes=[sml.k])
            S.op("dve", lambda e, n=n: e.tensor_tensor(lt.ap[0:n], lt.ap[0:n], sml.ap[0:n].unsqueeze(1).broadcast_to([n, G, 16]), ALU.mult), reads=[lt.k, sml.k], writes=[lt.k])
            for g in range(G):
                ov = Og[g].ap[0:n, :].rearrange("p (h d) -> p h d", h=16)
                wv_ = lt.ap[0:n, g, :].unsqueeze(2).broadcast_to([n, 16, 128])
                S.op("dve", lambda e, ov=ov, wv_=wv_: e.tensor_tensor(ov, ov, wv_, ALU.mult), reads=[Og[g].k, lt.k], writes=[Og[g].k])
            S.op("dve", lambda e, n=n: e.tensor_tensor(acc.ap[0:n], Og[0].ap[0:n], Og[1].ap[0:n], ALU.add), reads=[Og[0].k, Og[1].k], writes=[acc.k])
            S.op("dve", lambda e, n=n: e.tensor_tensor(obf.ap[0:n], acc.ap[0:n], Og[2].ap[0:n], ALU.add), reads=[acc.k, Og[2].k], writes=[obf.k])
        else:
            S.op("act", lambda e, n=n: e.copy(obf.ap[0:n], Og[0].ap[0:n]), reads=[Og[0].k], writes=[obf.k])
        for h2 in range(2):
            tb = bank(C)
            tbv = tb.ap.bitcast(BF16)
            for j in range(8):
                k = h2 * 8 + j
                S.op("pe", lambda e, k=k, j=j, n=n, tbv=tbv: e.transpose(tbv[:, j * 128:j * 128 + n], obf.ap[0:n, k * 128:(k + 1) * 128], C.identb.ap[0:n, 0:n]),
                     reads=[obf.k, C.identb.k], writes=[tb.k])
            src = tbv.rearrange("p (j t) -> p j t", j=8)[:, :, 0:n]
            dst = oTs.ap[:, h2 * 8:(h2 + 1) * 8, 0:n]
            if h2 == 0:
                S.op("act", lambda e, src=src, dst=dst: e.copy(dst, src), reads=[tb.k], writes=[oTs.k])
            else:
                S.op("dve", lambda e, src=src, dst=dst: e.tensor_copy(dst, src), reads=[tb.k], writes=[oTs.k])
        S.dma("sp", C.oT[0:KD, :, t0:t0 + n].rearrange("k p t -> p k t"), oTs.ap[:, :, 0:n], reads=[oTs.k])
    S.barrier()


def ssd_phase(C, l, A):
    S, I, O = C.S, C.I, C.O
    V = C.vec.ap
    one = C.onesf.ap[:, 0:1]
    XW = 3 + LP + NS * (3 + LS)

    def samp(ap2, w):
        return ap2.rearrange("p (s t) -> p s t", t=w)

    dtT = A.f32(T)
    aT = A.f32(T)
    stT = A.f32(48 * 15, ("p (k r) -> p k r", dict(k=48)))
    preT = A.f32(48 * 12, ("p (k r) -> p k r", dict(k=48)))
    Ah = A.f32(1)
    keep = A.off
    hTa = hT_all_phase(C, l, A)
    pst = A.f32(D)
    for pc in range(3):
        S.dma("sp", pst.ap[0:12, :], I.ssd_conv_in[:, pc * D:(pc + 1) * D], writes=[pst.k])
        b = bank(C)
        for k in range(KD):
            S.op("pe", lambda e, k=k, b=b: e.transpose(b.ap[:, k * 12:(k + 1) * 12], pst.ap[0:12, k * 128:(k + 1) * 128], C.ident.ap[0:12, 0:12]),
                 reads=[pst.k, C.ident.k], writes=[b.k])
        S.op("dve", lambda e, pc=pc, b=b: e.tensor_copy(preT.ap[:, pc * KD:(pc + 1) * KD, :], b.ap[:, 0:KD * 12].rearrange("p (k r) -> p k r", k=KD)),
             reads=[b.k], writes=[preT.k])
    tx = A.f32(512)
    ty = A.f32(512)

    def epi_dt(n, c0, cn, bk):
        S.op("act", lambda e: e.activation(out=tx.ap[:, 0:cn], in_=bk.ap[:, 0:cn], func=AF.Identity, bias=V[:, V_SDTB:V_SDTB + 1], scale=1.0),
             reads=[bk.k, C.vec.k], writes=[tx.k])
        S.op("act", lambda e: e.activation(out=ty.ap[:, 0:cn], in_=tx.ap[:, 0:cn], func=AF.Abs), reads=[tx.k], writes=[ty.k])
        S.op("act", lambda e: e.activation(out=ty.ap[:, 0:cn], in_=ty.ap[:, 0:cn], func=AF.Exp, scale=-1.0), reads=[ty.k], writes=[ty.k])
        S.op("act", lambda e: e.activation(out=ty.ap[:, 0:cn], in_=ty.ap[:, 0:cn], func=AF.Ln, bias=one, scale=1.0), reads=[ty.k, C.onesf.k], writes=[ty.k])
        S.op("dve", lambda e: e.tensor_scalar(tx.ap[:, 0:cn], tx.ap[:, 0:cn], 0.0, None, ALU.max), reads=[tx.k], writes=[tx.k])
        S.op("dve", lambda e: e.tensor_tensor(dtT.ap[:, c0:c0 + cn], tx.ap[:, 0:cn], ty.ap[:, 0:cn], ALU.add), reads=[tx.k, ty.k], writes=[dtT.k])

    proj_fm(C, hTa, I.ssd_wdt, 1, 128, epi_dt)
    S.op("act", lambda e: e.activation(out=Ah.ap, in_=V[:, V_SALOG:V_SALOG + 1], func=AF.Exp), reads=[C.vec.k], writes=[Ah.k])
    S.op("dve", lambda e: e.tensor_scalar(Ah.ap, Ah.ap, -1.0, None, ALU.mult), reads=[Ah.k], writes=[Ah.k])
    S.op("act", lambda e: e.activation(out=aT.ap, in_=dtT.ap, func=AF.Exp, scale=Ah.ap), reads=[dtT.k, Ah.k], writes=[aT.k])
    zo = A.bf16(T)

    def epi_z(n, c0, cn, bk):
        S.op("act", lambda e: e.activation(out=zo.ap[:, c0:c0 + cn], in_=bk.ap[:, 0:cn], func=AF.Silu), reads=[bk.k], writes=[zo.k])

    proj_fm(C, hTa, I.ssd_wz, 8, 512, epi_z, after_chunk=lambda n: S.dma("sp", C.zT[n], zo.ap, reads=[zo.k]))
    xpre = A.f32(XW)
    xc = A.f32(T)
    xcb = A.bf16(T)
    S.op("dve", lambda e: e.memset(xpre.ap[:, 0:3], 0.0), writes=[xpre.k])

    def epi_x(n, c0, cn, bk):
        if c0 < LP:
            S.op("act", lambda e: e.copy(xpre.ap[:, 3 + c0:3 + c0 + cn], bk.ap[:, 0:cn]), reads=[bk.k], writes=[xpre.k])
        else:
            S.op("act", lambda e: e.copy(samp(xpre.ap[:, 3 + LP:XW], 3 + LS)[:, :, 3:3 + LS], samp(bk.ap[:, 0:TS], LS)), reads=[bk.k], writes=[xpre.k])

    def after_x(n):
        xs_pre = samp(xpre.ap[:, 3 + LP:XW], 3 + LS)
        S.op("dve", lambda e: e.tensor_copy(xs_pre[:, :, 0:3], preT.ap[:, n, :].rearrange("p (s r) -> p s r", r=3)), reads=[preT.k], writes=[xpre.k])
        w = [V[:, V_SCW + i * 48 + n:V_SCW + i * 48 + n + 1] for i in range(4)]
        cb = V[:, V_SCB + n:V_SCB + n + 1]
        for (dst, src) in ((xc.ap[:, 0:LP], lambda i: xpre.ap[:, i:i + LP]), (samp(xc.ap[:, LP:T], LS), lambda i: xs_pre[:, :, i:i + LS])):
            S.op("dve", lambda e, dst=dst, src=src: e.tensor_scalar(dst, src(0), w[0], cb, ALU.mult, ALU.add), reads=[xpre.k, C.vec.k], writes=[xc.k])
            for i in range(1, 4):
                S.op("dve", lambda e, dst=dst, src=src, i=i: e.scalar_tensor_tensor(out=dst, in0=src(i), scalar=w[i], in1=dst, op0=ALU.mult, op1=ALU.add),
                     reads=[xpre.k, C.vec.k, xc.k], writes=[xc.k])
        S.op("act", lambda e: e.copy(stT.ap[:, n, 0:3], xpre.ap[:, LP:LP + 3]), reads=[xpre.k], writes=[stT.k])
        S.op("act", lambda e: e.copy(stT.ap[:, n, 3:15].rearrange("p (s r) -> p s r", r=3), xs_pre[:, :, LS:LS + 3]), reads=[xpre.k], writes=[stT.k])
        if n < 32:
            S.op("act", lambda e: e.activation(out=xc.ap, in_=xc.ap, func=AF.Silu), reads=[xc.k], writes=[xc.k])
            S.dma("sp", C.xsT[n], xc.ap, reads=[xc.k])
        else:
            S.op("act", lambda e: e.activation(out=xcb.ap, in_=xc.ap, func=AF.Silu), reads=[xc.k], writes=[xcb.k])
            S.dma("sp", C.bcT[n - 32], xcb.ap, reads=[xcb.k])

    proj_fm(C, hTa, I.ssd_wx, 12, 512, epi_x, after_chunk=after_x)
    stage = pst
    for pc in range(3):
        stage.src_trk = [stT.k]
        fm_to_tm(C, stT.ap[:, pc * KD:(pc + 1) * KD, :], KD, 15, stage,
                 [(O.ssd_conv_p[:, pc * D:(pc + 1) * D], 0, 3), (O.ssd_conv_s[:, pc * D:(pc + 1) * D], 3, 12)])
    S.barrier()
    A.off = keep
    NCP = 2
    xs = A.f32(NCP * T, ("p (c t) -> p c t", dict(c=NCP)))
    abc = A.f32(NCP * T, ("p (c t) -> p c t", dict(c=NCP)))
    yacc = A.f32(NCP * T, ("p (c t) -> p c t", dict(c=NCP)))
    H0 = A.f32(NCP * NS * 128, ("p (c q s) -> p c q s", dict(c=NCP, q=NS)))
    stF = A.f32(NCP * NCOND * 128, ("p (c q s) -> p c q s", dict(c=NCP, q=NCOND)))
    BTg = A.bf16(T)
    CTg = A.bf16(T)
    Bsb = A.f32(T)
    Csb = A.f32(T)
    d1s = [A.f32(T) for _ in range(2)]
    Hss = [A.f32(T) for _ in range(2)]
    tms = [A.f32(T) for _ in range(2)]
    selt = A.f32(128)
    selb = A.bf16(128)
    zt = A.bf16(NCP * T, ("p (c t) -> p c t", dict(c=NCP)))
    rot = 0
    for ps_ in range(32 // NCP):
        g = (ps_ * NCP) // 4
        c_lo = ps_ * NCP
        S.dma("sp", xs.ap, C.xsT[c_lo:c_lo + NCP].rearrange("c p t -> p c t"), writes=[xs.k])
        for c in range(NCP):
            S.dma("sp", H0.ap[:, c], I.ssd_h0[:, c_lo + c].rearrange("q p s -> p q s"), writes=[H0.k])
        S.dma("sp", BTg.ap, C.bcT[g], writes=[BTg.k])
        S.dma("sp", CTg.ap, C.bcT[8 + g], writes=[CTg.k])
        for c in range(NCP):
            cg = c_lo + c
            for hl in range(2):
                S.op("dve", lambda e, hl=hl, cg=cg: e.tensor_copy(selt.ap[:, hl * 64:(hl + 1) * 64], C.ident.ap[:, 2 * cg + hl:2 * cg + hl + 1].broadcast_to([128, 64])),
                     reads=[C.ident.k], writes=[selt.k])
            for (c0, cn) in TSPL:
                b1 = bank(C)
                S.op("pe", lambda e, b1=b1, c0=c0, cn=cn: e.matmul(b1.ap[:, 0:cn], selt.ap, aT.ap[:, c0:c0 + cn], start=True, stop=True),
                     reads=[selt.k, aT.k], writes=[b1.k])
                S.op("act", lambda e, b1=b1, c=c, c0=c0, cn=cn: e.copy(abc.ap[:, c, c0:c0 + cn], b1.ap[:, 0:cn]), reads=[b1.k], writes=[abc.k])
                b2 = bank(C)
                S.op("pe", lambda e, b2=b2, c0=c0, cn=cn: e.matmul(b2.ap[:, 0:cn], selt.ap, dtT.ap[:, c0:c0 + cn], start=True, stop=True),
                     reads=[selt.k, dtT.k], writes=[b2.k])
                S.op("dve", lambda e, c=c, cg=cg, c0=c0, cn=cn: e.tensor_scalar(yacc.ap[:, c, c0:c0 + cn], xs.ap[:, c, c0:c0 + cn], V[:, V_SD + cg:V_SD + cg + 1], None, ALU.mult),
                     reads=[xs.k, C.vec.k], writes=[yacc.k])
                S.op("dve", lambda e, b2=b2, c=c, c0=c0, cn=cn: e.tensor_tensor(xs.ap[:, c, c0:c0 + cn], xs.ap[:, c, c0:c0 + cn], b2.ap[:, 0:cn], ALU.mult),
                     reads=[xs.k, b2.k, yacc.k], writes=[xs.k])
        pending = None
        for s_ in range(128):
            S.op("dve", lambda e, s_=s_: e.tensor_copy(selb.ap, C.identb.ap[:, s_:s_ + 1].broadcast_to([128, 128])), reads=[C.identb.k], writes=[selb.k])
            for (src, dst) in ((BTg, Bsb), (CTg, Csb)):
                for (c0, cn) in TSPL:
                    b1 = bank(C)
                    S.op("pe", lambda e, b1=b1, c0=c0, cn=cn, src=src: e.matmul(b1.ap[:, 0:cn], selb.ap, src.ap[:, c0:c0 + cn], start=True, stop=True),
                         reads=[selb.k, src.k], writes=[b1.k])
                    S.op("act", lambda e, b1=b1, c0=c0, cn=cn, dst=dst: e.copy(dst.ap[:, c0:c0 + cn], b1.ap[:, 0:cn]), reads=[b1.k], writes=[dst.k])
            for c in range(NCP):
                d1, Hs, tm = d1s[rot % 2], Hss[rot % 2], tms[rot % 2]
                rot += 1
                S.op("dve", lambda e, c=c, d1=d1: e.tensor_tensor(d1.ap, xs.ap[:, c, :], Bsb.ap, ALU.mult), reads=[xs.k, Bsb.k], writes=[d1.k])
                S.op("dve", lambda e, c=c, d1=d1, Hs=Hs: e.tensor_tensor_scan(Hs.ap[:, 0:LP], abc.ap[:, c, 0:LP], d1.ap[:, 0:LP], 0.0, ALU.mult, ALU.add),
                     reads=[abc.k, d1.k], writes=[Hs.k])
                for q in range(NS):
                    cs = slice(LP + q * LS, LP + (q + 1) * LS)
                    S.op("dve", lambda e, c=c, cs=cs, q=q, s_=s_, d1=d1, Hs=Hs: e.tensor_tensor_scan(Hs.ap[:, cs], abc.ap[:, c, cs], d1.ap[:, cs], H0.ap[:, c, q, s_:s_ + 1], ALU.mult, ALU.add),
                         reads=[abc.k, d1.k, H0.k], writes=[Hs.k])
                S.op("act", lambda e, c=c, s_=s_, Hs=Hs: e.copy(stF.ap[:, c, 0, s_:s_ + 1], Hs.ap[:, LP - 1:LP]), reads=[Hs.k], writes=[stF.k])
                S.op("act", lambda e, c=c, s_=s_, Hs=Hs: e.copy(stF.ap[:, c, 1:NCOND, s_], samp(Hs.ap[:, LP:T], LS)[:, :, LS - 1]), reads=[Hs.k], writes=[stF.k])
                S.op("pool", lambda e, Hs=Hs, tm=tm: e.tensor_tensor(tm.ap, Hs.ap, Csb.ap, ALU.mult), reads=[Hs.k, Csb.k], writes=[tm.k])
                if pending is not None:
                    pc_, ptm_ = pending
                    S.op("dve", lambda e, pc_=pc_, ptm_=ptm_: e.tensor_tensor(yacc.ap[:, pc_, :], yacc.ap[:, pc_, :], ptm_.ap, ALU.add), reads=[yacc.k, ptm_.k], writes=[yacc.k])
                pending = (c, tm)
        pc_, ptm_ = pending
        S.op("dve", lambda e, pc_=pc_, ptm_=ptm_: e.tensor_tensor(yacc.ap[:, pc_, :], yacc.ap[:, pc_, :], ptm_.ap, ALU.add), reads=[yacc.k, ptm_.k], writes=[yacc.k])
        pending = None
        S.dma("sp", zt.ap, C.zT[c_lo:c_lo + NCP].rearrange("c p t -> p c t"), writes=[zt.k])
        S.op("dve", lambda e: e.tensor_tensor(yacc.ap, yacc.ap, zt.ap, ALU.mult), reads=[yacc.k, zt.k], writes=[yacc.k])
        S.dma("sp", C.ygT[c_lo:c_lo + NCP].rearrange("c p t -> p c t"), yacc.ap, reads=[yacc.k])
        for c in range(NCP):
            S.dma("sp", O.ssd_p[c_lo + c], stF.ap[:, c, 0, :], reads=[stF.k])
            S.dma("sp", O.ssd_s[:, c_lo + c].rearrange("q p s -> p q s"), stF.ap[:, c, 1:NCOND, :], reads=[stF.k])
    S.barrier()
    A.off = keep
    yg = A.f32(4 * T, ("p (c t) -> p c t", dict(c=4)))
    sq = A.bf16(4 * T, ("p (c t) -> p c t", dict(c=4)))
    rstd = A.f32(T)
    yo = A.bf16(4 * T, ("p (c t) -> p c t", dict(c=4)))
    for g in range(8):
        S.dma("sp", yg.ap, C.ygT[4 * g:4 * g + 4].rearrange("c p t -> p c t"), writes=[yg.k])
        S.op("act", lambda e: e.activation(out=sq.ap, in_=yg.ap, func=AF.Square), reads=[yg.k], writes=[sq.k])
        for (c0, cn) in TSPL:
            b = bank(C)
            for c in range(4):
                S.op("pe", lambda e, b=b, c=c, c0=c0, cn=cn: e.matmul(b.ap[:, 0:cn], C.onesb.ap, sq.ap[:, c, c0:c0 + cn], start=(c == 0), stop=(c == 3)),
                     reads=[sq.k, C.onesb.k], writes=[b.k])
            S.op("act", lambda e, b=b, c0=c0, cn=cn: e.activation(out=rstd.ap[:, c0:c0 + cn], in_=b.ap[:, 0:cn], func=AF.Sqrt, bias=C.epsb.ap, scale=1.0 / 512),
                 reads=[b.k, C.epsb.k], writes=[rstd.k])
        S.op("dve", lambda e: e.reciprocal(rstd.ap, rstd.ap), reads=[rstd.k], writes=[rstd.k])
        for c in range(4):
            S.op("dve", lambda e, c=c, g=g: e.scalar_tensor_tensor(out=yo.ap[:, c, :], in0=yg.ap[:, c, :], scalar=V[:, V_SNORM + 4 * g + c:V_SNORM + 4 * g + c + 1],
                                                                 in1=rstd.ap, op0=ALU.mult, op1=ALU.mult),
                 reads=[yg.k, rstd.k, C.vec.k], writes=[yo.k])
        S.dma("sp", C.oT[4 * g:4 * g + 4].rearrange("c p t -> p c t"), yo.ap, reads=[yo.k])
    S.barrier()


MIXERS = {}


def layer(C, l):
    S = C.S
    A = Arena(C)
    if l == 0:
        C.modT = es_persist(C, "modT", 96 * NCOND, F32)
        C.modT.ap = C.modT.ap.rearrange("p (n c) -> p n c", c=NCOND)
        C.Amix = es_persist(C, "Amix", 16 * NCOND, F32)
        C.Amix.ap = C.Amix.ap.rearrange("p (n c) -> p n c", c=NCOND)
        C.Affn = es_persist(C, "Affn", 16 * NCOND, F32)
        C.Affn.ap = C.Affn.ap.rearrange("p (n c) -> p n c", c=NCOND)
        C.epsb = es_persist(C, "epsb", 1, F32)
        S.op("dve", lambda e: e.memset(C.epsb.ap, EPS), writes=[C.epsb.k])
        C.wbs = [es_persist(C, "wb%d" % i, WSLAB, BF16) for i in range(2)]
        C.wrr = 0
        C.rrr = 0
    ada_phase(C, l, A)
    S.barrier()
    kind = l % 4
    mo = None
    en = C.dbg.get("mixers", (0, 1, 2, 3))
    I, O = C.I, C.O
    if kind == 3 and 3 in en:
        lru_phase(C, l, A)
        mo = make_mixer_out(16, lambda C: C.I.lru_wout)
    if kind == 1 and 1 in en:
        ssd_phase(C, l, A)
        mo = make_mixer_out(32, lambda C: C.I.ssd_wout)
    if kind == 0 and 0 in en:
        cfg = dict(name="swa", hd=64, G=1, HQ=8, dils=[1], wins=[128], rope=I.rope64, perm=I.perm64,
                   wq=I.swa_wq, wk=I.swa_wk, wv=I.swa_wv, wqp=I.swa_wqp, wkp=I.swa_wkp, cache=[I.swa_cache], kv_out=[(O.swa_kv_p, O.swa_kv_s)])
        attn_phase(C, l, A, cfg)
        mo = make_mixer_out(16, lambda C: C.I.swa_wo)
    if kind == 2 and 2 in en:
        cfg = dict(name="dil", hd=128, G=3, HQ=4, dils=[1, 4, 16], wins=[128, 512, 2048], rope=I.rope128, perm=I.perm128,
                   wq=I.dil_wq, wk=I.dil_wk, wv=I.dil_wv, wqp=I.dil_wqp, wkp=I.dil_wkp, cache=[I.dil_c0, I.dil_c1, I.dil_c2],
                   kv_out=[(O.dil_kv0_p, O.dil_kv0_s), (O.dil_kv1_p, O.dil_kv1_s), (O.dil_kv2_p, O.dil_kv2_s)])
        attn_phase(C, l, A, cfg)
        mo = make_mixer_out(16, lambda C: C.I.dil_wo)
    A.off = 0
    ffn_phase(C, l, A, mo)
    S.barrier()


def final(C):
    S = C.S
    A = Arena(C)
    NT = 512
    xT = A.f32(KD * NT, ("p (k t) -> p k t", dict(k=KD)))
    sq = A.bf16(KD * NT, ("p (k t) -> p k t", dict(k=KD)))
    rstd = A.f32(NT)
    yo = [A.f32(D) for _ in range(2)]
    gv = C.vec.ap[:, V_GFIN:V_GFIN + 16]
    cnt = 0
    for (t0, nt) in splits(T, NT):
        S.dma("sp", xT.ap[:, :, 0:nt], C.xres[:, :, t0:t0 + nt].rearrange("k p t -> p k t"), writes=[xT.k])
        S.op("act", lambda e, nt=nt: e.activation(out=sq.ap[:, :, 0:nt], in_=xT.ap[:, :, 0:nt], func=AF.Square),
             reads=[xT.k], writes=[sq.k])
        b = bank(C)
        for k in range(KD):
            S.op("pe", lambda e, k=k, b=b, nt=nt: e.matmul(b.ap[:, 0:nt], C.onesb.ap, sq.ap[:, k, 0:nt],
                                                        start=(k == 0), stop=(k == KD - 1)),
                 reads=[sq.k, C.onesb.k], writes=[b.k])
        S.op("act", lambda e, b=b, nt=nt: e.activation(out=rstd.ap[:, 0:nt], in_=b.ap[:, 0:nt], func=AF.Sqrt,
                                                       bias=C.epsb.ap, scale=1.0 / D),
             reads=[b.k, C.epsb.k], writes=[rstd.k])
        S.op("dve", lambda e, nt=nt: e.reciprocal(rstd.ap[:, 0:nt], rstd.ap[:, 0:nt]), reads=[rstd.k], writes=[rstd.k])
        S.op("dve", lambda e, nt=nt: e.tensor_tensor(
            xT.ap[:, :, 0:nt], xT.ap[:, :, 0:nt], rstd.ap[:, 0:nt].unsqueeze(1).broadcast_to([128, KD, nt]), ALU.mult),
            reads=[xT.k, rstd.k], writes=[xT.k])
        S.op("dve", lambda e, nt=nt: e.tensor_tensor(
            xT.ap[:, :, 0:nt], xT.ap[:, :, 0:nt], gv.unsqueeze(2).broadcast_to([128, KD, nt]), ALU.mult),
            reads=[xT.k, C.vec.k], writes=[xT.k])
        for (c0, n) in splits(nt, 128):
            y = yo[cnt % 2]
            cnt += 1
            for g in range(4):
                b = bank(C)
                for j in range(4):
                    k = g * 4 + j
                    S.op("pe", lambda e, k=k, j=j, b=b, c0=c0, n=n: e.transpose(
                        b.ap[0:n, j * 128:(j + 1) * 128], xT.ap[:, k, c0:c0 + n], C.ident.ap),
                        reads=[xT.k, C.ident.k], writes=[b.k])
                if g % 2 == 0:
                    S.op("act", lambda e, g=g, b=b, n=n, y=y: e.copy(y.ap[0:n, g * 512:(g + 1) * 512], b.ap[0:n, :]),
                         reads=[b.k], writes=[y.k])
                else:
                    S.op("dve", lambda e, g=g, b=b, n=n, y=y: e.tensor_copy(y.ap[0:n, g * 512:(g + 1) * 512], b.ap[0:n, :]),
                         reads=[b.k], writes=[y.k])
            S.dma("sp", C.O.y[t0 + c0:t0 + c0 + n, :], y.ap[0:n, :], reads=[y.k])


def make_inputs(inp, core):
    b = core % 4
    f = lambda a: np.ascontiguousarray(np.asarray(a, dtype=np.float32))
    m = {}
    xs = f(inp["x_sample"])[core * NS:(core + 1) * NS].reshape(TS, D)
    m["xin"] = np.concatenate([f(inp["x_prompt"])[b], xs], axis=0)
    m["cond"] = np.concatenate([f(inp["c_prompt"])[b:b + 1], f(inp["c_sample"])[core * NS:(core + 1) * NS]], axis=0)
    sl = slice(core * NS, (core + 1) * NS)
    m["swa_cache"] = f(inp["cache_swa_kv"])[0, sl]
    m["dil_c0"] = f(inp["cache_dil_kv_w128"])[0, sl]
    m["dil_c1"] = f(inp["cache_dil_kv_w512"])[0, sl]
    m["dil_c2"] = f(inp["cache_dil_kv_w2048"])[0, sl]
    m["ssd_conv_in"] = f(inp["state_ssd_conv"])[0, sl].reshape(NS * 3, 6144)
    m["ssd_h0"] = f(inp["state_ssd"])[0, sl].reshape(NS, 32, 128, 128)
    m["lru_conv_in"] = f(inp["state_lru_conv"])[0, sl].reshape(NS * 3, D)
    m["lru_h0"] = f(inp["state_lru"])[0, sl]
    return m


def shared_inputs(inp):
    f = lambda a: np.ascontiguousarray(np.asarray(a, dtype=np.float32))
    m = {}
    m["ident"] = np.eye(128, dtype=np.float32)
    vecs = np.zeros((NVEC, 128), np.float32)
    vecs[V_BADA:V_BADA + DEPTH * 96] = f(inp["b_ada"]).reshape(DEPTH * 96, 128)
    vecs[V_GMIX:V_GMIX + DEPTH * 16] = f(inp["g_mix"]).reshape(DEPTH * 16, 128)
    vecs[V_GFFN:V_GFFN + DEPTH * 16] = f(inp["g_ffn"]).reshape(DEPTH * 16, 128)
    vecs[V_GFIN:V_GFIN + 16] = f(inp["g_final"]).reshape(16, 128)
    vecs[V_SCW:V_SCW + 192] = f(inp["ssd_conv_w"]).reshape(192, 128)
    vecs[V_SCB:V_SCB + 48] = f(inp["ssd_conv_b"]).reshape(48, 128)
    vecs[V_SNORM:V_SNORM + 32] = f(inp["ssd_norm"]).reshape(32, 128)
    vecs[V_SD:V_SD + 32] = np.repeat(f(inp["ssd_d"])[0], 64).reshape(32, 128)
    vecs[V_SDTB, 0:64] = f(inp["ssd_dt_bias"])[0]
    vecs[V_SALOG, 0:64] = f(inp["ssd_a_log"])[0]
    vecs[V_LBIN:V_LBIN + 32] = f(inp["lru_b_in"]).reshape(32, 128)
    vecs[V_LCW:V_LCW + 64] = f(inp["lru_conv_w"]).reshape(64, 128)
    vecs[V_LCB:V_LCB + 16] = f(inp["lru_conv_b"]).reshape(16, 128)
    vecs[V_LBR:V_LBR + 16] = f(inp["lru_b_r"]).reshape(16, 128)
    vecs[V_LBI:V_LBI + 16] = f(inp["lru_b_i"]).reshape(16, 128)
    vecs[V_LLAM:V_LLAM + 16] = f(inp["lru_lam"]).reshape(16, 128)
    m["vecs"] = vecs
    pos = np.concatenate([np.arange(LP), PAST + (np.arange(TS) % LS)]).astype(np.float32)
    for hd in (64, 128):
        half = hd // 2
        inv = (10000.0 ** (-np.arange(half, dtype=np.float32) / np.float32(half))).astype(np.float32)
        ang = (pos[None, :] * inv[:, None]).astype(np.float32)
        p = np.arange(128)
        dd = p % hd
        cosT = np.cos(ang)[dd % half].astype(np.float32)
        sinT = np.sin(ang)[dd % half].astype(np.float32) * np.where(dd < half, -1.0, 1.0).astype(np.float32)[:, None]
        m["rope%d" % hd] = np.ascontiguousarray(np.stack([cosT, sinT]))
        perm = np.zeros((128, 128), np.float32)
        partner = (p - dd) + (dd + half) % hd
        perm[partner, p] = 1.0
        m["perm%d" % hd] = perm
    qi = np.arange(128)[:, None]
    si = np.arange(256)[None, :]
    m["pmask"] = np.where((si >= qi) & (si <= qi + 128), 0.0, MASKV).astype(np.float32)
    sm = np.full((3, 32, 2176), MASKV, np.float32)
    tt = (np.arange(32) % LS)[:, None]
    for g, (w, d) in enumerate(((128, 1), (512, 4), (2048, 16))):
        c = np.arange(w)[None, :]
        sm[g, :, :w] = np.where((c % d == tt % d) & (c >= tt), 0.0, MASKV)
        tn = np.arange(LS)[None, :]
        sm[g, :, w:w + LS] = np.where((tn <= tt) & (tn % d == tt % d), 0.0, MASKV)
    m["smask"] = sm
    sk = f(inp["swa_sinks"])[0]
    m["swa_sinks"] = sk.reshape(1, 32)
    rr_ = np.arange(32) // LS
    m["swa_sinkS"] = np.stack([sk[(u // 2) * 8 + 2 * rr_ + (u % 2)] for u in range(8)], axis=1).astype(np.float32)
    wqkv = f(inp["swa_w_qkv"][0])
    m["swa_wq"] = tile_w(wqkv[:, 0:2048], 512).reshape(4, 128, KD * 512)
    wk = wqkv[:, 2048:2304].reshape(D, 4, 64)
    wks = np.zeros((D, 4, 2, 2, 64), np.float32)
    wks[:, :, 0, 0, :] = wk
    wks[:, :, 1, 1, :] = wk
    m["swa_wk"] = tile_w(wks.reshape(D, 1024), 512).reshape(2, 128, KD * 512)
    m["swa_wv"] = tile_w(wqkv[:, 2304:2560], 256).reshape(1, 128, KD * 256)

    def rot_cols(w2, hd_):
        K_, N_ = w2.shape
        w3 = w2.reshape(K_, N_ // hd_, hd_)
        return np.ascontiguousarray(np.concatenate([w3[:, :, hd_ // 2:], w3[:, :, :hd_ // 2]], axis=2)).reshape(K_, N_)

    m["swa_wqp"] = tile_w(rot_cols(wqkv[:, 0:2048], 64), 512).reshape(4, 128, KD * 512)
    m["swa_wkp"] = tile_w(rot_cols(wks.reshape(D, 1024), 64), 512).reshape(2, 128, KD * 512)
    m["swa_wo"] = tile_w(f(inp["swa_w_o"][0]), 512).reshape(4, 128, KD * 512)
    dq = f(inp["dil_w_qkv"][0])
    m["dil_wq"] = tile_w(dq[:, 0:6144], 512).reshape(12, 128, KD * 512)
    m["dil_wk"] = tile_w(dq[:, 6144:7680], 512).reshape(3, 128, KD * 512)
    m["dil_wv"] = tile_w(dq[:, 7680:9216], 512).reshape(3, 128, KD * 512)
    m["dil_wqp"] = tile_w(rot_cols(dq[:, 0:6144], 128), 512).reshape(12, 128, KD * 512)
    m["dil_wkp"] = tile_w(rot_cols(dq[:, 6144:7680], 128), 512).reshape(3, 128, KD * 512)
    m["dil_wo"] = tile_w(f(inp["dil_w_o"][0]), 512).reshape(4, 128, KD * 512)
    win = f(inp["ssd_w_in"][0])
    m["ssd_wz"] = tile_w(win[:, 0:4096], 512).reshape(8, 128, KD * 512)
    m["ssd_wx"] = tile_w(win[:, 4096:10240], 512).reshape(12, 128, KD * 512)
    wdt = np.zeros((D, 128), np.float32)
    wdt[:, 0:64] = win[:, 10240:10304]
    m["ssd_wdt"] = tile_w(wdt, 128).reshape(1, 128, KD * 128)
    m["ssd_wout"] = tile_w(f(inp["ssd_w_out"][0]), 256).reshape(8, 128, 32 * 256)
    m["lru_win"] = tile_w(f(inp["lru_w_in"][0]), 512).reshape(8, 128, KD * 512)
    m["lru_wout"] = tile_w(f(inp["lru_w_out"][0]), 512).reshape(4, 128, KD * 512)
    for nm, key in (("lru_wr", "lru_w_r"), ("lru_wi", "lru_w_i")):
        w = f(inp[key][0])
        m[nm] = np.ascontiguousarray(w.reshape(8, 2, 128, 256).transpose(2, 0, 1, 3)).reshape(128, 4096)
    m["w_ada"] = np.stack([tile_w(f(inp["w_ada"][l]), 512) for l in range(DEPTH)]).reshape(DEPTH, 24, 128, KD * 512)
    m["w_ff1"] = np.stack([tile_w(f(inp["w_ff1"][l]), 512) for l in range(DEPTH)]).reshape(DEPTH, 16, 128, KD * 512)
    m["w_ff2"] = np.stack([tile_w(f(inp["w_ff2"][l]), 128) for l in range(DEPTH)]).reshape(DEPTH, 16, 128, 64 * 128)
    return m


_NC_CACHE = {}


def run(inp, dbg=None, trace=False, cores=None):
    cores = list(range(NCORES)) if cores is None else cores
    key = tuple(sorted((dbg or {}).items()))
    if key not in _NC_CACHE:
        _NC_CACHE[key] = build(dbg)
    nc = _NC_CACHE[key]
    sh = shared_inputs(inp)
    in_maps = []
    for c in cores:
        m = dict(sh)
        m.update(make_inputs(inp, c))
        in_maps.append(m)
    res = run_bass_kernel_spmd(nc, in_maps, core_ids=list(range(len(cores))), trace=trace)
    return res


def kernel(**inp):
    res = run(inp)
    R = res.results
    f32 = np.float32

    def pr(name, shape):
        return np.stack([np.asarray(R[b][name], f32).reshape(shape) for b in range(4)])[None]

    def sa(name, shape):
        return np.concatenate([np.asarray(R[c][name], f32).reshape((NS,) + shape) for c in range(NCORES)], axis=0)[None]

    y_prompt = np.stack([np.asarray(R[b]["y"], f32)[:LP] for b in range(4)])
    y_sample = np.concatenate([np.asarray(R[c]["y"], f32)[LP:].reshape(NS, LS, D) for c in range(NCORES)], axis=0)
    outs = [y_prompt, y_sample,
            pr("swa_kv_p", (128, 2, 4, 64)), sa("swa_kv_s", (LS, 2, 4, 64)),
            pr("ssd_conv_p", (3, 6144)), sa("ssd_conv_s", (3, 6144)),
            pr("ssd_p", (64, 64, 128)), sa("ssd_s", (64, 64, 128))]
    for g, w in enumerate((128, 512, 2048)):
        outs.append(pr("dil_kv%d_p" % g, (w, 2, 4, 128)))
        outs.append(sa("dil_kv%d_s" % g, (LS, 2, 4, 128)))
    outs += [pr("lru_conv_p", (3, D)), sa("lru_conv_s", (3, D)), pr("lru_p", (D,)), sa("lru_s", (D,))]
    return tuple(outs)
```
